# Optimizing a Trainium2 kernel written in Bass

```python
import jax, jax.numpy as jnp
from jax import lax
import numpy as np

D_MODEL = 1024
BATCH = 32
SEQ = 2048
DEPTH = 2
DEC_BATCH = 32
DEC_SEQ = 64
PAST_LEN = 2048

CHUNK = 64
N_EVEN = (DEPTH + 1) // 2
N_ODD = DEPTH // 2
D_POOL = D_MODEL // 2
POOL_WINDOWS = (2, 4, 8, 16)
N_POOL_GROUPS = len(POOL_WINDOWS)
POOL_GROUP = D_POOL // N_POOL_GROUPS
POOL_STATE = max(POOL_WINDOWS) - 1
D_RNN = D_MODEL // 2
N_RNN_BLOCKS = 8
RNN_BLOCK = D_RNN // N_RNN_BLOCKS
CONV_WIDTH = 4
RG_C = 8.0
HEAD_DIM = 64
N_Q_HEADS = D_MODEL // HEAD_DIM
N_KV_HEADS = 4
GROUP = N_Q_HEADS // N_KV_HEADS
WINDOW = 128
N_MEM = 256
N_MEM_HEADS = 4
MEM_HEAD_DIM = D_MODEL // N_MEM_HEADS
D_FF = 4 * D_MODEL
EPS = 1e-6
NEG_INF = -1e30

kernel_name = 'hybrid_stream_pool_rglru_swa_step'


def rmsnorm(x, g):
    x32 = x.astype(jnp.float32)
    y = x32 * lax.rsqrt(jnp.mean(x32 * x32, axis=-1, keepdims=True) + EPS) * g.astype(jnp.float32)
    return y.astype(x.dtype)


def pool_mixer(u, prev, pos0, w, scale):
    B, L, _ = u.shape
    up = jnp.concatenate([prev, u], axis=1)
    cs = jnp.cumsum(up.astype(jnp.float32), axis=1)
    cs = jnp.pad(cs, ((0, 0), (1, 0), (0, 0)))
    pos = (pos0 + jnp.arange(L, dtype=jnp.int32))[None, :, None]
    end = cs[:, POOL_STATE + 1:]
    means = []
    for g, win in enumerate(POOL_WINDOWS):
        sl = slice(g * POOL_GROUP, (g + 1) * POOL_GROUP)
        start = cs[:, POOL_STATE + 1 - win:POOL_STATE + 1 - win + L, sl]
        cnt = jnp.minimum(pos + 1, win).astype(jnp.float32)
        means.append((end[..., sl] - start) / cnt)
    d = (jnp.concatenate(means, axis=-1) - u.astype(jnp.float32)).astype(u.dtype)
    d = d.reshape(B, L, N_POOL_GROUPS, POOL_GROUP)
    y = jnp.einsum('blgc,gce->blge', d, w).reshape(B, L, D_POOL) * scale
    return y, up[:, -POOL_STATE:]


def causal_conv(u, prev, w, b):
    L = u.shape[1]
    up = jnp.concatenate([prev, u], axis=1)
    y = b + sum(up[:, k:k + L] * w[k] for k in range(CONV_WIDTH))
    return y, up[:, -(CONV_WIDTH - 1):]


def rg_lru(xc, h0, w_a, b_a, w_x, b_x, lam):
    B, L, _ = xc.shape
    xb = xc.reshape(B, L, N_RNN_BLOCKS, RNN_BLOCK)
    r_gate = jax.nn.sigmoid(jnp.einsum('blhi,hij->blhj', xb, w_a).reshape(B, L, D_RNN).astype(jnp.float32)
                            + b_a.astype(jnp.float32))
    i_gate = jax.nn.sigmoid(jnp.einsum('blhi,hij->blhj', xb, w_x).reshape(B, L, D_RNN).astype(jnp.float32)
                            + b_x.astype(jnp.float32))
    log_a = -RG_C * r_gate * jax.nn.softplus(-lam.astype(jnp.float32))
    a = jnp.exp(log_a)
    mult = jnp.sqrt(-jnp.expm1(2.0 * log_a))
    b = mult * i_gate * xc.astype(jnp.float32)
    b = b.at[:, 0].add(a[:, 0] * h0.astype(jnp.float32))

    def combine(lhs, rhs):
        a1, b1 = lhs
        a2, b2 = rhs
        return a1 * a2, a2 * b1 + b2

    _, h = lax.associative_scan(combine, (a, b), axis=1)
    return h.astype(xc.dtype), h[:, -1].astype(xc.dtype)


def swa_attention(q, k_all, v_all, prefix_valid, sinks):
    B, L = q.shape[0], q.shape[1]
    n_blk = -(-L // CHUNK)
    pad = n_blk * CHUNK - L
    q = jnp.pad(q, ((0, 0), (0, pad), (0, 0), (0, 0), (0, 0)))
    k_all = jnp.pad(k_all, ((0, 0), (0, pad), (0, 0), (0, 0)))
    v_all = jnp.pad(v_all, ((0, 0), (0, pad), (0, 0), (0, 0)))
    valid = jnp.concatenate([jnp.full((WINDOW,), prefix_valid, dtype=bool),
                             jnp.arange(n_blk * CHUNK) < L])
    span = WINDOW + CHUNK
    scale = HEAD_DIM ** -0.5
    sink = sinks.astype(jnp.float32).reshape(N_KV_HEADS, GROUP, 1, 1)

    def block(n):
        start = n * CHUNK
        qs = lax.dynamic_slice_in_dim(q, start, CHUNK, axis=1)
        kb = lax.dynamic_slice_in_dim(k_all, start, span, axis=1)
        vb = lax.dynamic_slice_in_dim(v_all, start, span, axis=1)
        mb = lax.dynamic_slice_in_dim(valid, start, span, axis=0)
        s = jnp.einsum('bqkgd,bmkd->bkgqm', qs, kb).astype(jnp.float32) * scale
        s = jnp.where(mb, s, NEG_INF)
        snk = jnp.broadcast_to(sink, s.shape[:-1] + (1,))
        pr = jax.nn.softmax(jnp.concatenate([s, snk], axis=-1), axis=-1)[..., :span]
        return jnp.einsum('bkgqm,bmkd->bqkgd', pr.astype(vb.dtype), vb)

    o = lax.map(block, jnp.arange(n_blk))
    o = jnp.moveaxis(o, 0, 1).reshape(B, n_blk * CHUNK, N_Q_HEADS * HEAD_DIM)
    return o[:, :L]


def even_mixer(h, pool_prev, conv_prev, lru_prev, pos0, p, e):
    z = h @ p['w_in_even'][e]
    u_pool = z[..., :D_POOL]
    u_x = z[..., D_POOL:D_POOL + D_RNN]
    u_gate = z[..., D_POOL + D_RNN:]
    y_pool, pool_new = pool_mixer(u_pool, pool_prev, pos0, p['pool_w'][e], p['pool_scale'][e])
    xc, conv_new = causal_conv(u_x, conv_prev, p['conv_w'][e], p['conv_b'][e])
    hr, lru_new = rg_lru(xc, lru_prev, p['w_rg_a'][e], p['b_rg_a'][e], p['w_rg_x'][e], p['b_rg_x'][e],
                         p['rg_lambda'][e])
    y_rnn = hr * jax.nn.gelu(u_gate)
    y = jnp.concatenate([y_pool, y_rnn], axis=-1) @ p['w_out_even'][e]
    return y, pool_new, conv_new, lru_new


def odd_mixer(h, k_prev, v_prev, prefix_valid, p, o):
    B, L, _ = h.shape
    z = h @ p['w_qkv_odd'][o]
    nq = N_Q_HEADS * HEAD_DIM
    nkv = N_KV_HEADS * HEAD_DIM
    q = z[..., :nq].reshape(B, L, N_KV_HEADS, GROUP, HEAD_DIM)
    k = z[..., nq:nq + nkv].reshape(B, L, N_KV_HEADS, HEAD_DIM)
    v = z[..., nq + nkv:].reshape(B, L, N_KV_HEADS, HEAD_DIM)
    k_all = jnp.concatenate([k_prev, k], axis=1)
    v_all = jnp.concatenate([v_prev, v], axis=1)
    att = swa_attention(q, k_all, v_all, prefix_valid, p['attn_sinks'][o])
    y = att @ p['w_o_odd'][o]
    return y, k_all[:, -WINDOW:], v_all[:, -WINDOW:]


def mem_kv(mem, g, w_k, w_v):
    B = mem.shape[0]
    mn = rmsnorm(mem, g)
    k = (mn @ w_k).reshape(B, N_MEM, N_MEM_HEADS, MEM_HEAD_DIM)
    v = (mn @ w_v).reshape(B, N_MEM, N_MEM_HEADS, MEM_HEAD_DIM)
    return k, v


def cross_attn(h, mk, mv, w_q, w_o):
    B, L, _ = h.shape
    q = (h @ w_q).reshape(B, L, N_MEM_HEADS, MEM_HEAD_DIM)
    s = jnp.einsum('blhd,bmhd->bhlm', q, mk).astype(jnp.float32) * (MEM_HEAD_DIM ** -0.5)
    pr = jax.nn.softmax(s, axis=-1)
    o = jnp.einsum('bhlm,bmhd->blhd', pr.astype(mv.dtype), mv).reshape(B, L, D_MODEL)
    return o @ w_o


def sq_relu_mlp(h, w_up, w_down):
    return jnp.square(jax.nn.relu(h @ w_up)) @ w_down


def trunk(x, pos0, prefix_valid, pool_st, conv_st, lru_st, swa_k, swa_v, mem_k, mem_v, p):
    new_pool, new_conv, new_lru, new_k, new_v = [], [], [], [], []
    for layer in range(DEPTH):
        h = rmsnorm(x, p['norm_mix'][layer])
        if layer % 2 == 0:
            e = layer // 2
            y, pn, cn, ln = even_mixer(h, pool_st[e], conv_st[e], lru_st[e], pos0, p, e)
            new_pool.append(pn)
            new_conv.append(cn)
            new_lru.append(ln)
        else:
            o = layer // 2
            y, kn, vn = odd_mixer(h, swa_k[o], swa_v[o], prefix_valid, p, o)
            new_k.append(kn)
            new_v.append(vn)
        x = x + y
        x = x + cross_attn(rmsnorm(x, p['norm_cross'][layer]), mem_k[layer], mem_v[layer],
                           p['w_mq'][layer], p['w_mo'][layer])
        x = x + sq_relu_mlp(rmsnorm(x, p['norm_mlp'][layer]), p['w_up'][layer], p['w_down'][layer])
    y = rmsnorm(x, p['norm_final'])
    return (y, jnp.stack(new_pool), jnp.stack(new_conv), jnp.stack(new_lru),
            jnp.stack(new_k), jnp.stack(new_v))


def setup_inputs(seed: int = 0) -> dict:
    key = jax.random.key(seed)
    ks = iter(jax.random.split(key, 64))

    def nrm(shape, scale):
        return jax.random.normal(next(ks), shape, jnp.float32) * scale

    def gain(shape):
        return 1.0 + 0.05 * jax.random.normal(next(ks), shape, jnp.float32)

    a_c = jax.random.uniform(next(ks), (N_EVEN, D_RNN), jnp.float32, 0.9, 0.999)
    s = a_c ** (1.0 / RG_C)
    rg_lambda = jnp.log(s) - jnp.log1p(-s)
    d_in_even = D_POOL + 2 * D_RNN
    d_qkv = (N_Q_HEADS + 2 * N_KV_HEADS) * HEAD_DIM
    return dict(
        x_prompt=nrm((BATCH, SEQ, D_MODEL), 1.0),
        x_sample=nrm((DEC_BATCH, DEC_SEQ, D_MODEL), 1.0),
        state_pool=nrm((N_EVEN, DEC_BATCH, POOL_STATE, D_POOL), 1.0),
        state_conv=nrm((N_EVEN, DEC_BATCH, CONV_WIDTH - 1, D_RNN), 1.0),
        state_lru=nrm((N_EVEN, DEC_BATCH, D_RNN), 0.5),
        cache_swa_k=nrm((N_ODD, DEC_BATCH, WINDOW, N_KV_HEADS, HEAD_DIM), 1.0),
        cache_swa_v=nrm((N_ODD, DEC_BATCH, WINDOW, N_KV_HEADS, HEAD_DIM), 1.0),
        cache_mem_k=nrm((DEPTH, DEC_BATCH, N_MEM, N_MEM_HEADS, MEM_HEAD_DIM), 1.0),
        cache_mem_v=nrm((DEPTH, DEC_BATCH, N_MEM, N_MEM_HEADS, MEM_HEAD_DIM), 1.0),
        mem_prompt=nrm((BATCH, N_MEM, D_MODEL), 1.0),
        norm_mix=gain((DEPTH, D_MODEL)),
        norm_cross=gain((DEPTH, D_MODEL)),
        norm_mem=gain((DEPTH, D_MODEL)),
        norm_mlp=gain((DEPTH, D_MODEL)),
        norm_final=gain((D_MODEL,)),
        w_in_even=nrm((N_EVEN, D_MODEL, d_in_even), D_MODEL ** -0.5),
        conv_w=nrm((N_EVEN, CONV_WIDTH, D_RNN), CONV_WIDTH ** -0.5),
        conv_b=nrm((N_EVEN, D_RNN), 0.01),
        w_rg_a=nrm((N_EVEN, N_RNN_BLOCKS, RNN_BLOCK, RNN_BLOCK), RNN_BLOCK ** -0.5),
        b_rg_a=nrm((N_EVEN, D_RNN), 0.01),
        w_rg_x=nrm((N_EVEN, N_RNN_BLOCKS, RNN_BLOCK, RNN_BLOCK), RNN_BLOCK ** -0.5),
        b_rg_x=nrm((N_EVEN, D_RNN), 0.01),
        rg_lambda=rg_lambda,
        pool_w=nrm((N_EVEN, N_POOL_GROUPS, POOL_GROUP, POOL_GROUP), POOL_GROUP ** -0.5),
        pool_scale=gain((N_EVEN, D_POOL)),
        w_out_even=nrm((N_EVEN, D_POOL + D_RNN, D_MODEL), (D_POOL + D_RNN) ** -0.5),
        w_qkv_odd=nrm((N_ODD, D_MODEL, d_qkv), D_MODEL ** -0.5),
        attn_sinks=nrm((N_ODD, N_Q_HEADS), 0.5),
        w_o_odd=nrm((N_ODD, N_Q_HEADS * HEAD_DIM, D_MODEL), (N_Q_HEADS * HEAD_DIM) ** -0.5),
        w_mq=nrm((DEPTH, D_MODEL, D_MODEL), D_MODEL ** -0.5),
        w_mk=nrm((DEPTH, D_MODEL, D_MODEL), D_MODEL ** -0.5),
        w_mv=nrm((DEPTH, D_MODEL, D_MODEL), D_MODEL ** -0.5),
        w_mo=nrm((DEPTH, D_MODEL, D_MODEL), D_MODEL ** -0.5),
        w_up=nrm((DEPTH, D_MODEL, D_FF), D_MODEL ** -0.5),
        w_down=nrm((DEPTH, D_FF, D_MODEL), D_FF ** -0.5),
    )


def reference(x_prompt, x_sample, state_pool, state_conv, state_lru, cache_swa_k, cache_swa_v,
              cache_mem_k, cache_mem_v, mem_prompt, norm_mix, norm_cross, norm_mem, norm_mlp, norm_final,
              w_in_even, conv_w, conv_b, w_rg_a, b_rg_a, w_rg_x, b_rg_x, rg_lambda, pool_w, pool_scale,
              w_out_even, w_qkv_odd, attn_sinks, w_o_odd, w_mq, w_mk, w_mv, w_mo, w_up, w_down):
    p = dict(norm_mix=norm_mix, norm_cross=norm_cross, norm_mlp=norm_mlp, norm_final=norm_final,
             w_in_even=w_in_even, conv_w=conv_w, conv_b=conv_b, w_rg_a=w_rg_a, b_rg_a=b_rg_a,
             w_rg_x=w_rg_x, b_rg_x=b_rg_x, rg_lambda=rg_lambda, pool_w=pool_w, pool_scale=pool_scale,
             w_out_even=w_out_even, w_qkv_odd=w_qkv_odd, attn_sinks=attn_sinks, w_o_odd=w_o_odd,
             w_mq=w_mq, w_mo=w_mo, w_up=w_up, w_down=w_down)
    B = x_prompt.shape[0]
    dt = x_prompt.dtype
    mks, mvs = [], []
    for layer in range(DEPTH):
        mk, mv = mem_kv(mem_prompt, norm_mem[layer], w_mk[layer], w_mv[layer])
        mks.append(mk)
        mvs.append(mv)
    mem_k_p = jnp.stack(mks)
    mem_v_p = jnp.stack(mvs)
    zero_pool = jnp.zeros((N_EVEN, B, POOL_STATE, D_POOL), dt)
    zero_conv = jnp.zeros((N_EVEN, B, CONV_WIDTH - 1, D_RNN), dt)
    zero_lru = jnp.zeros((N_EVEN, B, D_RNN), dt)
    zero_kv = jnp.zeros((N_ODD, B, WINDOW, N_KV_HEADS, HEAD_DIM), dt)
    y_prompt, pool_p, conv_p, lru_p, swa_k_p, swa_v_p = trunk(
        x_prompt, 0, False, zero_pool, zero_conv, zero_lru, zero_kv, zero_kv, mem_k_p, mem_v_p, p)
    y_sample, pool_s, conv_s, lru_s, swa_k_s, swa_v_s = trunk(
        x_sample, PAST_LEN, True, state_pool, state_conv, state_lru, cache_swa_k, cache_swa_v,
        cache_mem_k, cache_mem_v, p)
    return (y_prompt, y_sample, pool_p, conv_p, lru_p, swa_k_p, swa_v_p, mem_k_p, mem_v_p,
            pool_s, conv_s, lru_s, swa_k_s, swa_v_s)
```

```python
import os
import numpy as np
import concourse.bass as bass
import concourse.mybir as mybir
from concourse.bass_utils import run_bass_kernel_spmd

F32 = mybir.dt.float32
BF16 = mybir.dt.bfloat16
AF = mybir.ActivationFunctionType
ALU = mybir.AluOpType

NCORE = 8
D = 1024
KC = 8
TP = 512
SEQ = 2048
NB = 4
DEC = 64
NMEM = 256
EPS = 1e-6
NW = 4
GELU_K = 0.7978845608028654

COMPUTE = ("pe", "act", "dve", "pool")


class Prog:
    def __init__(self):
        self.ops = []
        self.lastw = {}
        self.readers = {}
        self.chan_n = {}

    def add(self, eng, fn, r=(), w=(), chan=None):
        i = len(self.ops)
        psr = [k for k in r if isinstance(k, tuple) and k[0] == "ps"]
        if psr:
            r = [k for k in r if not (isinstance(k, tuple) and k[0] == "ps")]
            w = list(w) + psr
        deps = set()
        raw = set()
        for k in r:
            j = self.lastw.get(k)
            if j is not None:
                deps.add(j)
                raw.add(j)
        for k in w:
            j = self.lastw.get(k)
            if j is not None:
                deps.add(j)
            for j in self.readers.get(k, ()):
                deps.add(j)
        deps.discard(i)
        cnt = None
        if chan is not None:
            cnt = self.chan_n.get(chan, 0) + 1
            self.chan_n[chan] = cnt
        self.ops.append(dict(eng=eng, fn=fn, deps=deps, raw=raw, chan=chan, cnt=cnt))
        for k in w:
            self.lastw[k] = i
            self.readers[k] = []
        for k in r:
            lst = self.readers.setdefault(k, [])
            if chan is None:
                lst[:] = [j for j in lst if not (self.ops[j]["chan"] is None and self.ops[j]["eng"] == eng)]
            lst.append(i)
        return i

    def emit(self, nc, block, sems, chan_sems):
        ops = self.ops
        for i, op in enumerate(ops):
            best = {}
            dmas = []
            for j in op["deps"]:
                d = ops[j]
                if d["chan"] is not None:
                    dmas.append(j)
                    continue
                if d["eng"] == op["eng"] and op["chan"] is None and op["eng"] == "pe":
                    continue
                if d["eng"] == op["eng"] and op["chan"] is not None:
                    pass
                if j > best.get(d["eng"], -1):
                    best[d["eng"]] = j
            op["wait_c"] = best
            op["wait_d"] = dmas
        sig = [False] * len(ops)
        for op in ops:
            for j in op["wait_c"].values():
                sig[j] = True
        counts = {e: 0 for e in COMPUTE}
        for i, op in enumerate(ops):
            if op["chan"] is None and sig[i]:
                counts[op["eng"]] += 1
                op["sigval"] = counts[op["eng"]]
        streams = {}
        for i, op in enumerate(ops):
            streams.setdefault(op["eng"], []).append(i)

        def run_stream(name, e):
            waited = {}
            for i in streams.get(name, []):
                op = ops[i]
                for eng2, j in op["wait_c"].items():
                    v = ops[j]["sigval"]
                    key = ("c", eng2)
                    if waited.get(key, 0) < v:
                        e.wait_ge(sems[eng2], v)
                        waited[key] = v
                for j in op["wait_d"]:
                    d = ops[j]
                    v = 16 * d["cnt"]
                    key = ("d", d["chan"])
                    if waited.get(key, 0) < v:
                        e.wait_ge(chan_sems[d["chan"]], v)
                        waited[key] = v
                if op["chan"] is not None:
                    v = 16 * (op["cnt"] - 1)
                    key = ("d", op["chan"])
                    if v > 0 and waited.get(key, 0) < v:
                        e.wait_ge(chan_sems[op["chan"]], v)
                        waited[key] = v
                ins = op["fn"](e)
                if op["chan"] is not None:
                    ins.then_inc(chan_sems[op["chan"]], 16)
                elif sig[i]:
                    ins.then_inc(sems[op["eng"]], 1)
            if name in ("sp", "act", "pool"):
                final = {}
                for i in streams.get(name, []):
                    op = ops[i]
                    if op["chan"] is not None:
                        final[op["chan"]] = max(final.get(op["chan"], 0), 16 * op["cnt"])
                for ch, v in final.items():
                    if waited.get(("d", ch), 0) < v:
                        e.wait_ge(chan_sems[ch], v)

        @block.sync
        def _(e):
            run_stream("sp", e)

        @block.tensor
        def _(e):
            run_stream("pe", e)

        @block.scalar
        def _(e):
            run_stream("act", e)

        @block.vector
        def _(e):
            run_stream("dve", e)

        @block.gpsimd
        def _(e):
            run_stream("pool", e)


def block_catalogue():
    blocks = []
    for l in range(2):
        if l == 0:
            blocks += [("in", 1), ("in", 0), ("in", 2)] + [("out", j) for j in range(2)]
        else:
            blocks += [("qkv", j) for j in range(4)] + [("wo", j) for j in range(2)]
        blocks += [("mq%d" % l, j) for j in range(2)] + [("mo%d" % l, j) for j in range(2)]
        for half in range(2):
            blocks += [("up%d" % l, half * 4 + j) for j in range(4)]
            blocks += [("down%d" % l, half * 4 + j) for j in range(4)]
    memb = []
    for l in range(2):
        memb += [("mk%d" % l, j) for j in range(2)] + [("mv%d" % l, j) for j in range(2)]
    return blocks, memb


def build_program(nseq=NB, do_sample=True):
    nc = bass.Bass("TRN2", target_bir_lowering=False)
    P = Prog()

    def din(name, shape):
        return nc.dram_tensor(name, shape, F32, kind="ExternalInput").ap()

    def dout(name, shape):
        return nc.dram_tensor(name, shape, F32, kind="ExternalOutput").ap()

    x_prompt = din("x_prompt", [NB, SEQ, D])
    x_sample = din("x_sample", [NB * DEC, D])
    state_pool = din("state_pool", [NB, 15, 512])
    state_conv = din("state_conv", [NB, 3, 512])
    state_lru = din("state_lru", [NB, 512])
    cache_k = din("cache_swa_k", [NB, 128, 256])
    cache_v = din("cache_swa_v", [NB, 128, 256])
    cmem_k = din("cache_mem_k", [2, NB, NMEM, D])
    cmem_v = din("cache_mem_v", [2, NB, NMEM, D])
    mem_prompt = din("mem_prompt", [NB, NMEM, D])
    vecs = din("vecs", [128, 128])
    sinks = din("attn_sinks", [1, 16])
    ident_d = din("ident", [128, 128])
    W = dict(
        w_in=din("w_in", [D, 1536]), w_out=din("w_out", [D, D]), w_qkv=din("w_qkv", [D, 1536]),
        w_o=din("w_o", [D, D]), w_mq=din("w_mq", [2, D, D]), w_mk=din("w_mk", [2, D, D]),
        w_mv=din("w_mv", [2, D, D]), w_mo=din("w_mo", [2, D, D]), w_up=din("w_up", [2, D, 4 * D]),
        w_down=din("w_down", [2, 4 * D, D]),
    )
    pool_w_d = din("pool_w", [4, 128, 128])
    w_rg_a_d = din("w_rg_a", [8, 64, 64])
    w_rg_x_d = din("w_rg_x", [8, 64, 64])

    y_prompt = dout("y_prompt", [NB, SEQ, D])
    y_sample = dout("y_sample", [NB * DEC, D])
    pool_p = dout("pool_p", [NB, 15, 512])
    conv_p = dout("conv_p", [NB, 3, 512])
    lru_p = dout("lru_p", [NB, 512])
    swa_k_p = dout("swa_k_p", [NB, 128, 256])
    swa_v_p = dout("swa_v_p", [NB, 128, 256])
    mem_k_p = dout("mem_k_p", [2, NB, NMEM, D])
    mem_v_p = dout("mem_v_p", [2, NB, NMEM, D])
    pool_s = dout("pool_s", [NB, 15, 512])
    conv_s = dout("conv_s", [NB, 3, 512])
    lru_s = dout("lru_s", [NB, 512])
    swa_k_s = dout("swa_k_s", [NB, 128, 256])
    swa_v_s = dout("swa_v_s", [NB, 128, 256])

    tile_blocks, mem_blocks = block_catalogue()
    all_blocks = tile_blocks + mem_blocks
    bid = {b: i for i, b in enumerate(all_blocks)}
    wscr = nc.dram_tensor("wscr", [len(all_blocks), 128, 4096], BF16, kind="Internal").ap()

    import contextlib
    es = contextlib.ExitStack()
    with es:
        def sb(name, shape, dt=F32):
            return es.enter_context(nc.sbuf_tensor(name, shape, dt))

        x = sb("x", [128, 8, TP])
        h = sb("h", [128, 8, TP], BF16)
        up = sb("up", [128, 4, 16 + TP])
        ux = sb("ux", [128, 4, 4 + TP])
        ug = sb("ug", [128, 4, TP])
        ycat = sb("ycat", [128, 8, TP], BF16)
        NTMP = 7
        tmp = [sb("tmp%d" % i, [128, 16 + TP]) for i in range(NTMP)]
        xc = sb("xc", [128, 4, TP])
        xcb = sb("xcb", [128, 4, TP], BF16)
        dbf = sb("dbf", [128, 4, TP], BF16)
        q = sb("q", [128, 8, TP], BF16)
        kbuf = sb("kbuf", [128, 4, 768], BF16)
        vtok = sb("vtok", [64, 12, 512], BF16)
        hid = sb("hid", [128, 16, TP], BF16)
        wring = [sb("wr%d" % i, [128, 4096], BF16) for i in range(NW)]
        memk = [sb("memk%d" % i, [128, 8, NMEM], BF16) for i in range(2)]
        memv = [sb("memv%d" % i, [128, 2, D], BF16) for i in range(2)]
        NSTG = 4
        stg = [sb("stg%d" % i, [128, 512]) for i in range(NSTG)]
        pT2 = [sb("pT2_%d" % i, [128, 2, TP], BF16) for i in range(2)]
        rden = [sb("rden%d" % i, [128, TP]) for i in range(2)]
        pTs = [sb("pTs%d" % i, [64, 3, 256], BF16) for i in range(2)]
        dns = [sb("dns%d" % i, [128, 256]) for i in range(2)]
        ident = sb("ident_sb", [128, 128])
        ones_bf = sb("ones_bf", [128, 128], BF16)
        ones_f = sb("ones_f", [128, 64])
        cvec = sb("cvec", [128, 128])
        negb = sb("negb", [128, 8])
        c8 = sb("c8", [128, 8])
        epsc = sb("epsc", [128, 1])
        onec = sb("onec", [128, 1])
        esink = sb("esink", [128, 16])
        esx = sb("esx", [64, 4, 256], BF16)
        poolw = sb("poolw", [128, 4, 128], BF16)
        wbd = sb("wbd", [128, 8, 128], BF16)
        invc = sb("invc", [128, 4, 16])
        hstate = sb("hstate", [128, 4, 4])
        hstate_s = sb("hstate_s", [128, 4, 4])
        tpad = sb("tpad", [128, 128])
        ps = [es.enter_context(nc.psum_tensor("ps%d" % i, [128, 512], F32)) for i in range(8)]

        COL = {}
        col = 0
        for nm, n in [("norm_mix0", 8), ("norm_mix1", 8), ("norm_cross0", 8), ("norm_cross1", 8),
                      ("norm_mem0", 8), ("norm_mem1", 8), ("norm_mlp0", 8), ("norm_mlp1", 8),
                      ("norm_final", 8), ("conv_w0", 4), ("conv_w1", 4), ("conv_w2", 4), ("conv_w3", 4),
                      ("conv_b", 4), ("b_rg_a", 4), ("b_rg_x", 4), ("rg_lambda", 4), ("pool_scale", 4)]:
            COL[nm] = col
            col += n
        assert col <= 128

        bank_ctr = [0]

        def nb():
            b = bank_ctr[0] % 8
            bank_ctr[0] += 1
            return b

        stg_ctr = [0]

        def nstg():
            s = stg_ctr[0] % NSTG
            stg_ctr[0] += 1
            return s

        def MM(out, lhsT, rhs, start, stop, r, w):
            P.add("pe", lambda e: e.matmul(out, lhsT=lhsT, rhs=rhs, start=start, stop=stop), r, w)

        def TR(out, in_, r, w):
            P.add("pe", lambda e: e.transpose(out, in_, ident[:, :]), list(r) + ["ident"], w)

        def ACT(out, in_, func, r, w, bias=None, scale=1.0):
            if bias is None:
                P.add("act", lambda e: e.activation(out=out, in_=in_, func=func, scale=scale), r, w)
            else:
                P.add("act", lambda e: e.activation(out=out, in_=in_, func=func, bias=bias, scale=scale), r, w)

        def TT(eng, out, in0, in1, op, r, w):
            P.add(eng, lambda e: e.tensor_tensor(out=out, in0=in0, in1=in1, op=op), r, w)

        def TS(eng, out, in0, s1, s2, op0, op1, r, w):
            if s2 is None:
                P.add(eng, lambda e: e.tensor_scalar(out=out, in0=in0, scalar1=s1, scalar2=None, op0=op0), r, w)
            else:
                P.add(eng, lambda e: e.tensor_scalar(out=out, in0=in0, scalar1=s1, scalar2=s2, op0=op0, op1=op1), r, w)

        def STT(out, in0, scalar, in1, op0, op1, r, w):
            P.add("dve", lambda e: e.scalar_tensor_tensor(out=out, in0=in0, scalar=scalar, in1=in1, op0=op0, op1=op1), r, w)

        def CP(eng, out, in_, r, w):
            if eng == "act":
                P.add("act", lambda e: e.copy(out=out, in_=in_), r, w)
            else:
                P.add(eng, lambda e: e.tensor_copy(out=out, in_=in_), r, w)

        def MSET(eng, ap, val, w):
            P.add(eng, lambda e: e.memset(ap, val), (), w)

        def DMA(queue, out, in_, r, w, chan):
            P.add(queue, lambda e: e.dma_start(out=out, in_=in_), r, w, chan=chan)

        evac_ctr = [0]

        def evac_eng():
            evac_ctr[0] += 1
            return "act" if evac_ctr[0] % 2 else "dve"

        def setup_consts():
            DMA("sp", ident[:, :], ident_d, (), ["ident"], "c_ident")
            DMA("sp", stg[0][:, 0:128], vecs, (), [("stg", 0)], "c_vecs")
            for i in range(1, NSTG):
                MSET("pool", stg[i][:, :], 0.0, [("stg", i)])
            MSET("pool", tpad[:, :], 0.0, ["tpad"])
            MSET("dve", ones_bf[:, :], 1.0, ["ones"])
            MSET("dve", ones_f[:, :], 1.0, ["ones_f"])
            MSET("dve", epsc[:, :], EPS, ["epsc"])
            MSET("dve", onec[:, :], 1.0, ["onec"])
            MSET("dve", hstate[:, :, :], 0.0, ["hstate"])
            b = nb()
            TR(ps[b][:, 0:128], stg[0][:, 0:128], [("stg", 0)], [("ps", b)])
            CP("dve", cvec[:, :], ps[b][:, 0:128], [("ps", b)], ["cvec"])
            ca = COL["b_rg_a"]
            TS("dve", negb[:, :], cvec[:, ca:ca + 8], -1.0, None, ALU.mult, None, ["cvec"], ["negb"])
            cl = COL["rg_lambda"]
            ACT(c8[:, 0:4], cvec[:, cl:cl + 4], AF.Exp, ["cvec"], ["c8"], scale=-1.0)
            ACT(c8[:, 0:4], c8[:, 0:4], AF.Ln, ["c8", "onec"], ["c8"], bias=onec[:, 0:1])
            TS("dve", c8[:, 4:8], c8[:, 0:4], -16.0, None, ALU.mult, None, ["c8"], ["c8b"])
            TS("dve", c8[:, 0:4], c8[:, 0:4], -8.0, None, ALU.mult, None, ["c8", "c8b"], ["c8"])
            DMA("sp", esink[:, :], sinks.partition_broadcast(128), (), ["esink"], "c_sink")
            ACT(esink[:, :], esink[:, :], AF.Exp, ["esink"], ["esink"])
            MSET("pool", esx[:, :, :], 0.0, ["esx"])
            for p0 in (0, 32):
                pr = slice(p0, p0 + 1)
                for kv in range(4):
                    for par in range(2):
                        for j2 in range(2):
                            g = kv * 4 + 2 * j2 + par
                            o = (par * 2 + j2) * 64
                            TS("dve", ug[pr, kv, o:o + 64], ones_f[pr, :], esink[pr, g:g + 1], None, ALU.mult, None,
                               ["esink", "ones_f"], [("ug", kv)])
                ugk = [("ug", kv) for kv in range(4)]
                if p0 == 0:
                    CP("dve", esx[pr, :, :], ug[pr, :, 0:256], ugk, ["esx"])
                else:
                    CP("dve", dbf[pr, :, 0:256], ug[pr, :, 0:256], ugk, [("dbf", 0)])
                    CP("dve", ug[pr, :, 256:512], dbf[pr, :, 0:256], [("dbf", 0)], ugk)
                    TT("dve", ug[pr, :, 256:512], ug[pr, :, 0:256], ug[pr, :, 256:512], ALU.subtract, ugk, ugk)
                    CP("dve", esx[pr, :, :], ug[pr, :, 256:512], ugk, ["esx"])
            DMA("sp", tmp[0][:, 0:512].rearrange("p (g e) -> p g e", g=4), pool_w_d.rearrange("g c e -> c g e"),
                (), [("tmp", 0)], "c_pw")
            CP("dve", poolw[:, :, :], tmp[0][:, 0:512].rearrange("p (g e) -> p g e", g=4), [("tmp", 0)], ["poolw"])
            for wi, wd in enumerate((w_rg_a_d, w_rg_x_d)):
                t = tmp[1 + wi]
                MSET("pool", t[:, 0:512], 0.0, [("tmp", 1 + wi)])
                tv = t[:, 0:512].rearrange("p (c j) -> p c j", c=4)
                src = wd.rearrange("(c r) i j -> r i c j", r=2)
                for r_ in range(2):
                    DMA("sp", tv[r_ * 64:(r_ + 1) * 64, :, r_ * 64:(r_ + 1) * 64], src[r_], (), [("tmp", 1 + wi)],
                        "c_bd%d%d" % (wi, r_))
                CP("dve", wbd[:, wi * 4:(wi + 1) * 4, :], tv, [("tmp", 1 + wi)], ["wbd"])
            for g in range(4):
                win = 2 << g
                for t_ in range(15):
                    MSET("pool", invc[:, g, t_:t_ + 1], 1.0 / min(t_ + 1, win), ["invc"])

        def wsrc(name, j, half):
            def std(Wm, j):
                src = Wm.rearrange("(k p) n -> p k n", p=128)[:, half * 4:(half + 1) * 4, j * 512:(j + 1) * 512]
                return [(lambda s: s.rearrange("p (k n) -> p k n", k=4), src)]
            if name == "in":
                return std(W["w_in"], j)
            if name == "out":
                return std(W["w_out"], j)
            if name == "wo":
                return std(W["w_o"], j)
            if name[:2] in ("mq", "mo", "mk", "mv", "up"):
                l = int(name[-1])
                return std(W["w_" + name[:-1]][l], j)
            if name.startswith("down"):
                l = int(name[-1])
                hh, cb = j // 4, j % 4
                src = W["w_down"][l].rearrange("(k p) n -> p k n", p=128)[
                    :, hh * 16 + half * 8: hh * 16 + half * 8 + 8, cb * 256:(cb + 1) * 256]
                return [(lambda s: s.rearrange("p (k n) -> p k n", k=8), src)]
            if name == "qkv":
                Wr = W["w_qkv"].rearrange("(k p) n -> p k n", p=128)
                if j < 2:
                    return std(W["w_qkv"], j)
                c0 = 1024 if j == 2 else 1280
                res = []
                for kk in range(4):
                    src = Wr[:, half * 4 + kk, c0:c0 + 256].rearrange("p (v d) -> p v d", v=4)
                    for r_ in range(2):
                        res.append((lambda s, r_=r_, kk=kk: s.rearrange("p (k v r d) -> p k v r d", k=4, v=4, r=2)[:, kk, :, r_, :], src))
                return res
            raise KeyError(name)

        def prepass():
            stage = [(x[:, 0:4, :].rearrange("p a b -> p (a b)"), [("x", c) for c in range(4)]),
                     (x[:, 4:8, :].rearrange("p a b -> p (a b)"), [("x", c) for c in range(4, 8)]),
                     (xc[:, :, :].rearrange("p a b -> p (a b)"), [("xc", c) for c in range(4)])]
            n = 0
            for bi, (name, j) in enumerate(all_blocks):
                for half in range(2):
                    sap, skeys = stage[n % 3]
                    for k_, (vf, src) in enumerate(wsrc(name, j, half)):
                        P.add("sp", (lambda e, o=vf(sap), s=src: e.dma_start(out=o, in_=s)), (), skeys,
                              chan="pp_ld%d_%d" % (n % 3, k_))
                    slot = (n // 2) % NW
                    dstv = wring[slot][:, half * 2048:(half + 1) * 2048]
                    eng = "dve" if n % 2 == 0 else "pool"
                    CP(eng, dstv, sap, skeys, [("w", slot, half)])
                    if half == 1:
                        P.add("act", (lambda e, o=wscr[bi], s=wring[slot][:, :]: e.dma_start(out=o, in_=s)),
                              [("w", slot, 0), ("w", slot, 1)], [("wscr", bi)], chan="pp_st%d" % slot)
                    n += 1

        DEBUG_ONDEMAND = int(os.environ.get("KDEBUG_STAGE", "99")) < 99
        MEMMASK = int(os.environ.get("KDEBUG_MEM", "255"))
        wseq = []
        wpos = [0, 0]

        def w_issue_upto(k):
            while wpos[1] < min(k, len(wseq)):
                i = wpos[1]
                b = wseq[i]
                if b is None:
                    break
                slot = i % NW
                if not DEBUG_ONDEMAND and i < CONV_N:
                    name_, j_ = all_blocks[b]
                    for half in range(2):
                        sap = wring[slot][:, half * 2048:(half + 1) * 2048]
                        for k_, (vf, src) in enumerate(wsrc(name_, j_, half)):
                            P.add("pool", (lambda e, o=vf(sap), s_=src: e.dma_start(out=o, in_=s_)), (),
                                  [("w", slot, half)], chan="cw%d_%d_%d" % (slot, half, k_))
                else:
                    DMA("sp", wring[slot][:, :], wscr[b], [("wscr", b)], [("w", slot, 0), ("w", slot, 1)], "w%d" % slot)
                wpos[1] += 1

        CONV_N = len(all_blocks)

        def wnext(name, j):
            i = wpos[0]
            if not DEBUG_ONDEMAND and 0 < i <= CONV_N:
                pslot = (i - 1) % NW
                pb = wseq[i - 1]
                P.add("sp", (lambda e, o=wscr[pb], s_=wring[pslot][:, :]: e.dma_start(out=o, in_=s_)),
                      [("w", pslot, 0), ("w", pslot, 1)], [("wscr", pb)], chan="cst%d" % pslot)
            if DEBUG_ONDEMAND:
                while len(wseq) <= i:
                    wseq.append(None)
                wseq[i] = bid[(name, j)]
                w_issue_upto(i + 1)
            assert wseq[i] == bid[(name, j)], (i, name, j, all_blocks[wseq[i]])
            w_issue_upto(i + NW)
            wpos[0] += 1
            slot = i % NW
            return wring[slot], [("w", slot, 0), ("w", slot, 1)]

        def rmsnorm(src, skey, dst, dkey, gname, T, inplace_out=None, out_keys=None, second=None):
            for c in range(8):
                if c % 2 == 0:
                    ACT(dst(c), src(c), AF.Square, [skey(c)], [dkey(c)])
                else:
                    TT("dve", dst(c), src(c), src(c), ALU.mult, [skey(c)], [dkey(c)])
            b = nb()
            for c in range(8):
                MM(ps[b][:, 0:T], ones_bf[:, :], dst(c), c == 0, c == 7, [dkey(c), "ones"], [("ps", b)])
            t = tmp[6]
            ACT(t[:, 0:T], ps[b][:, 0:T], AF.Ln, [("ps", b), "epsc"], [("tmp", 6)], bias=epsc[:, 0:1], scale=1.0 / D)
            ACT(t[:, 0:T], t[:, 0:T], AF.Exp, [("tmp", 6)], [("tmp", 6)], scale=-0.5)
            g0 = COL[gname]
            for c in range(8):
                if inplace_out is None:
                    STT(dst(c), src(c), cvec[:, g0 + c:g0 + c + 1], t[:, 0:T], ALU.mult, ALU.mult,
                        [skey(c), ("tmp", 6), "cvec"], [dkey(c)])
                else:
                    STT(inplace_out(c), src(c), cvec[:, g0 + c:g0 + c + 1], t[:, 0:T], ALU.mult, ALU.mult,
                        [skey(c), ("tmp", 6), "cvec", dkey(c)], out_keys(c))
            if second is not None:
                dst2, dkey2, gname2 = second
                g2 = COL[gname2]
                for c in range(8):
                    STT(dst2(c), src(c), cvec[:, g2 + c:g2 + c + 1], t[:, 0:T], ALU.mult, ALU.mult,
                        [skey(c), ("tmp", 6), "cvec"], [dkey2(c)])

        def proj_fm(name, nblk, src, skey, T, evac, kc=8, cols=512, kouter=False):
            ncol_chunks = cols // 128
            for j in range(nblk):
                wt, wk = wnext(name, j)
                wv = wt[:, :].rearrange("p (k n) -> p k n", k=kc)
                if j == 0 and kouter:
                    banks = [nb() for _ in range(ncol_chunks)]
                    for k in range(kc):
                        for oc in range(ncol_chunks):
                            MM(ps[banks[oc]][:, 0:T], wv[:, k, oc * 128:(oc + 1) * 128], src(k), k == 0, k == kc - 1,
                               wk + [skey(k)], [("ps", banks[oc])])
                    for oc in range(ncol_chunks):
                        evac(j * ncol_chunks + oc, banks[oc])
                    continue
                for oc in range(ncol_chunks):
                    b = nb()
                    for k in range(kc):
                        MM(ps[b][:, 0:T], wv[:, k, oc * 128:(oc + 1) * 128], src(k), k == 0, k == kc - 1,
                           wk + [skey(k)], [("ps", b)])
                    evac(j * ncol_chunks + oc, b)

        def resid_evac(T):
            def f(oc, b):
                TT("dve", x[:, oc, 0:T], ps[b][:, 0:T], x[:, oc, 0:T], ALU.add, [("ps", b), ("x", oc)], [("x", oc)])
            return f

        def load_T(dstf, dkeyf, src2d, ntok, ncols, queue="sp"):
            for t0 in range(0, ntok, 128):
                n = min(128, ntok - t0)
                for c0 in range(0, ncols, 512):
                    w_ = min(512, ncols - c0)
                    s = nstg()
                    DMA(queue, stg[s][0:n, 0:w_], src2d[t0:t0 + n, c0:c0 + w_], (), [("stg", s)], "stg%d" % s)
                    b = nb()
                    nch = w_ // 128
                    for cc in range(nch):
                        TR(ps[b][:, cc * 128:(cc + 1) * 128], stg[s][:, cc * 128:(cc + 1) * 128], [("stg", s)], [("ps", b)])
                    src = ps[b][:, 0:nch * 128].rearrange("p (c t) -> p c t", c=nch)[:, :, 0:n]
                    CP(evac_eng(), dstf(c0 // 128, nch, t0, n), src, [("ps", b)],
                       [dkeyf(c0 // 128 + cc) for cc in range(nch)])

        def store_T(srcf, skeyf, dst2d, ntok, ncols, pad=False):
            for t0 in range(0, ntok, 128):
                n = min(128, ntok - t0)
                for c0 in range(0, ncols, 512):
                    w_ = min(512, ncols - c0)
                    nch = w_ // 128
                    b = nb()
                    for cc in range(nch):
                        c = c0 // 128 + cc
                        if n == 128:
                            TR(ps[b][:, cc * 128:(cc + 1) * 128], srcf(c, t0, n), [skeyf(c)], [("ps", b)])
                        else:
                            CP("dve", tpad[:, 0:n], srcf(c, t0, n), [skeyf(c)], ["tpad"])
                            TR(ps[b][:, cc * 128:(cc + 1) * 128], tpad[:, :], ["tpad"], [("ps", b)])
                    s = nstg()
                    CP(evac_eng(), stg[s][0:n, 0:w_], ps[b][0:n, 0:w_], [("ps", b)], [("stg", s)])
                    DMA("act", dst2d[t0:t0 + n, c0:c0 + w_], stg[s][0:n, 0:w_], [("stg", s)], [], "stg%d" % s)

        mem_front_done = set()

        def mem_front(bl):
            if bl in mem_front_done or bl >= nseq:
                return
            mem_front_done.add(bl)
            memx = lambda c: xc[:, :, :].rearrange("p a (h t) -> p (a h) t", h=2)[:, c, :]
            memxk = lambda c: ("xc", c // 2)
            mn0 = lambda c: xcb[:, :, :].rearrange("p a (h t) -> p (a h) t", h=2)[:, c, :]
            mn0k = lambda c: ("xcb", c // 2)
            mn1 = lambda c: dbf[:, :, :].rearrange("p a (h t) -> p (a h) t", h=2)[:, c, :]
            mn1k = lambda c: ("dbf", c // 2)
            xv = xc[:, :, :].rearrange("p a (h t) -> p (a h) t", h=2)
            load_T(lambda c0, nch, t0, n: xv[:, c0:c0 + nch, t0:t0 + n], memxk, mem_prompt[bl], NMEM, D)
            rmsnorm(memx, memxk, mn0, mn0k, "norm_mem0", NMEM, second=(mn1, mn1k, "norm_mem1"))

        def mem_phase(bl):
            mem_front(bl)
            for l in range(2 if MEMMASK & 2 else 0):
                if l == 0:
                    mn = lambda c: xcb[:, :, :].rearrange("p a (h t) -> p (a h) t", h=2)[:, c, :]
                    mnk = lambda c: ("xcb", c // 2)
                else:
                    mn = lambda c: dbf[:, :, :].rearrange("p a (h t) -> p (a h) t", h=2)[:, c, :]
                    mnk = lambda c: ("dbf", c // 2)
                if not (MEMMASK & 4):
                    continue
                for j in range(2):
                    wt, wk = wnext("mk%d" % l, j)
                    wv = wt[:, :].rearrange("p (k n) -> p k n", k=8)
                    for oc in range(4):
                        b = nb()
                        for k in range(8):
                            MM(ps[b][:, 0:NMEM], wv[:, k, oc * 128:(oc + 1) * 128], mn(k), k == 0, k == 7,
                               wk + [mnk(k)], [("ps", b)])
                        CP(evac_eng(), memk[l][:, j * 4 + oc, :], ps[b][:, 0:NMEM], [("ps", b)], [("memk", l)])
                    for tb in range(2 if MEMMASK & 8 else 0):
                        b = nb()
                        for k in range(8):
                            MM(ps[b][:, :], mn(k)[:, tb * 128:(tb + 1) * 128], wv[:, k, :], k == 0, k == 7,
                               wk + [mnk(k)], [("ps", b)])
                        s = nstg()
                        CP(evac_eng(), stg[s][:, :], ps[b][:, :], [("ps", b)], [("stg", s)])
                        DMA("act", mem_k_p[l, bl, tb * 128:(tb + 1) * 128, j * 512:(j + 1) * 512], stg[s][:, :],
                            [("stg", s)], [], "stg%d" % s)
                for j in range(2 if MEMMASK & 16 else 0):
                    wt, wk = wnext("mv%d" % l, j)
                    wv = wt[:, :].rearrange("p (k n) -> p k n", k=8)
                    for tb in range(2):
                        b = nb()
                        for k in range(8):
                            MM(ps[b][:, :], mn(k)[:, tb * 128:(tb + 1) * 128], wv[:, k, :], k == 0, k == 7,
                               wk + [mnk(k)], [("ps", b)])
                        s = nstg()
                        CP("act", stg[s][:, :], ps[b][:, :], [("ps", b)], [("stg", s)])
                        CP("dve", memv[l][:, tb, j * 512:(j + 1) * 512], ps[b][:, :], [("ps", b)], [("memv", l)])
                        DMA("act", mem_v_p[l, bl, tb * 128:(tb + 1) * 128, j * 512:(j + 1) * 512], stg[s][:, :],
                            [("stg", s)], [], "stg%d" % s)

        def mem_load_sample(l, bl, slot):
            for tb in range(2):
                for c0 in range(0, D, 512):
                    s = nstg()
                    DMA("sp", stg[s][:, :], cmem_k[l, bl, tb * 128:(tb + 1) * 128, c0:c0 + 512], (), [("stg", s)], "stg%d" % s)
                    b = nb()
                    for cc in range(4):
                        TR(ps[b][:, cc * 128:(cc + 1) * 128], stg[s][:, cc * 128:(cc + 1) * 128], [("stg", s)], [("ps", b)])
                    CP(evac_eng(), memk[slot][:, c0 // 128:c0 // 128 + 4, tb * 128:(tb + 1) * 128],
                       ps[b][:, :].rearrange("p (c t) -> p c t", c=4), [("ps", b)], [("memk", slot)])
                    s = nstg()
                    DMA("sp", stg[s][:, :], cmem_v[l, bl, tb * 128:(tb + 1) * 128, c0:c0 + 512], (), [("stg", s)], "stg%d" % s)
                    CP(evac_eng(), memv[slot][:, tb, c0:c0 + 512], stg[s][:, :], [("stg", s)], [("memv", slot)])

        def cross_attn(l, segs, T, kvslot_of):
            rmsnorm(lambda c: x[:, c, 0:T], lambda c: ("x", c), lambda c: h[:, c, 0:T], lambda c: ("h", c),
                    "norm_cross%d" % l, T)

            def qev(oc, b):
                CP(evac_eng(), q[:, oc, 0:T], ps[b][:, 0:T], [("ps", b)], [("q", oc)])
            proj_fm("mq%d" % l, 2, lambda k: h[:, k, 0:T], lambda k: ("h", k), T, qev, kouter=True)
            pi = 0
            for sg in segs:
                c0, n = sg["c0"], sg["n"]
                slot = kvslot_of(sg)
                for hd in range(4):
                    pt = pT2[pi % 2]
                    ptk = ("pT2", pi % 2)
                    rd = rden[pi % 2]
                    rdk = ("rden", pi % 2)
                    pi += 1
                    for kb in range(2):
                        b = nb()
                        for dc in range(2):
                            MM(ps[b][:, 0:n], memk[slot][:, 2 * hd + dc, kb * 128:(kb + 1) * 128],
                               q[:, 2 * hd + dc, c0:c0 + n], dc == 0, dc == 1,
                               [("memk", slot), ("q", 2 * hd + dc)], [("ps", b)])
                        ACT(pt[:, kb, 0:n], ps[b][:, 0:n], AF.Exp, [("ps", b)], [ptk], scale=1.0 / 16.0)
                    b = nb()
                    for kb in range(2):
                        MM(ps[b][:, 0:n], ones_bf[:, :], pt[:, kb, 0:n], kb == 0, kb == 1, [ptk, "ones"], [("ps", b)])
                    ACT(rd[:, 0:n], ps[b][:, 0:n], AF.Ln, [("ps", b)], [rdk])
                    ACT(rd[:, 0:n], rd[:, 0:n], AF.Exp, [rdk], [rdk], scale=-1.0)
                    for dc in range(2):
                        b = nb()
                        for kb in range(2):
                            MM(ps[b][:, 0:n], memv[slot][:, kb, hd * 256 + dc * 128: hd * 256 + (dc + 1) * 128],
                               pt[:, kb, 0:n], kb == 0, kb == 1, [("memv", slot), ptk], [("ps", b)])
                        TT("dve", ycat[:, 2 * hd + dc, c0:c0 + n], ps[b][:, 0:n], rd[:, 0:n], ALU.mult,
                           [("ps", b), rdk], [("ycat", 2 * hd + dc)])
            proj_fm("mo%d" % l, 2, lambda k: ycat[:, k, 0:T], lambda k: ("ycat", k), T, resid_evac(T))

        def mlp(l, T):
            rmsnorm(lambda c: x[:, c, 0:T], lambda c: ("x", c), lambda c: h[:, c, 0:T], lambda c: ("h", c),
                    "norm_mlp%d" % l, T)
            rr = [0]
            for half in range(2):
                for j in range(4):
                    wt, wk = wnext("up%d" % l, half * 4 + j)
                    wv = wt[:, :].rearrange("p (k n) -> p k n", k=8)
                    kob = None
                    if half == 0 and j == 0:
                        kob = [nb() for _ in range(4)]
                        for k in range(8):
                            for oc in range(4):
                                MM(ps[kob[oc]][:, 0:T], wv[:, k, oc * 128:(oc + 1) * 128], h[:, k, 0:T], k == 0, k == 7,
                                   wk + [("h", k)], [("ps", kob[oc])])
                    for oc in range(4):
                        if kob is not None:
                            b = kob[oc]
                        else:
                            b = nb()
                            for k in range(8):
                                MM(ps[b][:, 0:T], wv[:, k, oc * 128:(oc + 1) * 128], h[:, k, 0:T], k == 0, k == 7,
                                   wk + [("h", k)], [("ps", b)])
                        ti = rr[0] % 4
                        rr[0] += 1
                        t = tmp[ti]
                        ACT(t[:, 0:T], ps[b][:, 0:T], AF.Relu, [("ps", b)], [("tmp", ti)])
                        TT("dve", hid[:, j * 4 + oc, 0:T], t[:, 0:T], t[:, 0:T], ALU.mult, [("tmp", ti)], [("hid", j * 4 + oc)])
                for cb in range(4):
                    wt, wk = wnext("down%d" % l, half * 4 + cb)
                    wv = wt[:, :].rearrange("p (k n) -> p k n", k=16)
                    for cl in range(2):
                        oc = cb * 2 + cl
                        b = nb()
                        for k in range(16):
                            MM(ps[b][:, 0:T], wv[:, k, cl * 128:(cl + 1) * 128], hid[:, k, 0:T], k == 0, k == 15,
                               wk + [("hid", k)], [("ps", b)])
                        TT("dve", x[:, oc, 0:T], ps[b][:, 0:T], x[:, oc, 0:T], ALU.add, [("ps", b), ("x", oc)], [("x", oc)])

        def even_layer(segs, T, sample):
            rmsnorm(lambda c: x[:, c, 0:T], lambda c: ("x", c), lambda c: h[:, c, 0:T], lambda c: ("h", c),
                    "norm_mix0", T)
            hs = hstate_s if sample else hstate
            hsk = "hstate_s" if sample else "hstate"
            def hidf(i):
                return hid[:, 2 * i:2 * i + 2, :].bitcast(F32).rearrange("p a b -> p (a b)"), [("hid", 2 * i), ("hid", 2 * i + 1)]
            def qf(i):
                return q[:, 2 * i:2 * i + 2, :].bitcast(F32).rearrange("p a b -> p (a b)"), [("q", 2 * i), ("q", 2 * i + 1)]
            TA = [hidf(c) for c in range(4)]
            TB = [hidf(4 + c) for c in range(4)]
            TC = [qf(c) for c in range(4)]
            TG = [(tmp[i][:, 0:TP], [("tmp", i)]) for i in (0, 1, 2, 6)]
            R4 = range(4)

            def in_block(j):
                wt, wk = wnext("in", j)
                wv = wt[:, :].rearrange("p (k n) -> p k n", k=8)
                kob = None
                if j == 1:
                    kob = [nb() for _ in range(4)]
                    for k in range(8):
                        for oc4 in range(4):
                            MM(ps[kob[oc4]][:, 0:T], wv[:, k, oc4 * 128:(oc4 + 1) * 128], h[:, k, 0:T], k == 0, k == 7,
                               wk + [("h", k)], [("ps", kob[oc4])])
                for oc4 in range(4):
                    if kob is not None:
                        b = kob[oc4]
                    else:
                        b = nb()
                        for k in range(8):
                            MM(ps[b][:, 0:T], wv[:, k, oc4 * 128:(oc4 + 1) * 128], h[:, k, 0:T], k == 0, k == 7,
                               wk + [("h", k)], [("ps", b)])
                    if j == 0:
                        for sg in segs:
                            CP("act", up[:, oc4, sg["ucol"]:sg["ucol"] + sg["n"]], ps[b][:, sg["c0"]:sg["c0"] + sg["n"]],
                               [("ps", b)], [("up", oc4)])
                    elif j == 1:
                        for sg in segs:
                            CP("act", ux[:, oc4, sg["xcol"]:sg["xcol"] + sg["n"]], ps[b][:, sg["c0"]:sg["c0"] + sg["n"]],
                               [("ps", b)], [("ux", oc4)])
                    else:
                        CP("act", ug[:, oc4, 0:T], ps[b][:, 0:T], [("ps", b)], [("ug", oc4)])

            in_block(1)
            for c in R4:
                cw = [COL["conv_w%d" % k] + c for k in range(4)]
                cbc = COL["conv_b"] + c
                for sg in segs:
                    xo, n, c0 = sg["xcol"], sg["n"], sg["c0"]
                    TS("dve", xc[:, c, c0:c0 + n], ux[:, c, xo:xo + n], cvec[:, cw[3]:cw[3] + 1], cvec[:, cbc:cbc + 1],
                       ALU.mult, ALU.add, [("ux", c), "cvec"], [("xc", c)])
                    for k in (2, 1, 0):
                        sh = 3 - k
                        STT(xc[:, c, c0:c0 + n], ux[:, c, xo - sh:xo - sh + n], cvec[:, cw[k]:cw[k] + 1], xc[:, c, c0:c0 + n],
                            ALU.mult, ALU.add, [("ux", c), ("uxh", c), "cvec", ("xc", c)], [("xc", c)])
                CP("act", xcb[:, c, 0:T], xc[:, c, 0:T], [("xc", c)], [("xcb", c)])
            in_block(0)
            flush_pending()
            for g in range(4):
                Wn = 2 << g
                for sg in segs:
                    uc, n, c0 = sg["ucol"], sg["n"], sg["c0"]
                    cur = lambda lo, hi, g=g, uc=uc: up[:, g, uc + lo:uc + hi]
                    curk = [("up", g), ("uph", g)]
                    for lev in range(1, g + 2):
                        sh = 1 << (lev - 1)
                        lo = -(Wn - (1 << lev))
                        ti = 4 + (lev % 2)
                        o = tmp[ti]
                        TT("dve", o[:, 16 + lo:16 + n], cur(lo, n), cur(lo - sh, n - sh), ALU.add, curk, [("tmp", ti)])
                        cur = lambda lo_, hi_, o=o: o[:, 16 + lo_:16 + hi_]
                        curk = [("tmp", ti)]
                    STT(dbf[:, g, c0:c0 + n], cur(0, n), 1.0 / Wn, up[:, g, uc:uc + n], ALU.mult, ALU.subtract,
                        curk + [("up", g)], [("dbf", g)])
                    if sg["first"] and not sample:
                        m = Wn - 1
                        t6 = tmp[3]
                        TT("dve", t6[:, 0:m], cur(0, m), invc[:, g, 0:m], ALU.mult, curk + ["invc"], [("tmp", 3)])
                        TT("dve", dbf[:, g, c0:c0 + m], t6[:, 0:m], up[:, g, uc:uc + m], ALU.subtract,
                           [("tmp", 3), ("up", g)], [("dbf", g)])
            in_block(2)
            for c in R4:
                G, gk = TG[c]
                u_ = ug[:, c, 0:T]
                ACT(G[:, 0:T], u_, AF.Square, [("ug", c)], gk)
                ACT(G[:, 0:T], G[:, 0:T], AF.Identity, gk + ["onec"], gk, bias=onec[:, 0:1], scale=0.044715)
                TT("dve", G[:, 0:T], G[:, 0:T], u_, ALU.mult, gk + [("ug", c)], gk)
            ba = COL["b_rg_a"]
            bx = COL["b_rg_x"]
            for c in R4:
                A, ak = TA[c]
                b_ = nb()
                MM(ps[b_][:, 0:T], wbd[:, c, :], xcb[:, c, 0:T], True, True, ["wbd", ("xcb", c)], [("ps", b_)])
                ACT(A[:, 0:T], ps[b_][:, 0:T], AF.Sigmoid, [("ps", b_), "cvec"], ak, bias=cvec[:, ba + c:ba + c + 1])
            for c in R4:
                Bt, bk = TB[c]
                b_ = nb()
                MM(ps[b_][:, 0:T], wbd[:, 4 + c, :], xcb[:, c, 0:T], True, True, ["wbd", ("xcb", c)], [("ps", b_)])
                ACT(Bt[:, 0:T], ps[b_][:, 0:T], AF.Sigmoid, [("ps", b_), "cvec"], bk, bias=cvec[:, bx + c:bx + c + 1])
            for c in R4:
                G, gk = TG[c]
                ACT(G[:, 0:T], G[:, 0:T], AF.Sigmoid, gk, gk, scale=2.0 * GELU_K)
            for g in range(4):
                b = nb()
                MM(ps[b][:, 0:T], poolw[:, g, :], dbf[:, g, 0:T], True, True, ["poolw", ("dbf", g)], [("ps", b)])
                pc = COL["pool_scale"] + g
                TS("dve", ycat[:, g, 0:T], ps[b][:, 0:T], cvec[:, pc:pc + 1], None, ALU.mult, None, [("ps", b), "cvec"], [("ycat", g)])
            for pair in ((0, 1), (2, 3)):
                for c in pair:
                    A, ak = TA[c]
                    C, ck = TC[c]
                    ACT(C[:, 0:T], A[:, 0:T], AF.Exp, ak + ["c8b"], ck, scale=c8[:, 4 + c:5 + c])
                for c in pair:
                    A, ak = TA[c]
                    ACT(A[:, 0:T], A[:, 0:T], AF.Exp, ak + ["c8"], ak, scale=c8[:, c:c + 1])
                for c in pair:
                    C, ck = TC[c]
                    ACT(C[:, 0:T], C[:, 0:T], AF.Ln, ck + ["onec"], ck, bias=onec[:, 0:1], scale=-1.0)
                for c in pair:
                    C, ck = TC[c]
                    Bt, bk = TB[c]
                    ACT(C[:, 0:T], C[:, 0:T], AF.Exp, ck, ck, scale=0.5)
                    TT("dve", Bt[:, 0:T], Bt[:, 0:T], C[:, 0:T], ALU.mult, bk + ck, bk)
                    TT("dve", Bt[:, 0:T], Bt[:, 0:T], xc[:, c, 0:T], ALU.mult, bk + [("xc", c)], bk)
                for c in pair:
                    A, ak = TA[c]
                    Bt, bk = TB[c]
                    C, ck = TC[c]
                    G, gk = TG[c]
                    for sg in segs:
                        n, c0, bl = sg["n"], sg["c0"], sg["bl"]
                        P.add("dve", (lambda e, o=C[:, c0:c0 + n], a_=A[:, c0:c0 + n], bb=Bt[:, c0:c0 + n],
                                      ini=hs[:, c, bl:bl + 1]:
                                      e.tensor_tensor_scan(out=o, data0=a_, data1=bb, initial=ini, op0=ALU.mult, op1=ALU.add)),
                              ak + bk + [hsk] + ck, ck)
                        CP("pool", hs[:, c, bl:bl + 1], C[:, c0 + n - 1:c0 + n], ck, [hsk])
                    TT("dve", G[:, 0:T], G[:, 0:T], ug[:, c, 0:T], ALU.mult, gk + [("ug", c)], gk)
                    TT("dve", ycat[:, 4 + c, 0:T], C[:, 0:T], G[:, 0:T], ALU.mult, ck + gk, [("ycat", 4 + c)])

            for sg in segs:
                uc, xo, n, bl = sg["ucol"], sg["xcol"], sg["n"], sg["bl"]
                if sg["last"]:
                    pd = pool_s if sample else pool_p
                    cd = conv_s if sample else conv_p
                    store_T(lambda c, t0, nn: up[:, c, uc + n - 15:uc + n], lambda c: ("up", c), pd[bl], 15, 512)
                    store_T(lambda c, t0, nn: ux[:, c, xo + n - 3:xo + n], lambda c: ("ux", c), cd[bl], 3, 512)
                else:
                    for g in range(4):
                        CP("pool", up[:, g, uc - 15:uc], up[:, g, uc + n - 15:uc + n], [("up", g)], [("uph", g)])
                        CP("pool", ux[:, g, xo - 3:xo], ux[:, g, xo + n - 3:xo + n], [("ux", g)], [("uxh", g)])
            proj_fm("out", 2, lambda k: ycat[:, k, 0:T], lambda k: ("ycat", k), T, resid_evac(T))

        def odd_layer(segs, T, sample):
            rmsnorm(lambda c: x[:, c, 0:T], lambda c: ("x", c), lambda c: h[:, c, 0:T], lambda c: ("h", c),
                    "norm_mix1", T)

            def qev(oc, b):
                CP(evac_eng(), q[:, oc, 0:T], ps[b][:, 0:T], [("ps", b)], [("q", oc)])
            proj_fm("qkv", 2, lambda k: h[:, k, 0:T], lambda k: ("h", k), T, qev, kouter=True)
            wt, wk = wnext("qkv", 2)
            wv = wt[:, :].rearrange("p (k n) -> p k n", k=8)
            for kv in range(4):
                b = nb()
                for k in range(8):
                    MM(ps[b][:, 0:T], wv[:, k, kv * 128:(kv + 1) * 128], h[:, k, 0:T], k == 0, k == 7,
                       wk + [("h", k)], [("ps", b)])
                for sg in segs:
                    kc0 = sg["kb0"] + 128
                    CP(evac_eng(), kbuf[:, kv, kc0:kc0 + sg["n"]], ps[b][:, sg["c0"]:sg["c0"] + sg["n"]],
                       [("ps", b)], [("kbuf", kv)])
            wv5 = wt[:, :].rearrange("p (k v r d) -> p k v r d", k=8, v=4, r=2)
            for sg in segs:
                if not sg["last"]:
                    continue
                bl, n, c0 = sg["bl"], sg["n"], sg["c0"]
                nrow = min(128, n)
                t0 = c0 + n - nrow
                b = nb()
                for k in range(8):
                    MM(ps[b][0:nrow, 0:256].rearrange("p (v d) -> p v d", v=4), h[:, k, t0:t0 + nrow], wv5[:, k, :, 0, :],
                       k == 0, k == 7, wk + [("h", k)], [("ps", b)])
                s = nstg()
                CP(evac_eng(), stg[s][0:nrow, 0:256], ps[b][0:nrow, 0:256], [("ps", b)], [("stg", s)])
                kd = swa_k_s if sample else swa_k_p
                DMA("act", kd[bl, 128 - nrow:128, :], stg[s][0:nrow, 0:256], [("stg", s)], [], "stg%d" % s)
                if sample:
                    DMA("sp", swa_k_s[bl, 0:64, :], cache_k[bl, 64:128, :], (), [], "d2d")
            wt, wk = wnext("qkv", 3)
            wv = wt[:, :].rearrange("p (k n) -> p k n", k=8)
            for sg in segs:
                n, c0, bl = sg["n"], sg["c0"], sg["bl"]
                for j in range(n // 64):
                    b = nb()
                    for k in range(8):
                        MM(ps[b][0:64, :], h[:, k, c0 + j * 64:c0 + (j + 1) * 64], wv[:, k, :], k == 0, k == 7,
                           wk + [("h", k)], [("ps", b)])
                    vs = sg["vb0"] + 2 + j
                    CP(evac_eng(), vtok[0:64, vs, :], ps[b][0:64, :], [("ps", b)], [("vtok", vs)])
                    if sg["last"] and j >= n // 64 - 2:
                        s = nstg()
                        CP(evac_eng(), stg[s][0:64, 0:256].rearrange("p (v d) -> p v d", v=4),
                           ps[b][0:64, :].rearrange("p (v r d) -> p v r d", v=4, r=2)[:, :, 0, :], [("ps", b)], [("stg", s)])
                        vd = swa_v_s if sample else swa_v_p
                        row0 = 128 - (n // 64 - j) * 64
                        DMA("act", vd[bl, row0:row0 + 64, :], stg[s][0:64, 0:256], [("stg", s)], [], "stg%d" % s)
                if sample:
                    DMA("sp", swa_v_s[bl, 0:64, :], cache_v[bl, 64:128, :], (), [], "d2d")
            units = []
            for sg in segs:
                for nq in range(sg["n"] // 64):
                    for kv in range(4):
                        units.append((sg, nq, kv))

            def unit_info(ui):
                sg, nq, kv = units[ui]
                ext = [e_ for e_ in (nq, nq + 1, nq + 2) if (e_ >= 2 or sg["hist_valid"])]
                return sg, nq, kv, ext, sg["c0"] + nq * 64, pTs[ui % 2], ("pTs", ui % 2), dns[ui % 2], ("dns", ui % 2)

            def swa_scores(ui):
                sg, nq, kv, ext, qc0, pt, ptk, dn, dnk = unit_info(ui)
                ne = len(ext)
                for par in range(2):
                    b = nb()
                    for ji, e_ in enumerate(ext):
                        kcol = sg["kb0"] + e_ * 64
                        MM(ps[b][0:64, ji * 128:(ji + 1) * 128].rearrange("p (a l) -> p a l", a=2),
                           kbuf[par * 64:(par + 1) * 64, kv, kcol:kcol + 64],
                           q[par * 64:(par + 1) * 64, 2 * kv:2 * kv + 2, qc0:qc0 + 64], True, True,
                           [("kbuf", kv), ("kbufh", kv), ("q", 2 * kv), ("q", 2 * kv + 1)], [("ps", b)])
                    ACT(pt[0:64, 0:ne, par * 128:(par + 1) * 128],
                        ps[b][0:64, 0:ne * 128].rearrange("p (j c) -> p j c", j=ne), AF.Exp, [("ps", b)], [ptk], scale=0.125)

            def swa_pv(ui):
                sg, nq, kv, ext, qc0, pt, ptk, dn, dnk = unit_info(ui)
                bd = nb()
                MM(ps[bd][:, 0:256], ones_bf[0:64, :], esx[0:64, kv, :], True, False, ["ones", "esx"], [("ps", bd)])
                for ji in range(len(ext)):
                    MM(ps[bd][:, 0:256], ones_bf[0:64, :], pt[0:64, ji, :], False, ji == len(ext) - 1,
                       ["ones", ptk], [("ps", bd)])
                ACT(dn[:, :], ps[bd][:, 0:256], AF.Ln, [("ps", bd)], [dnk])
                ACT(dn[:, :], dn[:, :], AF.Exp, [dnk], [dnk], scale=-1.0)
                bo = nb()
                for ji, e_ in enumerate(ext):
                    vs = sg["vb0"] + e_
                    MM(ps[bo][:, 0:256], vtok[0:64, vs, kv * 128:(kv + 1) * 128], pt[0:64, ji, :], ji == 0,
                       ji == len(ext) - 1, [("vtok", vs), ptk], [("ps", bo)])
                for par in range(2):
                    sl = slice(par * 64, (par + 1) * 64)
                    TT("dve", ycat[sl, 2 * kv:2 * kv + 2, qc0:qc0 + 64],
                       ps[bo][sl, par * 128:(par + 1) * 128].rearrange("p (a l) -> p a l", a=2),
                       dn[sl, par * 128:(par + 1) * 128].rearrange("p (a l) -> p a l", a=2), ALU.mult,
                       [("ps", bo), dnk], [("ycat", 2 * kv), ("ycat", 2 * kv + 1)])

            swa_scores(0)
            for ui in range(len(units)):
                if ui + 1 < len(units):
                    swa_scores(ui + 1)
                swa_pv(ui)
            for sg in segs:
                if sample or sg["last"]:
                    continue
                kb0, n, vb0 = sg["kb0"], sg["n"], sg["vb0"]
                for kv in range(4):
                    CP("pool", kbuf[:, kv, kb0:kb0 + 128], kbuf[:, kv, kb0 + n:kb0 + n + 128], [("kbuf", kv)], [("kbufh", kv)])
                for j in range(2):
                    CP("pool", vtok[0:64, vb0 + j, :], vtok[0:64, vb0 + n // 64 + j, :], [("vtok", vb0 + n // 64 + j)],
                       [("vtok", vb0 + j)])
            proj_fm("wo", 2, lambda k: ycat[:, k, 0:T], lambda k: ("ycat", k), T, resid_evac(T))

        pending = []

        def flush_pending():
            while pending:
                pending.pop(0)()

        def final_norm_store(T, dst2d):
            yb = lambda c: hid[:, 2 * c:2 * c + 2, :].bitcast(F32).rearrange("p a b -> p (a b)")
            ybk = lambda c: [("hid", 2 * c), ("hid", 2 * c + 1)]
            rmsnorm(lambda c: x[:, c, 0:T], lambda c: ("x", c), lambda c: h[:, c, 0:T], lambda c: ("h", c),
                    "norm_final", T, inplace_out=lambda c: yb(c)[:, 0:T], out_keys=ybk)

            def do_store():
                for t0 in range(0, T, 128):
                    for c0 in range(0, D, 512):
                        b = nb()
                        for cc in range(4):
                            c = c0 // 128 + cc
                            TR(ps[b][:, cc * 128:(cc + 1) * 128], yb(c)[:, t0:t0 + 128], ybk(c), [("ps", b)])
                        s_ = nstg()
                        CP(evac_eng(), stg[s_][:, :], ps[b][:, :], [("ps", b)], [("stg", s_)])
                        DMA("act", dst2d[t0:t0 + 128, c0:c0 + 512], stg[s_][:, :], [("stg", s_)], [], "stg%d" % s_)
            pending.append(do_store)

        def run_tile(segs, T, sample, src2d, dst2d, kvslot0, kvslot1, next_mem=None):
            load_T(lambda c0, nch, t0, n: x[:, c0:c0 + nch, t0:t0 + n], lambda c: ("x", c), src2d, T, D)
            even_layer(segs, T, sample)
            cross_attn(0, segs, T, kvslot0)
            mlp(0, T)
            odd_layer(segs, T, sample)
            cross_attn(1, segs, T, kvslot1)
            if next_mem is not None and STAGE >= 99:
                mem_front(next_mem)
            mlp(1, T)
            final_norm_store(T, dst2d)

        tb_ids = [bid[b] for b in tile_blocks]
        mb_ids = [bid[b] for b in mem_blocks]
        if do_sample and not DEBUG_ONDEMAND:
            wseq.extend(tb_ids)
        for s_ in range(nseq if not DEBUG_ONDEMAND else 0):
            wseq.extend(mb_ids)
            for _ in range(SEQ // TP):
                wseq.extend(tb_ids)

        STAGE = int(os.environ.get("KDEBUG_STAGE", "99"))
        NTI = int(os.environ.get("KDEBUG_NTILE", str(SEQ // TP)))
        if not DEBUG_ONDEMAND:
            w_issue_upto(NW)
        setup_consts()
        if DEBUG_ONDEMAND and STAGE >= 1:
            prepass()
        if not DEBUG_ONDEMAND:
            assert sorted(wseq[:CONV_N]) == list(range(CONV_N))
        if do_sample and STAGE >= 4:
            T = NB * DEC
            segs = []
            for bl in range(NB):
                segs.append(dict(bl=bl, c0=bl * DEC, n=DEC, ucol=bl * 80 + 16, xcol=bl * 68 + 4, kb0=bl * 192, vb0=bl * 3,
                                 first=True, last=True, hist_valid=True))
            for sg in segs:
                bl = sg["bl"]
                uc, xo = sg["ucol"], sg["xcol"]
                load_T(lambda c0, nch, t0, n, uc=uc: up[:, c0:c0 + nch, uc - 15:uc], lambda c: ("uph", c), state_pool[bl], 15, 512)
                load_T(lambda c0, nch, t0, n, xo=xo: ux[:, c0:c0 + nch, xo - 3:xo], lambda c: ("uxh", c), state_conv[bl], 3, 512)
                s = nstg()
                DMA("sp", stg[s][:, 0:256], cache_k[bl], (), [("stg", s)], "stg%d" % s)
                s2 = nstg()
                for r_ in range(2):
                    CP("pool", stg[s2][:, :].rearrange("p (v r d) -> p v r d", v=4, r=2)[:, :, r_, :],
                       stg[s][:, 0:256].rearrange("p (v d) -> p v d", v=4), [("stg", s)], [("stg", s2)])
                b = nb()
                for kv in range(4):
                    TR(ps[b][:, kv * 128:(kv + 1) * 128], stg[s2][:, kv * 128:(kv + 1) * 128], [("stg", s2)], [("ps", b)])
                CP(evac_eng(), kbuf[:, :, sg["kb0"]:sg["kb0"] + 128], ps[b][:, :].rearrange("p (v t) -> p v t", v=4),
                   [("ps", b)], [("kbufh", kv_) for kv_ in range(4)])
                for j in range(2):
                    s = nstg()
                    DMA("sp", stg[s][0:64, 0:256], cache_v[bl, j * 64:(j + 1) * 64, :], (), [("stg", s)], "stg%d" % s)
                    for r_ in range(2):
                        CP("pool", vtok[0:64, sg["vb0"] + j, :].rearrange("p (v r d) -> p v r d", v=4, r=2)[:, :, r_, :],
                           stg[s][0:64, 0:256].rearrange("p (v d) -> p v d", v=4), [("stg", s)], [("vtok", sg["vb0"] + j)])
            load_T(lambda c0, nch, t0, n: hstate_s[:, c0:c0 + nch, 0:4], lambda c: "hstate_s", state_lru, 4, 512)

            kvctr = [0]

            def mk_kvslot(l):
                def f(sg):
                    slot = kvctr[0] % 2
                    kvctr[0] += 1
                    mem_load_sample(l, sg["bl"], slot)
                    return slot
                return f
            run_tile(segs, T, True, x_sample, y_sample, mk_kvslot(0), mk_kvslot(1), next_mem=0)
            store_T(lambda c, t0, n: hstate_s[:, c, 0:4], lambda c: "hstate_s", lru_s, 4, 512)

        for bl in range(nseq if STAGE >= 2 else 0):
            mem_phase(bl)
            for ti in range(NTI if STAGE >= 3 else 0):
                seg = dict(bl=bl, c0=0, n=TP, ucol=16, xcol=4, kb0=0, vb0=0, first=(ti == 0), last=(ti == SEQ // TP - 1),
                           hist_valid=(ti != 0))
                if ti == 0:
                    for g in range(4):
                        MSET("pool", up[:, g, 0:16], 0.0, [("uph", g)])
                        MSET("pool", ux[:, g, 0:4], 0.0, [("uxh", g)])
                run_tile([seg], TP, False, x_prompt[bl, ti * TP:(ti + 1) * TP, :], y_prompt[bl, ti * TP:(ti + 1) * TP, :],
                         lambda sg: 0, lambda sg: 1, next_mem=(bl + 1 if ti == SEQ // TP - 1 else None))
            if bl == nseq - 1:
                store_T(lambda c, t0, n: hstate[:, c, 0:4], lambda c: "hstate", lru_p, 4, 512)
        flush_pending()
        assert STAGE < 99 or wpos[0] == len(wseq), (wpos, len(wseq))

        chans = sorted(P.chan_n.keys())
        sems = {e_: es.enter_context(nc.semaphore("s_" + e_)) for e_ in COMPUTE}
        chan_sems = {c_: es.enter_context(nc.semaphore("d_" + c_)) for c_ in chans}
        block = es.enter_context(nc.Block())
        P.emit(nc, block, sems, chan_sems)
    return nc, len(P.ops)


_CACHE = {}


def _stack_vecs(inp):
    rows = []
    for nm in ["norm_mix", "norm_cross", "norm_mem", "norm_mlp"]:
        for l in range(2):
            rows.append(np.asarray(inp[nm][l]).reshape(8, 128))
    rows.append(np.asarray(inp["norm_final"]).reshape(8, 128))
    cw = np.asarray(inp["conv_w"])[0]
    for k in range(4):
        rows.append(cw[k].reshape(4, 128))
    for nm in ["conv_b", "b_rg_a", "b_rg_x", "rg_lambda", "pool_scale"]:
        rows.append(np.asarray(inp[nm])[0].reshape(4, 128))
    v = np.concatenate(rows, axis=0).astype(np.float32)
    out = np.zeros((128, 128), np.float32)
    out[:v.shape[0]] = v
    return out


def kernel(**inp):
    nseq = int(os.environ.get("KDEBUG_NSEQ", NB))
    do_sample = os.environ.get("KDEBUG_NOSAMPLE", "0") != "1"
    key = (nseq, do_sample)
    if key not in _CACHE:
        _CACHE[key] = build_program(nseq, do_sample)[0]
    nc = _CACHE[key]
    f = lambda a: np.ascontiguousarray(np.asarray(a, dtype=np.float32))
    shared = dict(
        vecs=_stack_vecs(inp), attn_sinks=f(inp["attn_sinks"]).reshape(1, 16), ident=np.eye(128, dtype=np.float32),
        w_in=f(inp["w_in_even"][0]), w_out=f(inp["w_out_even"][0]), w_qkv=f(inp["w_qkv_odd"][0]), w_o=f(inp["w_o_odd"][0]),
        w_mq=f(inp["w_mq"]), w_mk=f(inp["w_mk"]), w_mv=f(inp["w_mv"]), w_mo=f(inp["w_mo"]), w_up=f(inp["w_up"]),
        w_down=f(inp["w_down"]), pool_w=f(inp["pool_w"][0]), w_rg_a=f(inp["w_rg_a"][0]), w_rg_x=f(inp["w_rg_x"][0]),
    )
    in_maps = []
    for i in range(NCORE):
        sl = slice(i * NB, (i + 1) * NB)
        m = dict(shared)
        m.update(
            x_prompt=f(inp["x_prompt"][sl]), x_sample=f(inp["x_sample"][sl]).reshape(NB * DEC, D),
            state_pool=f(inp["state_pool"][0, sl]), state_conv=f(inp["state_conv"][0, sl]), state_lru=f(inp["state_lru"][0, sl]),
            cache_swa_k=f(inp["cache_swa_k"][0, sl]).reshape(NB, 128, 256), cache_swa_v=f(inp["cache_swa_v"][0, sl]).reshape(NB, 128, 256),
            cache_mem_k=f(inp["cache_mem_k"][:, sl]).reshape(2, NB, NMEM, D), cache_mem_v=f(inp["cache_mem_v"][:, sl]).reshape(2, NB, NMEM, D),
            mem_prompt=f(inp["mem_prompt"][sl]),
        )
        in_maps.append(m)
    res = run_bass_kernel_spmd(nc, in_maps, core_ids=list(range(NCORE)))
    R = res.results
    cat = lambda k, ax=0: np.concatenate([np.asarray(r[k]) for r in R], axis=ax)
    B = NCORE * NB
    y_prompt = cat("y_prompt")
    y_sample = cat("y_sample").reshape(B, DEC, D)
    pool_p = cat("pool_p")[None]
    conv_p = cat("conv_p")[None]
    lru_p = cat("lru_p")[None]
    swa_k_p = cat("swa_k_p").reshape(1, B, 128, 4, 64)
    swa_v_p = cat("swa_v_p").reshape(1, B, 128, 4, 64)
    mem_k_p = cat("mem_k_p", 1).reshape(2, B, NMEM, 4, 256)
    mem_v_p = cat("mem_v_p", 1).reshape(2, B, NMEM, 4, 256)
    pool_s = cat("pool_s")[None]
    conv_s = cat("conv_s")[None]
    lru_s = cat("lru_s")[None]
    swa_k_s = cat("swa_k_s").reshape(1, B, 128, 4, 64)
    swa_v_s = cat("swa_v_s").reshape(1, B, 128, 4, 64)
    return (y_prompt, y_sample, pool_p, conv_p, lru_p, swa_k_p, swa_v_p, mem_k_p, mem_v_p,
            pool_s, conv_s, lru_s, swa_k_s, swa_v_s)
```

```python
import os
import numpy as np
import concourse.bass as bass
import concourse.mybir as mybir
from concourse.bass_utils import run_bass_kernel_spmd

F32 = mybir.dt.float32
BF16 = mybir.dt.bfloat16
AF = mybir.ActivationFunctionType
ALU = mybir.AluOpType

NCORE = 8
D = 1024
KC = 8
TP = 512
SEQ = 2048
NB = 4
DEC = 64
NMEM = 256
EPS = 1e-6
NW = 4
GELU_K = 0.7978845608028654

COMPUTE = ("pe", "act", "dve", "pool")


class Prog:
    def __init__(self):
        self.ops = []
        self.lastw = {}
        self.readers = {}
        self.chan_n = {}

    def add(self, eng, fn, r=(), w=(), chan=None):
        i = len(self.ops)
        psr = [k for k in r if isinstance(k, tuple) and k[0] == "ps"]
        if psr:
            r = [k for k in r if not (isinstance(k, tuple) and k[0] == "ps")]
            w = list(w) + psr
        deps = set()
        raw = set()
        for k in r:
            j = self.lastw.get(k)
            if j is not None:
                deps.add(j)
                raw.add(j)
        for k in w:
            j = self.lastw.get(k)
            if j is not None:
                deps.add(j)
            for j in self.readers.get(k, ()):
                deps.add(j)
        deps.discard(i)
        cnt = None
        if chan is not None:
            cnt = self.chan_n.get(chan, 0) + 1
            self.chan_n[chan] = cnt
        self.ops.append(dict(eng=eng, fn=fn, deps=deps, raw=raw, chan=chan, cnt=cnt))
        for k in w:
            self.lastw[k] = i
            self.readers[k] = []
        for k in r:
            lst = self.readers.setdefault(k, [])
            if chan is None:
                lst[:] = [j for j in lst if not (self.ops[j]["chan"] is None and self.ops[j]["eng"] == eng)]
            lst.append(i)
        return i

    def emit(self, nc, block, sems, chan_sems):
        ops = self.ops
        for i, op in enumerate(ops):
            best = {}
            dmas = []
            for j in op["deps"]:
                d = ops[j]
                if d["chan"] is not None:
                    dmas.append(j)
                    continue
                if d["eng"] == op["eng"] and op["chan"] is None and op["eng"] == "pe":
                    continue
                if d["eng"] == op["eng"] and op["chan"] is not None:
                    pass
                if j > best.get(d["eng"], -1):
                    best[d["eng"]] = j
            op["wait_c"] = best
            op["wait_d"] = dmas
        sig = [False] * len(ops)
        for op in ops:
            for j in op["wait_c"].values():
                sig[j] = True
        counts = {e: 0 for e in COMPUTE}
        for i, op in enumerate(ops):
            if op["chan"] is None and sig[i]:
                counts[op["eng"]] += 1
                op["sigval"] = counts[op["eng"]]
        streams = {}
        for i, op in enumerate(ops):
            streams.setdefault(op["eng"], []).append(i)

        def run_stream(name, e):
            waited = {}
            for i in streams.get(name, []):
                op = ops[i]
                for eng2, j in op["wait_c"].items():
                    v = ops[j]["sigval"]
                    key = ("c", eng2)
                    if waited.get(key, 0) < v:
                        e.wait_ge(sems[eng2], v)
                        waited[key] = v
                for j in op["wait_d"]:
                    d = ops[j]
                    v = 16 * d["cnt"]
                    key = ("d", d["chan"])
                    if waited.get(key, 0) < v:
                        e.wait_ge(chan_sems[d["chan"]], v)
                        waited[key] = v
                if op["chan"] is not None:
                    v = 16 * (op["cnt"] - 1)
                    key = ("d", op["chan"])
                    if v > 0 and waited.get(key, 0) < v:
                        e.wait_ge(chan_sems[op["chan"]], v)
                        waited[key] = v
                ins = op["fn"](e)
                if op["chan"] is not None:
                    ins.then_inc(chan_sems[op["chan"]], 16)
                elif sig[i]:
                    ins.then_inc(sems[op["eng"]], 1)
            if name in ("sp", "act", "pool"):
                final = {}
                for i in streams.get(name, []):
                    op = ops[i]
                    if op["chan"] is not None:
                        final[op["chan"]] = max(final.get(op["chan"], 0), 16 * op["cnt"])
                for ch, v in final.items():
                    if waited.get(("d", ch), 0) < v:
                        e.wait_ge(chan_sems[ch], v)

        @block.sync
        def _(e):
            run_stream("sp", e)

        @block.tensor
        def _(e):
            run_stream("pe", e)

        @block.scalar
        def _(e):
            run_stream("act", e)

        @block.vector
        def _(e):
            run_stream("dve", e)

        @block.gpsimd
        def _(e):
            run_stream("pool", e)


def block_catalogue():
    blocks = []
    for l in range(2):
        if l == 0:
            blocks += [("in", 1), ("in", 0), ("in", 2)] + [("out", j) for j in range(2)]
        else:
            blocks += [("qkv", j) for j in range(4)] + [("wo", j) for j in range(2)]
        blocks += [("mq%d" % l, j) for j in range(2)] + [("mo%d" % l, j) for j in range(2)]
        for half in range(2):
            blocks += [("up%d" % l, half * 4 + j) for j in range(4)]
            blocks += [("down%d" % l, half * 4 + j) for j in range(4)]
    memb = []
    for l in range(2):
        memb += [("mk%d" % l, j) for j in range(2)] + [("mv%d" % l, j) for j in range(2)]
    return blocks, memb


def build_program(nseq=NB, do_sample=True):
    nc = bass.Bass("TRN2", target_bir_lowering=False)
    P = Prog()

    def din(name, shape):
        return nc.dram_tensor(name, shape, F32, kind="ExternalInput").ap()

    def dout(name, shape):
        return nc.dram_tensor(name, shape, F32, kind="ExternalOutput").ap()

    x_prompt = din("x_prompt", [NB, SEQ, D])
    x_sample = din("x_sample", [NB * DEC, D])
    state_pool = din("state_pool", [NB, 15, 512])
    state_conv = din("state_conv", [NB, 3, 512])
    state_lru = din("state_lru", [NB, 512])
    cache_k = din("cache_swa_k", [NB, 128, 256])
    cache_v = din("cache_swa_v", [NB, 128, 256])
    cmem_k = din("cache_mem_k", [2, NB, NMEM, D])
    cmem_v = din("cache_mem_v", [2, NB, NMEM, D])
    mem_prompt = din("mem_prompt", [NB, NMEM, D])
    vecs = din("vecs", [128, 128])
    sinks = din("attn_sinks", [1, 16])
    ident_d = din("ident", [128, 128])
    W = dict(
        w_in=din("w_in", [D, 1536]), w_out=din("w_out", [D, D]), w_qkv=din("w_qkv", [D, 1536]),
        w_o=din("w_o", [D, D]), w_mq=din("w_mq", [2, D, D]), w_mk=din("w_mk", [2, D, D]),
        w_mv=din("w_mv", [2, D, D]), w_mo=din("w_mo", [2, D, D]), w_up=din("w_up", [2, D, 4 * D]),
        w_down=din("w_down", [2, 4 * D, D]),
    )
    pool_w_d = din("pool_w", [4, 128, 128])
    w_rg_a_d = din("w_rg_a", [8, 64, 64])
    w_rg_x_d = din("w_rg_x", [8, 64, 64])

    y_prompt = dout("y_prompt", [NB, SEQ, D])
    y_sample = dout("y_sample", [NB * DEC, D])
    pool_p = dout("pool_p", [NB, 15, 512])
    conv_p = dout("conv_p", [NB, 3, 512])
    lru_p = dout("lru_p", [NB, 512])
    swa_k_p = dout("swa_k_p", [NB, 128, 256])
    swa_v_p = dout("swa_v_p", [NB, 128, 256])
    mem_k_p = dout("mem_k_p", [2, NB, NMEM, D])
    mem_v_p = dout("mem_v_p", [2, NB, NMEM, D])
    pool_s = dout("pool_s", [NB, 15, 512])
    conv_s = dout("conv_s", [NB, 3, 512])
    lru_s = dout("lru_s", [NB, 512])
    swa_k_s = dout("swa_k_s", [NB, 128, 256])
    swa_v_s = dout("swa_v_s", [NB, 128, 256])

    tile_blocks, mem_blocks = block_catalogue()
    all_blocks = tile_blocks + mem_blocks
    bid = {b: i for i, b in enumerate(all_blocks)}
    wscr = nc.dram_tensor("wscr", [len(all_blocks), 128, 4096], BF16, kind="Internal").ap()

    import contextlib
    es = contextlib.ExitStack()
    with es:
        def sb(name, shape, dt=F32):
            return es.enter_context(nc.sbuf_tensor(name, shape, dt))

        x = sb("x", [128, 8, TP])
        h = sb("h", [128, 8, TP], BF16)
        up = sb("up", [128, 4, 16 + TP])
        ux = sb("ux", [128, 4, 4 + TP])
        ug = sb("ug", [128, 4, TP])
        ycat = sb("ycat", [128, 8, TP], BF16)
        NTMP = 7
        tmp = [sb("tmp%d" % i, [128, 16 + TP]) for i in range(NTMP)]
        xc = sb("xc", [128, 4, TP])
        xcb = sb("xcb", [128, 4, TP], BF16)
        dbf = sb("dbf", [128, 4, TP], BF16)
        q = sb("q", [128, 8, TP], BF16)
        kbuf = sb("kbuf", [128, 4, 768], BF16)
        vtok = sb("vtok", [64, 12, 512], BF16)
        hid = sb("hid", [128, 16, TP], BF16)
        wring = [sb("wr%d" % i, [128, 4096], BF16) for i in range(NW)]
        memk = [sb("memk%d" % i, [128, 8, NMEM], BF16) for i in range(2)]
        memv = [sb("memv%d" % i, [128, 2, D], BF16) for i in range(2)]
        NSTG = 4
        stg = [sb("stg%d" % i, [128, 512]) for i in range(NSTG)]
        pT2 = [sb("pT2_%d" % i, [128, 2, TP], BF16) for i in range(2)]
        rden = [sb("rden%d" % i, [128, TP]) for i in range(2)]
        pTs = [sb("pTs%d" % i, [64, 3, 256], BF16) for i in range(2)]
        dns = [sb("dns%d" % i, [128, 256]) for i in range(2)]
        ident = sb("ident_sb", [128, 128])
        ones_bf = sb("ones_bf", [128, 128], BF16)
        ones_f = sb("ones_f", [128, 64])
        cvec = sb("cvec", [128, 128])
        negb = sb("negb", [128, 8])
        c8 = sb("c8", [128, 8])
        epsc = sb("epsc", [128, 1])
        onec = sb("onec", [128, 1])
        esink = sb("esink", [128, 16])
        esx = sb("esx", [64, 4, 256], BF16)
        poolw = sb("poolw", [128, 4, 128], BF16)
        wbd = sb("wbd", [128, 8, 128], BF16)
        invc = sb("invc", [128, 4, 16])
        hstate = sb("hstate", [128, 4, 4])
        hstate_s = sb("hstate_s", [128, 4, 4])
        tpad = sb("tpad", [128, 128])
        ps = [es.enter_context(nc.psum_tensor("ps%d" % i, [128, 512], F32)) for i in range(8)]

        COL = {}
        col = 0
        for nm, n in [("norm_mix0", 8), ("norm_mix1", 8), ("norm_cross0", 8), ("norm_cross1", 8),
                      ("norm_mem0", 8), ("norm_mem1", 8), ("norm_mlp0", 8), ("norm_mlp1", 8),
                      ("norm_final", 8), ("conv_w0", 4), ("conv_w1", 4), ("conv_w2", 4), ("conv_w3", 4),
                      ("conv_b", 4), ("b_rg_a", 4), ("b_rg_x", 4), ("rg_lambda", 4), ("pool_scale", 4)]:
            COL[nm] = col
            col += n
        assert col <= 128

        bank_ctr = [0]

        def nb():
            b = bank_ctr[0] % 8
            bank_ctr[0] += 1
            return b

        stg_ctr = [0]

        def nstg():
            s = stg_ctr[0] % NSTG
            stg_ctr[0] += 1
            return s

        def MM(out, lhsT, rhs, start, stop, r, w):
            P.add("pe", lambda e: e.matmul(out, lhsT=lhsT, rhs=rhs, start=start, stop=stop), r, w)

        def TR(out, in_, r, w):
            P.add("pe", lambda e: e.transpose(out, in_, ident[:, :]), list(r) + ["ident"], w)

        def ACT(out, in_, func, r, w, bias=None, scale=1.0):
            if bias is None:
                P.add("act", lambda e: e.activation(out=out, in_=in_, func=func, scale=scale), r, w)
            else:
                P.add("act", lambda e: e.activation(out=out, in_=in_, func=func, bias=bias, scale=scale), r, w)

        def TT(eng, out, in0, in1, op, r, w):
            P.add(eng, lambda e: e.tensor_tensor(out=out, in0=in0, in1=in1, op=op), r, w)

        def TS(eng, out, in0, s1, s2, op0, op1, r, w):
            if s2 is None:
                P.add(eng, lambda e: e.tensor_scalar(out=out, in0=in0, scalar1=s1, scalar2=None, op0=op0), r, w)
            else:
                P.add(eng, lambda e: e.tensor_scalar(out=out, in0=in0, scalar1=s1, scalar2=s2, op0=op0, op1=op1), r, w)

        def STT(out, in0, scalar, in1, op0, op1, r, w):
            P.add("dve", lambda e: e.scalar_tensor_tensor(out=out, in0=in0, scalar=scalar, in1=in1, op0=op0, op1=op1), r, w)

        def CP(eng, out, in_, r, w):
            if eng == "act":
                P.add("act", lambda e: e.copy(out=out, in_=in_), r, w)
            else:
                P.add(eng, lambda e: e.tensor_copy(out=out, in_=in_), r, w)

        def MSET(eng, ap, val, w):
            P.add(eng, lambda e: e.memset(ap, val), (), w)

        def DMA(queue, out, in_, r, w, chan):
            P.add(queue, lambda e: e.dma_start(out=out, in_=in_), r, w, chan=chan)

        evac_ctr = [0]

        def evac_eng():
            evac_ctr[0] += 1
            return "act" if evac_ctr[0] % 2 else "dve"

        def setup_consts():
            DMA("sp", ident[:, :], ident_d, (), ["ident"], "c_ident")
            DMA("sp", stg[0][:, 0:128], vecs, (), [("stg", 0)], "c_vecs")
            for i in range(1, NSTG):
                MSET("pool", stg[i][:, :], 0.0, [("stg", i)])
            MSET("pool", tpad[:, :], 0.0, ["tpad"])
            MSET("dve", ones_bf[:, :], 1.0, ["ones"])
            MSET("dve", ones_f[:, :], 1.0, ["ones_f"])
            MSET("dve", epsc[:, :], EPS, ["epsc"])
            MSET("dve", onec[:, :], 1.0, ["onec"])
            MSET("dve", hstate[:, :, :], 0.0, ["hstate"])
            b = nb()
            TR(ps[b][:, 0:128], stg[0][:, 0:128], [("stg", 0)], [("ps", b)])
            CP("dve", cvec[:, :], ps[b][:, 0:128], [("ps", b)], ["cvec"])
            ca = COL["b_rg_a"]
            TS("dve", negb[:, :], cvec[:, ca:ca + 8], -1.0, None, ALU.mult, None, ["cvec"], ["negb"])
            cl = COL["rg_lambda"]
            ACT(c8[:, 0:4], cvec[:, cl:cl + 4], AF.Exp, ["cvec"], ["c8"], scale=-1.0)
            ACT(c8[:, 0:4], c8[:, 0:4], AF.Ln, ["c8", "onec"], ["c8"], bias=onec[:, 0:1])
            TS("dve", c8[:, 4:8], c8[:, 0:4], -16.0, None, ALU.mult, None, ["c8"], ["c8b"])
            TS("dve", c8[:, 0:4], c8[:, 0:4], -8.0, None, ALU.mult, None, ["c8", "c8b"], ["c8"])
            DMA("sp", esink[:, :], sinks.partition_broadcast(128), (), ["esink"], "c_sink")
            ACT(esink[:, :], esink[:, :], AF.Exp, ["esink"], ["esink"])
            MSET("pool", esx[:, :, :], 0.0, ["esx"])
            for p0 in (0, 32):
                pr = slice(p0, p0 + 1)
                for kv in range(4):
                    for par in range(2):
                        for j2 in range(2):
                            g = kv * 4 + 2 * j2 + par
                            o = (par * 2 + j2) * 64
                            TS("dve", ug[pr, kv, o:o + 64], ones_f[pr, :], esink[pr, g:g + 1], None, ALU.mult, None,
                               ["esink", "ones_f"], [("ug", kv)])
                ugk = [("ug", kv) for kv in range(4)]
                if p0 == 0:
                    CP("dve", esx[pr, :, :], ug[pr, :, 0:256], ugk, ["esx"])
                else:
                    CP("dve", dbf[pr, :, 0:256], ug[pr, :, 0:256], ugk, [("dbf", 0)])
                    CP("dve", ug[pr, :, 256:512], dbf[pr, :, 0:256], [("dbf", 0)], ugk)
                    TT("dve", ug[pr, :, 256:512], ug[pr, :, 0:256], ug[pr, :, 256:512], ALU.subtract, ugk, ugk)
                    CP("dve", esx[pr, :, :], ug[pr, :, 256:512], ugk, ["esx"])
            DMA("sp", tmp[0][:, 0:512].rearrange("p (g e) -> p g e", g=4), pool_w_d.rearrange("g c e -> c g e"),
                (), [("tmp", 0)], "c_pw")
            CP("dve", poolw[:, :, :], tmp[0][:, 0:512].rearrange("p (g e) -> p g e", g=4), [("tmp", 0)], ["poolw"])
            for wi, wd in enumerate((w_rg_a_d, w_rg_x_d)):
                t = tmp[1 + wi]
                MSET("pool", t[:, 0:512], 0.0, [("tmp", 1 + wi)])
                tv = t[:, 0:512].rearrange("p (c j) -> p c j", c=4)
                src = wd.rearrange("(c r) i j -> r i c j", r=2)
                for r_ in range(2):
                    DMA("sp", tv[r_ * 64:(r_ + 1) * 64, :, r_ * 64:(r_ + 1) * 64], src[r_], (), [("tmp", 1 + wi)],
                        "c_bd%d%d" % (wi, r_))
                CP("dve", wbd[:, wi * 4:(wi + 1) * 4, :], tv, [("tmp", 1 + wi)], ["wbd"])
            for g in range(4):
                win = 2 << g
                for t_ in range(15):
                    MSET("pool", invc[:, g, t_:t_ + 1], 1.0 / min(t_ + 1, win), ["invc"])

        def wsrc(name, j, half):
            def std(Wm, j):
                src = Wm.rearrange("(k p) n -> p k n", p=128)[:, half * 4:(half + 1) * 4, j * 512:(j + 1) * 512]
                return [(lambda s: s.rearrange("p (k n) -> p k n", k=4), src)]
            if name == "in":
                return std(W["w_in"], j)
            if name == "out":
                return std(W["w_out"], j)
            if name == "wo":
                return std(W["w_o"], j)
            if name[:2] in ("mq", "mo", "mk", "mv", "up"):
                l = int(name[-1])
                return std(W["w_" + name[:-1]][l], j)
            if name.startswith("down"):
                l = int(name[-1])
                hh, cb = j // 4, j % 4
                src = W["w_down"][l].rearrange("(k p) n -> p k n", p=128)[
                    :, hh * 16 + half * 8: hh * 16 + half * 8 + 8, cb * 256:(cb + 1) * 256]
                return [(lambda s: s.rearrange("p (k n) -> p k n", k=8), src)]
            if name == "qkv":
                Wr = W["w_qkv"].rearrange("(k p) n -> p k n", p=128)
                if j < 2:
                    return std(W["w_qkv"], j)
                c0 = 1024 if j == 2 else 1280
                res = []
                for kk in range(4):
                    src = Wr[:, half * 4 + kk, c0:c0 + 256].rearrange("p (v d) -> p v d", v=4)
                    for r_ in range(2):
                        res.append((lambda s, r_=r_, kk=kk: s.rearrange("p (k v r d) -> p k v r d", k=4, v=4, r=2)[:, kk, :, r_, :], src))
                return res
            raise KeyError(name)

        def prepass():
            stage = [(x[:, 0:4, :].rearrange("p a b -> p (a b)"), [("x", c) for c in range(4)]),
                     (x[:, 4:8, :].rearrange("p a b -> p (a b)"), [("x", c) for c in range(4, 8)]),
                     (xc[:, :, :].rearrange("p a b -> p (a b)"), [("xc", c) for c in range(4)])]
            n = 0
            for bi, (name, j) in enumerate(all_blocks):
                for half in range(2):
                    sap, skeys = stage[n % 3]
                    for k_, (vf, src) in enumerate(wsrc(name, j, half)):
                        P.add("sp", (lambda e, o=vf(sap), s=src: e.dma_start(out=o, in_=s)), (), skeys,
                              chan="pp_ld%d_%d" % (n % 3, k_))
                    slot = (n // 2) % NW
                    dstv = wring[slot][:, half * 2048:(half + 1) * 2048]
                    eng = "dve" if n % 2 == 0 else "pool"
                    CP(eng, dstv, sap, skeys, [("w", slot, half)])
                    if half == 1:
                        P.add("act", (lambda e, o=wscr[bi], s=wring[slot][:, :]: e.dma_start(out=o, in_=s)),
                              [("w", slot, 0), ("w", slot, 1)], [("wscr", bi)], chan="pp_st%d" % slot)
                    n += 1

        DEBUG_ONDEMAND = int(os.environ.get("KDEBUG_STAGE", "99")) < 99
        MEMMASK = int(os.environ.get("KDEBUG_MEM", "255"))
        wseq = []
        wpos = [0, 0]

        def w_issue_upto(k):
            while wpos[1] < min(k, len(wseq)):
                i = wpos[1]
                b = wseq[i]
                if b is None:
                    break
                slot = i % NW
                if not DEBUG_ONDEMAND and i < CONV_N:
                    name_, j_ = all_blocks[b]
                    for half in range(2):
                        sap = wring[slot][:, half * 2048:(half + 1) * 2048]
                        for k_, (vf, src) in enumerate(wsrc(name_, j_, half)):
                            P.add("pool", (lambda e, o=vf(sap), s_=src: e.dma_start(out=o, in_=s_)), (),
                                  [("w", slot, half)], chan="cw%d_%d_%d" % (slot, half, k_))
                else:
                    DMA("sp", wring[slot][:, :], wscr[b], [("wscr", b)], [("w", slot, 0), ("w", slot, 1)], "w%d" % slot)
                wpos[1] += 1

        CONV_N = len(all_blocks)

        def wnext(name, j):
            i = wpos[0]
            if not DEBUG_ONDEMAND and 0 < i <= CONV_N:
                pslot = (i - 1) % NW
                pb = wseq[i - 1]
                P.add("sp", (lambda e, o=wscr[pb], s_=wring[pslot][:, :]: e.dma_start(out=o, in_=s_)),
                      [("w", pslot, 0), ("w", pslot, 1)], [("wscr", pb)], chan="cst%d" % pslot)
            if DEBUG_ONDEMAND:
                while len(wseq) <= i:
                    wseq.append(None)
                wseq[i] = bid[(name, j)]
                w_issue_upto(i + 1)
            assert wseq[i] == bid[(name, j)], (i, name, j, all_blocks[wseq[i]])
            w_issue_upto(i + NW)
            wpos[0] += 1
            slot = i % NW
            return wring[slot], [("w", slot, 0), ("w", slot, 1)]

        def rmsnorm(src, skey, dst, dkey, gname, T, inplace_out=None, out_keys=None, second=None):
            for c in range(8):
                if c % 2 == 0:
                    ACT(dst(c), src(c), AF.Square, [skey(c)], [dkey(c)])
                else:
                    TT("dve", dst(c), src(c), src(c), ALU.mult, [skey(c)], [dkey(c)])
            b = nb()
            for c in range(8):
                MM(ps[b][:, 0:T], ones_bf[:, :], dst(c), c == 0, c == 7, [dkey(c), "ones"], [("ps", b)])
            t = tmp[6]
            ACT(t[:, 0:T], ps[b][:, 0:T], AF.Ln, [("ps", b), "epsc"], [("tmp", 6)], bias=epsc[:, 0:1], scale=1.0 / D)
            ACT(t[:, 0:T], t[:, 0:T], AF.Exp, [("tmp", 6)], [("tmp", 6)], scale=-0.5)
            g0 = COL[gname]
            for c in range(8):
                if inplace_out is None:
                    STT(dst(c), src(c), cvec[:, g0 + c:g0 + c + 1], t[:, 0:T], ALU.mult, ALU.mult,
                        [skey(c), ("tmp", 6), "cvec"], [dkey(c)])
                else:
                    STT(inplace_out(c), src(c), cvec[:, g0 + c:g0 + c + 1], t[:, 0:T], ALU.mult, ALU.mult,
                        [skey(c), ("tmp", 6), "cvec", dkey(c)], out_keys(c))
            if second is not None:
                dst2, dkey2, gname2 = second
                g2 = COL[gname2]
                for c in range(8):
                    STT(dst2(c), src(c), cvec[:, g2 + c:g2 + c + 1], t[:, 0:T], ALU.mult, ALU.mult,
                        [skey(c), ("tmp", 6), "cvec"], [dkey2(c)])

        def proj_fm(name, nblk, src, skey, T, evac, kc=8, cols=512, kouter=False):
            ncol_chunks = cols // 128
            for j in range(nblk):
                wt, wk = wnext(name, j)
                wv = wt[:, :].rearrange("p (k n) -> p k n", k=kc)
                if j == 0 and kouter:
                    banks = [nb() for _ in range(ncol_chunks)]
                    for k in range(kc):
                        for oc in range(ncol_chunks):
                            MM(ps[banks[oc]][:, 0:T], wv[:, k, oc * 128:(oc + 1) * 128], src(k), k == 0, k == kc - 1,
                               wk + [skey(k)], [("ps", banks[oc])])
                    for oc in range(ncol_chunks):
                        evac(j * ncol_chunks + oc, banks[oc])
                    continue
                for oc in range(ncol_chunks):
                    b = nb()
                    for k in range(kc):
                        MM(ps[b][:, 0:T], wv[:, k, oc * 128:(oc + 1) * 128], src(k), k == 0, k == kc - 1,
                           wk + [skey(k)], [("ps", b)])
                    evac(j * ncol_chunks + oc, b)

        def resid_evac(T):
            def f(oc, b):
                TT("dve", x[:, oc, 0:T], ps[b][:, 0:T], x[:, oc, 0:T], ALU.add, [("ps", b), ("x", oc)], [("x", oc)])
            return f

        def load_T(dstf, dkeyf, src2d, ntok, ncols, queue="sp"):
            for t0 in range(0, ntok, 128):
                n = min(128, ntok - t0)
                for c0 in range(0, ncols, 512):
                    w_ = min(512, ncols - c0)
                    s = nstg()
                    DMA(queue, stg[s][0:n, 0:w_], src2d[t0:t0 + n, c0:c0 + w_], (), [("stg", s)], "stg%d" % s)
                    b = nb()
                    nch = w_ // 128
                    for cc in range(nch):
                        TR(ps[b][:, cc * 128:(cc + 1) * 128], stg[s][:, cc * 128:(cc + 1) * 128], [("stg", s)], [("ps", b)])
                    src = ps[b][:, 0:nch * 128].rearrange("p (c t) -> p c t", c=nch)[:, :, 0:n]
                    CP(evac_eng(), dstf(c0 // 128, nch, t0, n), src, [("ps", b)],
                       [dkeyf(c0 // 128 + cc) for cc in range(nch)])

        def store_T(srcf, skeyf, dst2d, ntok, ncols, pad=False):
            for t0 in range(0, ntok, 128):
                n = min(128, ntok - t0)
                for c0 in range(0, ncols, 512):
                    w_ = min(512, ncols - c0)
                    nch = w_ // 128
                    b = nb()
                    for cc in range(nch):
                        c = c0 // 128 + cc
                        if n == 128:
                            TR(ps[b][:, cc * 128:(cc + 1) * 128], srcf(c, t0, n), [skeyf(c)], [("ps", b)])
                        else:
                            CP("dve", tpad[:, 0:n], srcf(c, t0, n), [skeyf(c)], ["tpad"])
                            TR(ps[b][:, cc * 128:(cc + 1) * 128], tpad[:, :], ["tpad"], [("ps", b)])
                    s = nstg()
                    CP(evac_eng(), stg[s][0:n, 0:w_], ps[b][0:n, 0:w_], [("ps", b)], [("stg", s)])
                    DMA("act", dst2d[t0:t0 + n, c0:c0 + w_], stg[s][0:n, 0:w_], [("stg", s)], [], "stg%d" % s)

        mem_front_done = set()

        def mem_front(bl):
            if bl in mem_front_done or bl >= nseq:
                return
            mem_front_done.add(bl)
            memx = lambda c: xc[:, :, :].rearrange("p a (h t) -> p (a h) t", h=2)[:, c, :]
            memxk = lambda c: ("xc", c // 2)
            mn0 = lambda c: xcb[:, :, :].rearrange("p a (h t) -> p (a h) t", h=2)[:, c, :]
            mn0k = lambda c: ("xcb", c // 2)
            mn1 = lambda c: dbf[:, :, :].rearrange("p a (h t) -> p (a h) t", h=2)[:, c, :]
            mn1k = lambda c: ("dbf", c // 2)
            xv = xc[:, :, :].rearrange("p a (h t) -> p (a h) t", h=2)
            load_T(lambda c0, nch, t0, n: xv[:, c0:c0 + nch, t0:t0 + n], memxk, mem_prompt[bl], NMEM, D)
            rmsnorm(memx, memxk, mn0, mn0k, "norm_mem0", NMEM, second=(mn1, mn1k, "norm_mem1"))

        def mem_phase(bl):
            mem_front(bl)
            for l in range(2 if MEMMASK & 2 else 0):
                if l == 0:
                    mn = lambda c: xcb[:, :, :].rearrange("p a (h t) -> p (a h) t", h=2)[:, c, :]
                    mnk = lambda c: ("xcb", c // 2)
                else:
                    mn = lambda c: dbf[:, :, :].rearrange("p a (h t) -> p (a h) t", h=2)[:, c, :]
                    mnk = lambda c: ("dbf", c // 2)
                if not (MEMMASK & 4):
                    continue
                for j in range(2):
                    wt, wk = wnext("mk%d" % l, j)
                    wv = wt[:, :].rearrange("p (k n) -> p k n", k=8)
                    for oc in range(4):
                        b = nb()
                        for k in range(8):
                            MM(ps[b][:, 0:NMEM], wv[:, k, oc * 128:(oc + 1) * 128], mn(k), k == 0, k == 7,
                               wk + [mnk(k)], [("ps", b)])
                        CP(evac_eng(), memk[l][:, j * 4 + oc, :], ps[b][:, 0:NMEM], [("ps", b)], [("memk", l)])
                    for tb in range(2 if MEMMASK & 8 else 0):
                        b = nb()
                        for k in range(8):
                            MM(ps[b][:, :], mn(k)[:, tb * 128:(tb + 1) * 128], wv[:, k, :], k == 0, k == 7,
                               wk + [mnk(k)], [("ps", b)])
                        s = nstg()
                        CP(evac_eng(), stg[s][:, :], ps[b][:, :], [("ps", b)], [("stg", s)])
                        DMA("act", mem_k_p[l, bl, tb * 128:(tb + 1) * 128, j * 512:(j + 1) * 512], stg[s][:, :],
                            [("stg", s)], [], "stg%d" % s)
                for j in range(2 if MEMMASK & 16 else 0):
                    wt, wk = wnext("mv%d" % l, j)
                    wv = wt[:, :].rearrange("p (k n) -> p k n", k=8)
                    for tb in range(2):
                        b = nb()
                        for k in range(8):
                            MM(ps[b][:, :], mn(k)[:, tb * 128:(tb + 1) * 128], wv[:, k, :], k == 0, k == 7,
                               wk + [mnk(k)], [("ps", b)])
                        s = nstg()
                        CP("act", stg[s][:, :], ps[b][:, :], [("ps", b)], [("stg", s)])
                        CP("dve", memv[l][:, tb, j * 512:(j + 1) * 512], ps[b][:, :], [("ps", b)], [("memv", l)])
                        DMA("act", mem_v_p[l, bl, tb * 128:(tb + 1) * 128, j * 512:(j + 1) * 512], stg[s][:, :],
                            [("stg", s)], [], "stg%d" % s)

        def mem_load_sample(l, bl, slot):
            for tb in range(2):
                for c0 in range(0, D, 512):
                    s = nstg()
                    DMA("sp", stg[s][:, :], cmem_k[l, bl, tb * 128:(tb + 1) * 128, c0:c0 + 512], (), [("stg", s)], "stg%d" % s)
                    b = nb()
                    for cc in range(4):
                        TR(ps[b][:, cc * 128:(cc + 1) * 128], stg[s][:, cc * 128:(cc + 1) * 128], [("stg", s)], [("ps", b)])
                    CP(evac_eng(), memk[slot][:, c0 // 128:c0 // 128 + 4, tb * 128:(tb + 1) * 128],
                       ps[b][:, :].rearrange("p (c t) -> p c t", c=4), [("ps", b)], [("memk", slot)])
                    s = nstg()
                    DMA("sp", stg[s][:, :], cmem_v[l, bl, tb * 128:(tb + 1) * 128, c0:c0 + 512], (), [("stg", s)], "stg%d" % s)
                    CP(evac_eng(), memv[slot][:, tb, c0:c0 + 512], stg[s][:, :], [("stg", s)], [("memv", slot)])

        def cross_attn(l, segs, T, kvslot_of):
            rmsnorm(lambda c: x[:, c, 0:T], lambda c: ("x", c), lambda c: h[:, c, 0:T], lambda c: ("h", c),
                    "norm_cross%d" % l, T)

            def qev(oc, b):
                CP(evac_eng(), q[:, oc, 0:T], ps[b][:, 0:T], [("ps", b)], [("q", oc)])
            proj_fm("mq%d" % l, 2, lambda k: h[:, k, 0:T], lambda k: ("h", k), T, qev, kouter=True)
            pi = 0
            for sg in segs:
                c0, n = sg["c0"], sg["n"]
                slot = kvslot_of(sg)
                for hd in range(4):
                    pt = pT2[pi % 2]
                    ptk = ("pT2", pi % 2)
                    rd = rden[pi % 2]
                    rdk = ("rden", pi % 2)
                    pi += 1
                    for kb in range(2):
                        b = nb()
                        for dc in range(2):
                            MM(ps[b][:, 0:n], memk[slot][:, 2 * hd + dc, kb * 128:(kb + 1) * 128],
                               q[:, 2 * hd + dc, c0:c0 + n], dc == 0, dc == 1,
                               [("memk", slot), ("q", 2 * hd + dc)], [("ps", b)])
                        ACT(pt[:, kb, 0:n], ps[b][:, 0:n], AF.Exp, [("ps", b)], [ptk], scale=1.0 / 16.0)
                    b = nb()
                    for kb in range(2):
                        MM(ps[b][:, 0:n], ones_bf[:, :], pt[:, kb, 0:n], kb == 0, kb == 1, [ptk, "ones"], [("ps", b)])
                    ACT(rd[:, 0:n], ps[b][:, 0:n], AF.Ln, [("ps", b)], [rdk])
                    ACT(rd[:, 0:n], rd[:, 0:n], AF.Exp, [rdk], [rdk], scale=-1.0)
                    for dc in range(2):
                        b = nb()
                        for kb in range(2):
                            MM(ps[b][:, 0:n], memv[slot][:, kb, hd * 256 + dc * 128: hd * 256 + (dc + 1) * 128],
                               pt[:, kb, 0:n], kb == 0, kb == 1, [("memv", slot), ptk], [("ps", b)])
                        TT("dve", ycat[:, 2 * hd + dc, c0:c0 + n], ps[b][:, 0:n], rd[:, 0:n], ALU.mult,
                           [("ps", b), rdk], [("ycat", 2 * hd + dc)])
            proj_fm("mo%d" % l, 2, lambda k: ycat[:, k, 0:T], lambda k: ("ycat", k), T, resid_evac(T), kouter=True)

        def mlp(l, T):
            rmsnorm(lambda c: x[:, c, 0:T], lambda c: ("x", c), lambda c: h[:, c, 0:T], lambda c: ("h", c),
                    "norm_mlp%d" % l, T)
            rr = [0]
            for half in range(2):
                for j in range(4):
                    wt, wk = wnext("up%d" % l, half * 4 + j)
                    wv = wt[:, :].rearrange("p (k n) -> p k n", k=8)
                    kob = None
                    if half == 0 and j == 0:
                        kob = [nb() for _ in range(4)]
                        for k in range(8):
                            for oc in range(4):
                                MM(ps[kob[oc]][:, 0:T], wv[:, k, oc * 128:(oc + 1) * 128], h[:, k, 0:T], k == 0, k == 7,
                                   wk + [("h", k)], [("ps", kob[oc])])
                    for oc in range(4):
                        if kob is not None:
                            b = kob[oc]
                        else:
                            b = nb()
                            for k in range(8):
                                MM(ps[b][:, 0:T], wv[:, k, oc * 128:(oc + 1) * 128], h[:, k, 0:T], k == 0, k == 7,
                                   wk + [("h", k)], [("ps", b)])
                        ti = rr[0] % 4
                        rr[0] += 1
                        t = tmp[ti]
                        ACT(t[:, 0:T], ps[b][:, 0:T], AF.Relu, [("ps", b)], [("tmp", ti)])
                        TT("dve", hid[:, j * 4 + oc, 0:T], t[:, 0:T], t[:, 0:T], ALU.mult, [("tmp", ti)], [("hid", j * 4 + oc)])
                for cb in range(4):
                    wt, wk = wnext("down%d" % l, half * 4 + cb)
                    wv = wt[:, :].rearrange("p (k n) -> p k n", k=16)
                    for cl in range(2):
                        oc = cb * 2 + cl
                        b = nb()
                        for k in range(16):
                            MM(ps[b][:, 0:T], wv[:, k, cl * 128:(cl + 1) * 128], hid[:, k, 0:T], k == 0, k == 15,
                               wk + [("hid", k)], [("ps", b)])
                        TT("dve", x[:, oc, 0:T], ps[b][:, 0:T], x[:, oc, 0:T], ALU.add, [("ps", b), ("x", oc)], [("x", oc)])

        def even_layer(segs, T, sample):
            rmsnorm(lambda c: x[:, c, 0:T], lambda c: ("x", c), lambda c: h[:, c, 0:T], lambda c: ("h", c),
                    "norm_mix0", T)
            hs = hstate_s if sample else hstate
            hsk = "hstate_s" if sample else "hstate"
            def hidf(i):
                return hid[:, 2 * i:2 * i + 2, :].bitcast(F32).rearrange("p a b -> p (a b)"), [("hid", 2 * i), ("hid", 2 * i + 1)]
            def qf(i):
                return q[:, 2 * i:2 * i + 2, :].bitcast(F32).rearrange("p a b -> p (a b)"), [("q", 2 * i), ("q", 2 * i + 1)]
            TA = [hidf(c) for c in range(4)]
            TB = [hidf(4 + c) for c in range(4)]
            TC = [qf(c) for c in range(4)]
            TG = [(tmp[i][:, 0:TP], [("tmp", i)]) for i in (0, 1, 2, 6)]
            R4 = range(4)

            def in_block(j):
                wt, wk = wnext("in", j)
                wv = wt[:, :].rearrange("p (k n) -> p k n", k=8)
                kob = None
                if j == 1:
                    kob = [nb() for _ in range(4)]
                    for k in range(8):
                        for oc4 in range(4):
                            MM(ps[kob[oc4]][:, 0:T], wv[:, k, oc4 * 128:(oc4 + 1) * 128], h[:, k, 0:T], k == 0, k == 7,
                               wk + [("h", k)], [("ps", kob[oc4])])
                for oc4 in range(4):
                    if kob is not None:
                        b = kob[oc4]
                    else:
                        b = nb()
                        for k in range(8):
                            MM(ps[b][:, 0:T], wv[:, k, oc4 * 128:(oc4 + 1) * 128], h[:, k, 0:T], k == 0, k == 7,
                               wk + [("h", k)], [("ps", b)])
                    if j == 0:
                        for sg in segs:
                            CP("act", up[:, oc4, sg["ucol"]:sg["ucol"] + sg["n"]], ps[b][:, sg["c0"]:sg["c0"] + sg["n"]],
                               [("ps", b)], [("up", oc4)])
                    elif j == 1:
                        for sg in segs:
                            CP("act", ux[:, oc4, sg["xcol"]:sg["xcol"] + sg["n"]], ps[b][:, sg["c0"]:sg["c0"] + sg["n"]],
                               [("ps", b)], [("ux", oc4)])
                    else:
                        CP("act", ug[:, oc4, 0:T], ps[b][:, 0:T], [("ps", b)], [("ug", oc4)])

            in_block(1)
            for c in R4:
                cw = [COL["conv_w%d" % k] + c for k in range(4)]
                cbc = COL["conv_b"] + c
                for sg in segs:
                    xo, n, c0 = sg["xcol"], sg["n"], sg["c0"]
                    TS("dve", xc[:, c, c0:c0 + n], ux[:, c, xo:xo + n], cvec[:, cw[3]:cw[3] + 1], cvec[:, cbc:cbc + 1],
                       ALU.mult, ALU.add, [("ux", c), "cvec"], [("xc", c)])
                    for k in (2, 1, 0):
                        sh = 3 - k
                        STT(xc[:, c, c0:c0 + n], ux[:, c, xo - sh:xo - sh + n], cvec[:, cw[k]:cw[k] + 1], xc[:, c, c0:c0 + n],
                            ALU.mult, ALU.add, [("ux", c), ("uxh", c), "cvec", ("xc", c)], [("xc", c)])
                CP("act", xcb[:, c, 0:T], xc[:, c, 0:T], [("xc", c)], [("xcb", c)])
            in_block(0)
            flush_pending()
            for g in range(4):
                Wn = 2 << g
                for sg in segs:
                    uc, n, c0 = sg["ucol"], sg["n"], sg["c0"]
                    cur = lambda lo, hi, g=g, uc=uc: up[:, g, uc + lo:uc + hi]
                    curk = [("up", g), ("uph", g)]
                    for lev in range(1, g + 2):
                        sh = 1 << (lev - 1)
                        lo = -(Wn - (1 << lev))
                        ti = 4 + (lev % 2)
                        o = tmp[ti]
                        TT("dve", o[:, 16 + lo:16 + n], cur(lo, n), cur(lo - sh, n - sh), ALU.add, curk, [("tmp", ti)])
                        cur = lambda lo_, hi_, o=o: o[:, 16 + lo_:16 + hi_]
                        curk = [("tmp", ti)]
                    STT(dbf[:, g, c0:c0 + n], cur(0, n), 1.0 / Wn, up[:, g, uc:uc + n], ALU.mult, ALU.subtract,
                        curk + [("up", g)], [("dbf", g)])
                    if sg["first"] and not sample:
                        m = Wn - 1
                        t6 = tmp[3]
                        TT("dve", t6[:, 0:m], cur(0, m), invc[:, g, 0:m], ALU.mult, curk + ["invc"], [("tmp", 3)])
                        TT("dve", dbf[:, g, c0:c0 + m], t6[:, 0:m], up[:, g, uc:uc + m], ALU.subtract,
                           [("tmp", 3), ("up", g)], [("dbf", g)])
            in_block(2)
            for c in R4:
                G, gk = TG[c]
                u_ = ug[:, c, 0:T]
                ACT(G[:, 0:T], u_, AF.Square, [("ug", c)], gk)
                ACT(G[:, 0:T], G[:, 0:T], AF.Identity, gk + ["onec"], gk, bias=onec[:, 0:1], scale=0.044715)
                TT("dve", G[:, 0:T], G[:, 0:T], u_, ALU.mult, gk + [("ug", c)], gk)
            ba = COL["b_rg_a"]
            bx = COL["b_rg_x"]
            for c in R4:
                A, ak = TA[c]
                b_ = nb()
                MM(ps[b_][:, 0:T], wbd[:, c, :], xcb[:, c, 0:T], True, True, ["wbd", ("xcb", c)], [("ps", b_)])
                ACT(A[:, 0:T], ps[b_][:, 0:T], AF.Sigmoid, [("ps", b_), "cvec"], ak, bias=cvec[:, ba + c:ba + c + 1])
            for c in R4:
                Bt, bk = TB[c]
                b_ = nb()
                MM(ps[b_][:, 0:T], wbd[:, 4 + c, :], xcb[:, c, 0:T], True, True, ["wbd", ("xcb", c)], [("ps", b_)])
                ACT(Bt[:, 0:T], ps[b_][:, 0:T], AF.Sigmoid, [("ps", b_), "cvec"], bk, bias=cvec[:, bx + c:bx + c + 1])
            for c in R4:
                G, gk = TG[c]
                ACT(G[:, 0:T], G[:, 0:T], AF.Sigmoid, gk, gk, scale=2.0 * GELU_K)
            for g in range(4):
                b = nb()
                MM(ps[b][:, 0:T], poolw[:, g, :], dbf[:, g, 0:T], True, True, ["poolw", ("dbf", g)], [("ps", b)])
                pc = COL["pool_scale"] + g
                TS("dve", ycat[:, g, 0:T], ps[b][:, 0:T], cvec[:, pc:pc + 1], None, ALU.mult, None, [("ps", b), "cvec"], [("ycat", g)])
            for pair in ((0, 1), (2, 3)):
                for c in pair:
                    A, ak = TA[c]
                    C, ck = TC[c]
                    ACT(C[:, 0:T], A[:, 0:T], AF.Exp, ak + ["c8b"], ck, scale=c8[:, 4 + c:5 + c])
                for c in pair:
                    A, ak = TA[c]
                    ACT(A[:, 0:T], A[:, 0:T], AF.Exp, ak + ["c8"], ak, scale=c8[:, c:c + 1])
                for c in pair:
                    C, ck = TC[c]
                    ACT(C[:, 0:T], C[:, 0:T], AF.Ln, ck + ["onec"], ck, bias=onec[:, 0:1], scale=-1.0)
                for c in pair:
                    C, ck = TC[c]
                    Bt, bk = TB[c]
                    ACT(C[:, 0:T], C[:, 0:T], AF.Exp, ck, ck, scale=0.5)
                    TT("dve", Bt[:, 0:T], Bt[:, 0:T], C[:, 0:T], ALU.mult, bk + ck, bk)
                    TT("dve", Bt[:, 0:T], Bt[:, 0:T], xc[:, c, 0:T], ALU.mult, bk + [("xc", c)], bk)
                for c in pair:
                    A, ak = TA[c]
                    Bt, bk = TB[c]
                    C, ck = TC[c]
                    G, gk = TG[c]
                    for sg in segs:
                        n, c0, bl = sg["n"], sg["c0"], sg["bl"]
                        P.add("dve", (lambda e, o=C[:, c0:c0 + n], a_=A[:, c0:c0 + n], bb=Bt[:, c0:c0 + n],
                                      ini=hs[:, c, bl:bl + 1]:
                                      e.tensor_tensor_scan(out=o, data0=a_, data1=bb, initial=ini, op0=ALU.mult, op1=ALU.add)),
                              ak + bk + [hsk] + ck, ck)
                        CP("pool", hs[:, c, bl:bl + 1], C[:, c0 + n - 1:c0 + n], ck, [hsk])
                    TT("dve", G[:, 0:T], G[:, 0:T], ug[:, c, 0:T], ALU.mult, gk + [("ug", c)], gk)
                    TT("dve", ycat[:, 4 + c, 0:T], C[:, 0:T], G[:, 0:T], ALU.mult, ck + gk, [("ycat", 4 + c)])

            for sg in segs:
                uc, xo, n, bl = sg["ucol"], sg["xcol"], sg["n"], sg["bl"]
                if sg["last"]:
                    pd = pool_s if sample else pool_p
                    cd = conv_s if sample else conv_p
                    store_T(lambda c, t0, nn: up[:, c, uc + n - 15:uc + n], lambda c: ("up", c), pd[bl], 15, 512)
                    store_T(lambda c, t0, nn: ux[:, c, xo + n - 3:xo + n], lambda c: ("ux", c), cd[bl], 3, 512)
                else:
                    for g in range(4):
                        CP("pool", up[:, g, uc - 15:uc], up[:, g, uc + n - 15:uc + n], [("up", g)], [("uph", g)])
                        CP("pool", ux[:, g, xo - 3:xo], ux[:, g, xo + n - 3:xo + n], [("ux", g)], [("uxh", g)])
            proj_fm("out", 2, lambda k: ycat[:, k, 0:T], lambda k: ("ycat", k), T, resid_evac(T), kouter=True)

        def odd_layer(segs, T, sample):
            rmsnorm(lambda c: x[:, c, 0:T], lambda c: ("x", c), lambda c: h[:, c, 0:T], lambda c: ("h", c),
                    "norm_mix1", T)

            def qev(oc, b):
                CP(evac_eng(), q[:, oc, 0:T], ps[b][:, 0:T], [("ps", b)], [("q", oc)])
            proj_fm("qkv", 2, lambda k: h[:, k, 0:T], lambda k: ("h", k), T, qev, kouter=True)
            wt, wk = wnext("qkv", 2)
            wv = wt[:, :].rearrange("p (k n) -> p k n", k=8)
            for kv in range(4):
                b = nb()
                for k in range(8):
                    MM(ps[b][:, 0:T], wv[:, k, kv * 128:(kv + 1) * 128], h[:, k, 0:T], k == 0, k == 7,
                       wk + [("h", k)], [("ps", b)])
                for sg in segs:
                    kc0 = sg["kb0"] + 128
                    CP(evac_eng(), kbuf[:, kv, kc0:kc0 + sg["n"]], ps[b][:, sg["c0"]:sg["c0"] + sg["n"]],
                       [("ps", b)], [("kbuf", kv)])
            wv5 = wt[:, :].rearrange("p (k v r d) -> p k v r d", k=8, v=4, r=2)
            for sg in segs:
                if not sg["last"]:
                    continue
                bl, n, c0 = sg["bl"], sg["n"], sg["c0"]
                nrow = min(128, n)
                t0 = c0 + n - nrow
                b = nb()
                for k in range(8):
                    MM(ps[b][0:nrow, 0:256].rearrange("p (v d) -> p v d", v=4), h[:, k, t0:t0 + nrow], wv5[:, k, :, 0, :],
                       k == 0, k == 7, wk + [("h", k)], [("ps", b)])
                s = nstg()
                CP(evac_eng(), stg[s][0:nrow, 0:256], ps[b][0:nrow, 0:256], [("ps", b)], [("stg", s)])
                kd = swa_k_s if sample else swa_k_p
                DMA("act", kd[bl, 128 - nrow:128, :], stg[s][0:nrow, 0:256], [("stg", s)], [], "stg%d" % s)
                if sample:
                    DMA("sp", swa_k_s[bl, 0:64, :], cache_k[bl, 64:128, :], (), [], "d2d")
            wt, wk = wnext("qkv", 3)
            wv = wt[:, :].rearrange("p (k n) -> p k n", k=8)
            for sg in segs:
                n, c0, bl = sg["n"], sg["c0"], sg["bl"]
                for j in range(n // 64):
                    b = nb()
                    for k in range(8):
                        MM(ps[b][0:64, :], h[:, k, c0 + j * 64:c0 + (j + 1) * 64], wv[:, k, :], k == 0, k == 7,
                           wk + [("h", k)], [("ps", b)])
                    vs = sg["vb0"] + 2 + j
                    CP(evac_eng(), vtok[0:64, vs, :], ps[b][0:64, :], [("ps", b)], [("vtok", vs)])
                    if sg["last"] and j >= n // 64 - 2:
                        s = nstg()
                        CP(evac_eng(), stg[s][0:64, 0:256].rearrange("p (v d) -> p v d", v=4),
                           ps[b][0:64, :].rearrange("p (v r d) -> p v r d", v=4, r=2)[:, :, 0, :], [("ps", b)], [("stg", s)])
                        vd = swa_v_s if sample else swa_v_p
                        row0 = 128 - (n // 64 - j) * 64
                        DMA("act", vd[bl, row0:row0 + 64, :], stg[s][0:64, 0:256], [("stg", s)], [], "stg%d" % s)
                if sample:
                    DMA("sp", swa_v_s[bl, 0:64, :], cache_v[bl, 64:128, :], (), [], "d2d")
            units = []
            for sg in segs:
                for nq in range(sg["n"] // 64):
                    for kv in range(4):
                        units.append((sg, nq, kv))

            def unit_info(ui):
                sg, nq, kv = units[ui]
                ext = [e_ for e_ in (nq, nq + 1, nq + 2) if (e_ >= 2 or sg["hist_valid"])]
                return sg, nq, kv, ext, sg["c0"] + nq * 64, pTs[ui % 2], ("pTs", ui % 2), dns[ui % 2], ("dns", ui % 2)

            def swa_scores(ui):
                sg, nq, kv, ext, qc0, pt, ptk, dn, dnk = unit_info(ui)
                ne = len(ext)
                for par in range(2):
                    b = nb()
                    for ji, e_ in enumerate(ext):
                        kcol = sg["kb0"] + e_ * 64
                        MM(ps[b][0:64, ji * 128:(ji + 1) * 128].rearrange("p (a l) -> p a l", a=2),
                           kbuf[par * 64:(par + 1) * 64, kv, kcol:kcol + 64],
                           q[par * 64:(par + 1) * 64, 2 * kv:2 * kv + 2, qc0:qc0 + 64], True, True,
                           [("kbuf", kv), ("kbufh", kv), ("q", 2 * kv), ("q", 2 * kv + 1)], [("ps", b)])
                    ACT(pt[0:64, 0:ne, par * 128:(par + 1) * 128],
                        ps[b][0:64, 0:ne * 128].rearrange("p (j c) -> p j c", j=ne), AF.Exp, [("ps", b)], [ptk], scale=0.125)

            def swa_pv(ui):
                sg, nq, kv, ext, qc0, pt, ptk, dn, dnk = unit_info(ui)
                bd = nb()
                MM(ps[bd][:, 0:256], ones_bf[0:64, :], esx[0:64, kv, :], True, False, ["ones", "esx"], [("ps", bd)])
                for ji in range(len(ext)):
                    MM(ps[bd][:, 0:256], ones_bf[0:64, :], pt[0:64, ji, :], False, ji == len(ext) - 1,
                       ["ones", ptk], [("ps", bd)])
                ACT(dn[:, :], ps[bd][:, 0:256], AF.Ln, [("ps", bd)], [dnk])
                ACT(dn[:, :], dn[:, :], AF.Exp, [dnk], [dnk], scale=-1.0)
                bo = nb()
                for ji, e_ in enumerate(ext):
                    vs = sg["vb0"] + e_
                    MM(ps[bo][:, 0:256], vtok[0:64, vs, kv * 128:(kv + 1) * 128], pt[0:64, ji, :], ji == 0,
                       ji == len(ext) - 1, [("vtok", vs), ptk], [("ps", bo)])
                for par in range(2):
                    sl = slice(par * 64, (par + 1) * 64)
                    TT("dve", ycat[sl, 2 * kv:2 * kv + 2, qc0:qc0 + 64],
                       ps[bo][sl, par * 128:(par + 1) * 128].rearrange("p (a l) -> p a l", a=2),
                       dn[sl, par * 128:(par + 1) * 128].rearrange("p (a l) -> p a l", a=2), ALU.mult,
                       [("ps", bo), dnk], [("ycat", 2 * kv), ("ycat", 2 * kv + 1)])

            swa_scores(0)
            for ui in range(len(units)):
                if ui + 1 < len(units):
                    swa_scores(ui + 1)
                swa_pv(ui)
            for sg in segs:
                if sample or sg["last"]:
                    continue
                kb0, n, vb0 = sg["kb0"], sg["n"], sg["vb0"]
                for kv in range(4):
                    CP("pool", kbuf[:, kv, kb0:kb0 + 128], kbuf[:, kv, kb0 + n:kb0 + n + 128], [("kbuf", kv)], [("kbufh", kv)])
                for j in range(2):
                    CP("pool", vtok[0:64, vb0 + j, :], vtok[0:64, vb0 + n // 64 + j, :], [("vtok", vb0 + n // 64 + j)],
                       [("vtok", vb0 + j)])
            proj_fm("wo", 2, lambda k: ycat[:, k, 0:T], lambda k: ("ycat", k), T, resid_evac(T), kouter=True)

        pending = []

        def flush_pending():
            while pending:
                pending.pop(0)()

        def final_norm_store(T, dst2d):
            yb = lambda c: hid[:, 2 * c:2 * c + 2, :].bitcast(F32).rearrange("p a b -> p (a b)")
            ybk = lambda c: [("hid", 2 * c), ("hid", 2 * c + 1)]
            rmsnorm(lambda c: x[:, c, 0:T], lambda c: ("x", c), lambda c: h[:, c, 0:T], lambda c: ("h", c),
                    "norm_final", T, inplace_out=lambda c: yb(c)[:, 0:T], out_keys=ybk)

            def do_store():
                for t0 in range(0, T, 128):
                    for c0 in range(0, D, 512):
                        b = nb()
                        for cc in range(4):
                            c = c0 // 128 + cc
                            TR(ps[b][:, cc * 128:(cc + 1) * 128], yb(c)[:, t0:t0 + 128], ybk(c), [("ps", b)])
                        s_ = nstg()
                        CP(evac_eng(), stg[s_][:, :], ps[b][:, :], [("ps", b)], [("stg", s_)])
                        DMA("act", dst2d[t0:t0 + 128, c0:c0 + 512], stg[s_][:, :], [("stg", s_)], [], "stg%d" % s_)
            pending.append(do_store)

        def run_tile(segs, T, sample, src2d, dst2d, kvslot0, kvslot1, next_mem=None):
            load_T(lambda c0, nch, t0, n: x[:, c0:c0 + nch, t0:t0 + n], lambda c: ("x", c), src2d, T, D)
            even_layer(segs, T, sample)
            cross_attn(0, segs, T, kvslot0)
            mlp(0, T)
            odd_layer(segs, T, sample)
            cross_attn(1, segs, T, kvslot1)
            if next_mem is not None and STAGE >= 99:
                mem_front(next_mem)
            mlp(1, T)
            final_norm_store(T, dst2d)

        tb_ids = [bid[b] for b in tile_blocks]
        mb_ids = [bid[b] for b in mem_blocks]
        if do_sample and not DEBUG_ONDEMAND:
            wseq.extend(tb_ids)
        for s_ in range(nseq if not DEBUG_ONDEMAND else 0):
            wseq.extend(mb_ids)
            for _ in range(SEQ // TP):
                wseq.extend(tb_ids)

        STAGE = int(os.environ.get("KDEBUG_STAGE", "99"))
        NTI = int(os.environ.get("KDEBUG_NTILE", str(SEQ // TP)))
        if not DEBUG_ONDEMAND:
            w_issue_upto(NW)
        setup_consts()
        if DEBUG_ONDEMAND and STAGE >= 1:
            prepass()
        if not DEBUG_ONDEMAND:
            assert sorted(wseq[:CONV_N]) == list(range(CONV_N))
        if do_sample and STAGE >= 4:
            T = NB * DEC
            segs = []
            for bl in range(NB):
                segs.append(dict(bl=bl, c0=bl * DEC, n=DEC, ucol=bl * 80 + 16, xcol=bl * 68 + 4, kb0=bl * 192, vb0=bl * 3,
                                 first=True, last=True, hist_valid=True))
            for sg in segs:
                bl = sg["bl"]
                uc, xo = sg["ucol"], sg["xcol"]
                load_T(lambda c0, nch, t0, n, uc=uc: up[:, c0:c0 + nch, uc - 15:uc], lambda c: ("uph", c), state_pool[bl], 15, 512)
                load_T(lambda c0, nch, t0, n, xo=xo: ux[:, c0:c0 + nch, xo - 3:xo], lambda c: ("uxh", c), state_conv[bl], 3, 512)
                s = nstg()
                DMA("sp", stg[s][:, 0:256], cache_k[bl], (), [("stg", s)], "stg%d" % s)
                s2 = nstg()
                for r_ in range(2):
                    CP("pool", stg[s2][:, :].rearrange("p (v r d) -> p v r d", v=4, r=2)[:, :, r_, :],
                       stg[s][:, 0:256].rearrange("p (v d) -> p v d", v=4), [("stg", s)], [("stg", s2)])
                b = nb()
                for kv in range(4):
                    TR(ps[b][:, kv * 128:(kv + 1) * 128], stg[s2][:, kv * 128:(kv + 1) * 128], [("stg", s2)], [("ps", b)])
                CP(evac_eng(), kbuf[:, :, sg["kb0"]:sg["kb0"] + 128], ps[b][:, :].rearrange("p (v t) -> p v t", v=4),
                   [("ps", b)], [("kbufh", kv_) for kv_ in range(4)])
                for j in range(2):
                    s = nstg()
                    DMA("sp", stg[s][0:64, 0:256], cache_v[bl, j * 64:(j + 1) * 64, :], (), [("stg", s)], "stg%d" % s)
                    for r_ in range(2):
                        CP("pool", vtok[0:64, sg["vb0"] + j, :].rearrange("p (v r d) -> p v r d", v=4, r=2)[:, :, r_, :],
                           stg[s][0:64, 0:256].rearrange("p (v d) -> p v d", v=4), [("stg", s)], [("vtok", sg["vb0"] + j)])
            load_T(lambda c0, nch, t0, n: hstate_s[:, c0:c0 + nch, 0:4], lambda c: "hstate_s", state_lru, 4, 512)

            kvctr = [0]

            def mk_kvslot(l):
                def f(sg):
                    slot = kvctr[0] % 2
                    kvctr[0] += 1
                    mem_load_sample(l, sg["bl"], slot)
                    return slot
                return f
            run_tile(segs, T, True, x_sample, y_sample, mk_kvslot(0), mk_kvslot(1), next_mem=0)
            store_T(lambda c, t0, n: hstate_s[:, c, 0:4], lambda c: "hstate_s", lru_s, 4, 512)

        for bl in range(nseq if STAGE >= 2 else 0):
            mem_phase(bl)
            for ti in range(NTI if STAGE >= 3 else 0):
                seg = dict(bl=bl, c0=0, n=TP, ucol=16, xcol=4, kb0=0, vb0=0, first=(ti == 0), last=(ti == SEQ // TP - 1),
                           hist_valid=(ti != 0))
                if ti == 0:
                    for g in range(4):
                        MSET("pool", up[:, g, 0:16], 0.0, [("uph", g)])
                        MSET("pool", ux[:, g, 0:4], 0.0, [("uxh", g)])
                run_tile([seg], TP, False, x_prompt[bl, ti * TP:(ti + 1) * TP, :], y_prompt[bl, ti * TP:(ti + 1) * TP, :],
                         lambda sg: 0, lambda sg: 1, next_mem=(bl + 1 if ti == SEQ // TP - 1 else None))
            if bl == nseq - 1:
                store_T(lambda c, t0, n: hstate[:, c, 0:4], lambda c: "hstate", lru_p, 4, 512)
        flush_pending()
        assert STAGE < 99 or wpos[0] == len(wseq), (wpos, len(wseq))

        chans = sorted(P.chan_n.keys())
        sems = {e_: es.enter_context(nc.semaphore("s_" + e_)) for e_ in COMPUTE}
        chan_sems = {c_: es.enter_context(nc.semaphore("d_" + c_)) for c_ in chans}
        block = es.enter_context(nc.Block())
        P.emit(nc, block, sems, chan_sems)
    return nc, len(P.ops)


_CACHE = {}


def _stack_vecs(inp):
    rows = []
    for nm in ["norm_mix", "norm_cross", "norm_mem", "norm_mlp"]:
        for l in range(2):
            rows.append(np.asarray(inp[nm][l]).reshape(8, 128))
    rows.append(np.asarray(inp["norm_final"]).reshape(8, 128))
    cw = np.asarray(inp["conv_w"])[0]
    for k in range(4):
        rows.append(cw[k].reshape(4, 128))
    for nm in ["conv_b", "b_rg_a", "b_rg_x", "rg_lambda", "pool_scale"]:
        rows.append(np.asarray(inp[nm])[0].reshape(4, 128))
    v = np.concatenate(rows, axis=0).astype(np.float32)
    out = np.zeros((128, 128), np.float32)
    out[:v.shape[0]] = v
    return out


def kernel(**inp):
    nseq = int(os.environ.get("KDEBUG_NSEQ", NB))
    do_sample = os.environ.get("KDEBUG_NOSAMPLE", "0") != "1"
    key = (nseq, do_sample)
    if key not in _CACHE:
        _CACHE[key] = build_program(nseq, do_sample)[0]
    nc = _CACHE[key]
    f = lambda a: np.ascontiguousarray(np.asarray(a, dtype=np.float32))
    shared = dict(
        vecs=_stack_vecs(inp), attn_sinks=f(inp["attn_sinks"]).reshape(1, 16), ident=np.eye(128, dtype=np.float32),
        w_in=f(inp["w_in_even"][0]), w_out=f(inp["w_out_even"][0]), w_qkv=f(inp["w_qkv_odd"][0]), w_o=f(inp["w_o_odd"][0]),
        w_mq=f(inp["w_mq"]), w_mk=f(inp["w_mk"]), w_mv=f(inp["w_mv"]), w_mo=f(inp["w_mo"]), w_up=f(inp["w_up"]),
        w_down=f(inp["w_down"]), pool_w=f(inp["pool_w"][0]), w_rg_a=f(inp["w_rg_a"][0]), w_rg_x=f(inp["w_rg_x"][0]),
    )
    in_maps = []
    for i in range(NCORE):
        sl = slice(i * NB, (i + 1) * NB)
        m = dict(shared)
        m.update(
            x_prompt=f(inp["x_prompt"][sl]), x_sample=f(inp["x_sample"][sl]).reshape(NB * DEC, D),
            state_pool=f(inp["state_pool"][0, sl]), state_conv=f(inp["state_conv"][0, sl]), state_lru=f(inp["state_lru"][0, sl]),
            cache_swa_k=f(inp["cache_swa_k"][0, sl]).reshape(NB, 128, 256), cache_swa_v=f(inp["cache_swa_v"][0, sl]).reshape(NB, 128, 256),
            cache_mem_k=f(inp["cache_mem_k"][:, sl]).reshape(2, NB, NMEM, D), cache_mem_v=f(inp["cache_mem_v"][:, sl]).reshape(2, NB, NMEM, D),
            mem_prompt=f(inp["mem_prompt"][sl]),
        )
        in_maps.append(m)
    res = run_bass_kernel_spmd(nc, in_maps, core_ids=list(range(NCORE)))
    R = res.results
    cat = lambda k, ax=0: np.concatenate([np.asarray(r[k]) for r in R], axis=ax)
    B = NCORE * NB
    y_prompt = cat("y_prompt")
    y_sample = cat("y_sample").reshape(B, DEC, D)
    pool_p = cat("pool_p")[None]
    conv_p = cat("conv_p")[None]
    lru_p = cat("lru_p")[None]
    swa_k_p = cat("swa_k_p").reshape(1, B, 128, 4, 64)
    swa_v_p = cat("swa_v_p").reshape(1, B, 128, 4, 64)
    mem_k_p = cat("mem_k_p", 1).reshape(2, B, NMEM, 4, 256)
    mem_v_p = cat("mem_v_p", 1).reshape(2, B, NMEM, 4, 256)
    pool_s = cat("pool_s")[None]
    conv_s = cat("conv_s")[None]
    lru_s = cat("lru_s")[None]
    swa_k_s = cat("swa_k_s").reshape(1, B, 128, 4, 64)
    swa_v_s = cat("swa_v_s").reshape(1, B, 128, 4, 64)
    return (y_prompt, y_sample, pool_p, conv_p, lru_p, swa_k_p, swa_v_p, mem_k_p, mem_v_p,
            pool_s, conv_s, lru_s, swa_k_s, swa_v_s)
```

```python
import os
import numpy as np
import concourse.bass as bass
import concourse.mybir as mybir
from concourse.bass_utils import run_bass_kernel_spmd

F32 = mybir.dt.float32
BF16 = mybir.dt.bfloat16
AF = mybir.ActivationFunctionType
ALU = mybir.AluOpType

NCORE = 8
D = 1024
KC = 8
TP = 512
SEQ = 2048
NB = 4
DEC = 64
NMEM = 256
EPS = 1e-6
NW = 4
GELU_K = 0.7978845608028654

COMPUTE = ("pe", "act", "dve", "pool")


class Prog:
    def __init__(self):
        self.ops = []
        self.lastw = {}
        self.readers = {}
        self.chan_n = {}

    def add(self, eng, fn, r=(), w=(), chan=None):
        i = len(self.ops)
        psr = [k for k in r if isinstance(k, tuple) and k[0] == "ps"]
        if psr:
            r = [k for k in r if not (isinstance(k, tuple) and k[0] == "ps")]
            w = list(w) + psr
        deps = set()
        raw = set()
        for k in r:
            j = self.lastw.get(k)
            if j is not None:
                deps.add(j)
                raw.add(j)
        for k in w:
            j = self.lastw.get(k)
            if j is not None:
                deps.add(j)
            for j in self.readers.get(k, ()):
                deps.add(j)
        deps.discard(i)
        cnt = None
        if chan is not None:
            cnt = self.chan_n.get(chan, 0) + 1
            self.chan_n[chan] = cnt
        self.ops.append(dict(eng=eng, fn=fn, deps=deps, raw=raw, chan=chan, cnt=cnt))
        for k in w:
            self.lastw[k] = i
            self.readers[k] = []
        for k in r:
            lst = self.readers.setdefault(k, [])
            if chan is None:
                lst[:] = [j for j in lst if not (self.ops[j]["chan"] is None and self.ops[j]["eng"] == eng)]
            lst.append(i)
        return i

    def emit(self, nc, block, sems, chan_sems):
        ops = self.ops
        for i, op in enumerate(ops):
            best = {}
            dmas = []
            for j in op["deps"]:
                d = ops[j]
                if d["chan"] is not None:
                    dmas.append(j)
                    continue
                if d["eng"] == op["eng"] and op["chan"] is None and op["eng"] == "pe":
                    continue
                if d["eng"] == op["eng"] and op["chan"] is not None:
                    pass
                if j > best.get(d["eng"], -1):
                    best[d["eng"]] = j
            op["wait_c"] = best
            op["wait_d"] = dmas
        sig = [False] * len(ops)
        for op in ops:
            for j in op["wait_c"].values():
                sig[j] = True
        counts = {e: 0 for e in COMPUTE}
        for i, op in enumerate(ops):
            if op["chan"] is None and sig[i]:
                counts[op["eng"]] += 1
                op["sigval"] = counts[op["eng"]]
        streams = {}
        for i, op in enumerate(ops):
            streams.setdefault(op["eng"], []).append(i)

        def run_stream(name, e):
            waited = {}
            for i in streams.get(name, []):
                op = ops[i]
                for eng2, j in op["wait_c"].items():
                    v = ops[j]["sigval"]
                    key = ("c", eng2)
                    if waited.get(key, 0) < v:
                        e.wait_ge(sems[eng2], v)
                        waited[key] = v
                for j in op["wait_d"]:
                    d = ops[j]
                    v = 16 * d["cnt"]
                    key = ("d", d["chan"])
                    if waited.get(key, 0) < v:
                        e.wait_ge(chan_sems[d["chan"]], v)
                        waited[key] = v
                if op["chan"] is not None:
                    v = 16 * (op["cnt"] - 1)
                    key = ("d", op["chan"])
                    if v > 0 and waited.get(key, 0) < v:
                        e.wait_ge(chan_sems[op["chan"]], v)
                        waited[key] = v
                ins = op["fn"](e)
                if op["chan"] is not None:
                    ins.then_inc(chan_sems[op["chan"]], 16)
                elif sig[i]:
                    ins.then_inc(sems[op["eng"]], 1)
            if name in ("sp", "act", "pool"):
                final = {}
                for i in streams.get(name, []):
                    op = ops[i]
                    if op["chan"] is not None:
                        final[op["chan"]] = max(final.get(op["chan"], 0), 16 * op["cnt"])
                for ch, v in final.items():
                    if waited.get(("d", ch), 0) < v:
                        e.wait_ge(chan_sems[ch], v)

        @block.sync
        def _(e):
            run_stream("sp", e)

        @block.tensor
        def _(e):
            run_stream("pe", e)

        @block.scalar
        def _(e):
            run_stream("act", e)

        @block.vector
        def _(e):
            run_stream("dve", e)

        @block.gpsimd
        def _(e):
            run_stream("pool", e)


def block_catalogue():
    blocks = []
    for l in range(2):
        if l == 0:
            blocks += [("in", 1), ("in", 0), ("in", 2)] + [("out", j) for j in range(2)]
        else:
            blocks += [("qkv", j) for j in range(4)] + [("wo", j) for j in range(2)]
        blocks += [("mq%d" % l, j) for j in range(2)] + [("mo%d" % l, j) for j in range(2)]
        for half in range(2):
            blocks += [("up%d" % l, half * 4 + j) for j in range(4)]
            blocks += [("down%d" % l, half * 4 + j) for j in range(4)]
    memb = []
    for l in range(2):
        memb += [("mk%d" % l, j) for j in range(2)] + [("mv%d" % l, j) for j in range(2)]
    return blocks, memb


def build_program(nseq=NB, do_sample=True):
    nc = bass.Bass("TRN2", target_bir_lowering=False)
    P = Prog()

    def din(name, shape):
        return nc.dram_tensor(name, shape, F32, kind="ExternalInput").ap()

    def dout(name, shape):
        return nc.dram_tensor(name, shape, F32, kind="ExternalOutput").ap()

    x_prompt = din("x_prompt", [NB, SEQ, D])
    x_sample = din("x_sample", [NB * DEC, D])
    state_pool = din("state_pool", [NB, 15, 512])
    state_conv = din("state_conv", [NB, 3, 512])
    state_lru = din("state_lru", [NB, 512])
    cache_k = din("cache_swa_k", [NB, 128, 256])
    cache_v = din("cache_swa_v", [NB, 128, 256])
    cmem_k = din("cache_mem_k", [2, NB, NMEM, D])
    cmem_v = din("cache_mem_v", [2, NB, NMEM, D])
    mem_prompt = din("mem_prompt", [NB, NMEM, D])
    vecs = din("vecs", [128, 128])
    sinks = din("attn_sinks", [1, 16])
    ident_d = din("ident", [128, 128])
    W = dict(
        w_in=din("w_in", [D, 1536]), w_out=din("w_out", [D, D]), w_qkv=din("w_qkv", [D, 1536]),
        w_o=din("w_o", [D, D]), w_mq=din("w_mq", [2, D, D]), w_mk=din("w_mk", [2, D, D]),
        w_mv=din("w_mv", [2, D, D]), w_mo=din("w_mo", [2, D, D]), w_up=din("w_up", [2, D, 4 * D]),
        w_down=din("w_down", [2, 4 * D, D]),
    )
    pool_w_d = din("pool_w", [4, 128, 128])
    w_rg_a_d = din("w_rg_a", [8, 64, 64])
    w_rg_x_d = din("w_rg_x", [8, 64, 64])

    y_prompt = dout("y_prompt", [NB, SEQ, D])
    y_sample = dout("y_sample", [NB * DEC, D])
    pool_p = dout("pool_p", [NB, 15, 512])
    conv_p = dout("conv_p", [NB, 3, 512])
    lru_p = dout("lru_p", [NB, 512])
    swa_k_p = dout("swa_k_p", [NB, 128, 256])
    swa_v_p = dout("swa_v_p", [NB, 128, 256])
    mem_k_p = dout("mem_k_p", [2, NB, NMEM, D])
    mem_v_p = dout("mem_v_p", [2, NB, NMEM, D])
    pool_s = dout("pool_s", [NB, 15, 512])
    conv_s = dout("conv_s", [NB, 3, 512])
    lru_s = dout("lru_s", [NB, 512])
    swa_k_s = dout("swa_k_s", [NB, 128, 256])
    swa_v_s = dout("swa_v_s", [NB, 128, 256])

    tile_blocks, mem_blocks = block_catalogue()
    all_blocks = tile_blocks + mem_blocks
    bid = {b: i for i, b in enumerate(all_blocks)}
    wscr = nc.dram_tensor("wscr", [len(all_blocks), 128, 4096], BF16, kind="Internal").ap()

    import contextlib
    es = contextlib.ExitStack()
    with es:
        def sb(name, shape, dt=F32):
            return es.enter_context(nc.sbuf_tensor(name, shape, dt))

        x = sb("x", [128, 8, TP])
        h = sb("h", [128, 8, TP], BF16)
        up = sb("up", [128, 4, 16 + TP])
        ux = sb("ux", [128, 4, 4 + TP])
        ug = sb("ug", [128, 4, TP])
        ycat = sb("ycat", [128, 8, TP], BF16)
        NTMP = 7
        tmp = [sb("tmp%d" % i, [128, 16 + TP]) for i in range(NTMP)]
        xc = sb("xc", [128, 4, TP])
        xcb = sb("xcb", [128, 4, TP], BF16)
        dbf = sb("dbf", [128, 4, TP], BF16)
        q = sb("q", [128, 8, TP], BF16)
        kbuf = sb("kbuf", [128, 4, 768], BF16)
        vtok = sb("vtok", [64, 12, 512], BF16)
        hid = sb("hid", [128, 16, TP], BF16)
        wring = [sb("wr%d" % i, [128, 4096], BF16) for i in range(NW)]
        memk = [sb("memk%d" % i, [128, 8, NMEM], BF16) for i in range(2)]
        memv = [sb("memv%d" % i, [128, 2, D], BF16) for i in range(2)]
        NSTG = 4
        stg = [sb("stg%d" % i, [128, 512]) for i in range(NSTG)]
        pT2 = [sb("pT2_%d" % i, [128, 2, TP], BF16) for i in range(2)]
        rden = [sb("rden%d" % i, [128, TP]) for i in range(2)]
        pTs = [sb("pTs%d" % i, [64, 3, 256], BF16) for i in range(2)]
        dns = [sb("dns%d" % i, [128, 256]) for i in range(2)]
        ident = sb("ident_sb", [128, 128])
        ones_bf = sb("ones_bf", [128, 128], BF16)
        ones_f = sb("ones_f", [128, 64])
        cvec = sb("cvec", [128, 128])
        negb = sb("negb", [128, 8])
        c8 = sb("c8", [128, 8])
        epsc = sb("epsc", [128, 1])
        onec = sb("onec", [128, 1])
        esink = sb("esink", [128, 16])
        esx = sb("esx", [64, 4, 256], BF16)
        poolw = sb("poolw", [128, 4, 128], BF16)
        wbd = sb("wbd", [128, 8, 128], BF16)
        invc = sb("invc", [128, 4, 16])
        hstate = sb("hstate", [128, 4, 4])
        hstate_s = sb("hstate_s", [128, 4, 4])
        tpad = sb("tpad", [128, 128])
        ps = [es.enter_context(nc.psum_tensor("ps%d" % i, [128, 512], F32)) for i in range(8)]

        COL = {}
        col = 0
        for nm, n in [("norm_mix0", 8), ("norm_mix1", 8), ("norm_cross0", 8), ("norm_cross1", 8),
                      ("norm_mem0", 8), ("norm_mem1", 8), ("norm_mlp0", 8), ("norm_mlp1", 8),
                      ("norm_final", 8), ("conv_w0", 4), ("conv_w1", 4), ("conv_w2", 4), ("conv_w3", 4),
                      ("conv_b", 4), ("b_rg_a", 4), ("b_rg_x", 4), ("rg_lambda", 4), ("pool_scale", 4)]:
            COL[nm] = col
            col += n
        assert col <= 128

        bank_ctr = [0]

        def nb():
            b = bank_ctr[0] % 8
            bank_ctr[0] += 1
            return b

        stg_ctr = [0]

        def nstg():
            s = stg_ctr[0] % NSTG
            stg_ctr[0] += 1
            return s

        def MM(out, lhsT, rhs, start, stop, r, w):
            P.add("pe", lambda e: e.matmul(out, lhsT=lhsT, rhs=rhs, start=start, stop=stop), r, w)

        def TR(out, in_, r, w):
            P.add("pe", lambda e: e.transpose(out, in_, ident[:, :]), list(r) + ["ident"], w)

        def ACT(out, in_, func, r, w, bias=None, scale=1.0):
            if bias is None:
                P.add("act", lambda e: e.activation(out=out, in_=in_, func=func, scale=scale), r, w)
            else:
                P.add("act", lambda e: e.activation(out=out, in_=in_, func=func, bias=bias, scale=scale), r, w)

        def TT(eng, out, in0, in1, op, r, w):
            P.add(eng, lambda e: e.tensor_tensor(out=out, in0=in0, in1=in1, op=op), r, w)

        def TS(eng, out, in0, s1, s2, op0, op1, r, w):
            if s2 is None:
                P.add(eng, lambda e: e.tensor_scalar(out=out, in0=in0, scalar1=s1, scalar2=None, op0=op0), r, w)
            else:
                P.add(eng, lambda e: e.tensor_scalar(out=out, in0=in0, scalar1=s1, scalar2=s2, op0=op0, op1=op1), r, w)

        def STT(out, in0, scalar, in1, op0, op1, r, w):
            P.add("dve", lambda e: e.scalar_tensor_tensor(out=out, in0=in0, scalar=scalar, in1=in1, op0=op0, op1=op1), r, w)

        def CP(eng, out, in_, r, w):
            if eng == "act":
                P.add("act", lambda e: e.copy(out=out, in_=in_), r, w)
            else:
                P.add(eng, lambda e: e.tensor_copy(out=out, in_=in_), r, w)

        def MSET(eng, ap, val, w):
            P.add(eng, lambda e: e.memset(ap, val), (), w)

        def DMA(queue, out, in_, r, w, chan):
            P.add(queue, lambda e: e.dma_start(out=out, in_=in_), r, w, chan=chan)

        evac_ctr = [0]

        def evac_eng():
            evac_ctr[0] += 1
            return "act" if evac_ctr[0] % 2 else "dve"

        def setup_consts():
            DMA("sp", ident[:, :], ident_d, (), ["ident"], "c_ident")
            DMA("sp", stg[0][:, 0:128], vecs, (), [("stg", 0)], "c_vecs")
            for i in range(1, NSTG):
                MSET("pool", stg[i][:, :], 0.0, [("stg", i)])
            MSET("pool", tpad[:, :], 0.0, ["tpad"])
            MSET("dve", ones_bf[:, :], 1.0, ["ones"])
            MSET("dve", ones_f[:, :], 1.0, ["ones_f"])
            MSET("dve", epsc[:, :], EPS, ["epsc"])
            MSET("dve", onec[:, :], 1.0, ["onec"])
            MSET("dve", hstate[:, :, :], 0.0, ["hstate"])
            b = nb()
            TR(ps[b][:, 0:128], stg[0][:, 0:128], [("stg", 0)], [("ps", b)])
            CP("dve", cvec[:, :], ps[b][:, 0:128], [("ps", b)], ["cvec"])
            ca = COL["b_rg_a"]
            TS("dve", negb[:, :], cvec[:, ca:ca + 8], -1.0, None, ALU.mult, None, ["cvec"], ["negb"])
            cl = COL["rg_lambda"]
            ACT(c8[:, 0:4], cvec[:, cl:cl + 4], AF.Exp, ["cvec"], ["c8"], scale=-1.0)
            ACT(c8[:, 0:4], c8[:, 0:4], AF.Ln, ["c8", "onec"], ["c8"], bias=onec[:, 0:1])
            TS("dve", c8[:, 4:8], c8[:, 0:4], -16.0, None, ALU.mult, None, ["c8"], ["c8b"])
            TS("dve", c8[:, 0:4], c8[:, 0:4], -8.0, None, ALU.mult, None, ["c8", "c8b"], ["c8"])
            DMA("sp", esink[:, :], sinks.partition_broadcast(128), (), ["esink"], "c_sink")
            ACT(esink[:, :], esink[:, :], AF.Exp, ["esink"], ["esink"])
            MSET("pool", esx[:, :, :], 0.0, ["esx"])
            for p0 in (0, 32):
                pr = slice(p0, p0 + 1)
                for kv in range(4):
                    for par in range(2):
                        for j2 in range(2):
                            g = kv * 4 + 2 * j2 + par
                            o = (par * 2 + j2) * 64
                            TS("dve", ug[pr, kv, o:o + 64], ones_f[pr, :], esink[pr, g:g + 1], None, ALU.mult, None,
                               ["esink", "ones_f"], [("ug", kv)])
                ugk = [("ug", kv) for kv in range(4)]
                if p0 == 0:
                    CP("dve", esx[pr, :, :], ug[pr, :, 0:256], ugk, ["esx"])
                else:
                    CP("dve", dbf[pr, :, 0:256], ug[pr, :, 0:256], ugk, [("dbf", 0)])
                    CP("dve", ug[pr, :, 256:512], dbf[pr, :, 0:256], [("dbf", 0)], ugk)
                    TT("dve", ug[pr, :, 256:512], ug[pr, :, 0:256], ug[pr, :, 256:512], ALU.subtract, ugk, ugk)
                    CP("dve", esx[pr, :, :], ug[pr, :, 256:512], ugk, ["esx"])
            DMA("sp", tmp[0][:, 0:512].rearrange("p (g e) -> p g e", g=4), pool_w_d.rearrange("g c e -> c g e"),
                (), [("tmp", 0)], "c_pw")
            CP("dve", poolw[:, :, :], tmp[0][:, 0:512].rearrange("p (g e) -> p g e", g=4), [("tmp", 0)], ["poolw"])
            for wi, wd in enumerate((w_rg_a_d, w_rg_x_d)):
                t = tmp[1 + wi]
                MSET("pool", t[:, 0:512], 0.0, [("tmp", 1 + wi)])
                tv = t[:, 0:512].rearrange("p (c j) -> p c j", c=4)
                src = wd.rearrange("(c r) i j -> r i c j", r=2)
                for r_ in range(2):
                    DMA("sp", tv[r_ * 64:(r_ + 1) * 64, :, r_ * 64:(r_ + 1) * 64], src[r_], (), [("tmp", 1 + wi)],
                        "c_bd%d%d" % (wi, r_))
                CP("dve", wbd[:, wi * 4:(wi + 1) * 4, :], tv, [("tmp", 1 + wi)], ["wbd"])
            for g in range(4):
                win = 2 << g
                for t_ in range(15):
                    MSET("pool", invc[:, g, t_:t_ + 1], 1.0 / min(t_ + 1, win), ["invc"])

        def wsrc(name, j, half):
            def std(Wm, j):
                src = Wm.rearrange("(k p) n -> p k n", p=128)[:, half * 4:(half + 1) * 4, j * 512:(j + 1) * 512]
                return [(lambda s: s.rearrange("p (k n) -> p k n", k=4), src)]
            if name == "in":
                return std(W["w_in"], j)
            if name == "out":
                return std(W["w_out"], j)
            if name == "wo":
                return std(W["w_o"], j)
            if name[:2] in ("mq", "mo", "mk", "mv", "up"):
                l = int(name[-1])
                return std(W["w_" + name[:-1]][l], j)
            if name.startswith("down"):
                l = int(name[-1])
                hh, cb = j // 4, j % 4
                src = W["w_down"][l].rearrange("(k p) n -> p k n", p=128)[
                    :, hh * 16 + half * 8: hh * 16 + half * 8 + 8, cb * 256:(cb + 1) * 256]
                return [(lambda s: s.rearrange("p (k n) -> p k n", k=8), src)]
            if name == "qkv":
                Wr = W["w_qkv"].rearrange("(k p) n -> p k n", p=128)
                if j < 2:
                    return std(W["w_qkv"], j)
                c0 = 1024 if j == 2 else 1280
                res = []
                for kk in range(4):
                    src = Wr[:, half * 4 + kk, c0:c0 + 256].rearrange("p (v d) -> p v d", v=4)
                    for r_ in range(2):
                        res.append((lambda s, r_=r_, kk=kk: s.rearrange("p (k v r d) -> p k v r d", k=4, v=4, r=2)[:, kk, :, r_, :], src))
                return res
            raise KeyError(name)

        def prepass():
            stage = [(x[:, 0:4, :].rearrange("p a b -> p (a b)"), [("x", c) for c in range(4)]),
                     (x[:, 4:8, :].rearrange("p a b -> p (a b)"), [("x", c) for c in range(4, 8)]),
                     (xc[:, :, :].rearrange("p a b -> p (a b)"), [("xc", c) for c in range(4)])]
            n = 0
            for bi, (name, j) in enumerate(all_blocks):
                for half in range(2):
                    sap, skeys = stage[n % 3]
                    for k_, (vf, src) in enumerate(wsrc(name, j, half)):
                        P.add("sp", (lambda e, o=vf(sap), s=src: e.dma_start(out=o, in_=s)), (), skeys,
                              chan="pp_ld%d_%d" % (n % 3, k_))
                    slot = (n // 2) % NW
                    dstv = wring[slot][:, half * 2048:(half + 1) * 2048]
                    eng = "dve" if n % 2 == 0 else "pool"
                    CP(eng, dstv, sap, skeys, [("w", slot, half)])
                    if half == 1:
                        P.add("act", (lambda e, o=wscr[bi], s=wring[slot][:, :]: e.dma_start(out=o, in_=s)),
                              [("w", slot, 0), ("w", slot, 1)], [("wscr", bi)], chan="pp_st%d" % slot)
                    n += 1

        DEBUG_ONDEMAND = int(os.environ.get("KDEBUG_STAGE", "99")) < 99
        MEMMASK = int(os.environ.get("KDEBUG_MEM", "255"))
        wseq = []
        wpos = [0, 0]

        def w_issue_upto(k):
            while wpos[1] < min(k, len(wseq)):
                i = wpos[1]
                b = wseq[i]
                if b is None:
                    break
                slot = i % NW
                if not DEBUG_ONDEMAND and i < CONV_N:
                    name_, j_ = all_blocks[b]
                    for half in range(2):
                        sap = wring[slot][:, half * 2048:(half + 1) * 2048]
                        for k_, (vf, src) in enumerate(wsrc(name_, j_, half)):
                            P.add("pool", (lambda e, o=vf(sap), s_=src: e.dma_start(out=o, in_=s_)), (),
                                  [("w", slot, half)], chan="cw%d_%d_%d" % (slot, half, k_))
                else:
                    DMA("sp", wring[slot][:, :], wscr[b], [("wscr", b)], [("w", slot, 0), ("w", slot, 1)], "w%d" % slot)
                wpos[1] += 1

        CONV_N = len(all_blocks)

        def wnext(name, j):
            i = wpos[0]
            if not DEBUG_ONDEMAND and 0 < i <= CONV_N:
                pslot = (i - 1) % NW
                pb = wseq[i - 1]
                P.add("sp", (lambda e, o=wscr[pb], s_=wring[pslot][:, :]: e.dma_start(out=o, in_=s_)),
                      [("w", pslot, 0), ("w", pslot, 1)], [("wscr", pb)], chan="cst%d" % pslot)
            if DEBUG_ONDEMAND:
                while len(wseq) <= i:
                    wseq.append(None)
                wseq[i] = bid[(name, j)]
                w_issue_upto(i + 1)
            assert wseq[i] == bid[(name, j)], (i, name, j, all_blocks[wseq[i]])
            w_issue_upto(i + NW)
            wpos[0] += 1
            slot = i % NW
            return wring[slot], [("w", slot, 0), ("w", slot, 1)]

        def rmsnorm(src, skey, dst, dkey, gname, T, inplace_out=None, out_keys=None, second=None):
            for c in range(8):
                if c % 2 == 0:
                    ACT(dst(c), src(c), AF.Square, [skey(c)], [dkey(c)])
                else:
                    TT("dve", dst(c), src(c), src(c), ALU.mult, [skey(c)], [dkey(c)])
            b = nb()
            for c in range(8):
                MM(ps[b][:, 0:T], ones_bf[:, :], dst(c), c == 0, c == 7, [dkey(c), "ones"], [("ps", b)])
            t = tmp[6]
            ACT(t[:, 0:T], ps[b][:, 0:T], AF.Ln, [("ps", b), "epsc"], [("tmp", 6)], bias=epsc[:, 0:1], scale=1.0 / D)
            ACT(t[:, 0:T], t[:, 0:T], AF.Exp, [("tmp", 6)], [("tmp", 6)], scale=-0.5)
            g0 = COL[gname]
            for c in range(8):
                if inplace_out is None:
                    STT(dst(c), src(c), cvec[:, g0 + c:g0 + c + 1], t[:, 0:T], ALU.mult, ALU.mult,
                        [skey(c), ("tmp", 6), "cvec"], [dkey(c)])
                else:
                    STT(inplace_out(c), src(c), cvec[:, g0 + c:g0 + c + 1], t[:, 0:T], ALU.mult, ALU.mult,
                        [skey(c), ("tmp", 6), "cvec", dkey(c)], out_keys(c))
            if second is not None:
                dst2, dkey2, gname2 = second
                g2 = COL[gname2]
                for c in range(8):
                    STT(dst2(c), src(c), cvec[:, g2 + c:g2 + c + 1], t[:, 0:T], ALU.mult, ALU.mult,
                        [skey(c), ("tmp", 6), "cvec"], [dkey2(c)])

        def proj_fm(name, nblk, src, skey, T, evac, kc=8, cols=512, kouter=False):
            ncol_chunks = cols // 128
            for j in range(nblk):
                wt, wk = wnext(name, j)
                wv = wt[:, :].rearrange("p (k n) -> p k n", k=kc)
                if j == 0 and kouter:
                    banks = [nb() for _ in range(ncol_chunks)]
                    for k in range(kc):
                        for oc in range(ncol_chunks):
                            MM(ps[banks[oc]][:, 0:T], wv[:, k, oc * 128:(oc + 1) * 128], src(k), k == 0, k == kc - 1,
                               wk + [skey(k)], [("ps", banks[oc])])
                    for oc in range(ncol_chunks):
                        evac(j * ncol_chunks + oc, banks[oc])
                    continue
                for oc in range(ncol_chunks):
                    b = nb()
                    for k in range(kc):
                        MM(ps[b][:, 0:T], wv[:, k, oc * 128:(oc + 1) * 128], src(k), k == 0, k == kc - 1,
                           wk + [skey(k)], [("ps", b)])
                    evac(j * ncol_chunks + oc, b)

        def resid_evac(T):
            def f(oc, b):
                TT("dve", x[:, oc, 0:T], ps[b][:, 0:T], x[:, oc, 0:T], ALU.add, [("ps", b), ("x", oc)], [("x", oc)])
            return f

        def load_T(dstf, dkeyf, src2d, ntok, ncols, queue="sp"):
            for t0 in range(0, ntok, 128):
                n = min(128, ntok - t0)
                for c0 in range(0, ncols, 512):
                    w_ = min(512, ncols - c0)
                    s = nstg()
                    DMA(queue, stg[s][0:n, 0:w_], src2d[t0:t0 + n, c0:c0 + w_], (), [("stg", s)], "stg%d" % s)
                    b = nb()
                    nch = w_ // 128
                    for cc in range(nch):
                        TR(ps[b][:, cc * 128:(cc + 1) * 128], stg[s][:, cc * 128:(cc + 1) * 128], [("stg", s)], [("ps", b)])
                    src = ps[b][:, 0:nch * 128].rearrange("p (c t) -> p c t", c=nch)[:, :, 0:n]
                    CP(evac_eng(), dstf(c0 // 128, nch, t0, n), src, [("ps", b)],
                       [dkeyf(c0 // 128 + cc) for cc in range(nch)])

        def store_T(srcf, skeyf, dst2d, ntok, ncols, pad=False):
            for t0 in range(0, ntok, 128):
                n = min(128, ntok - t0)
                for c0 in range(0, ncols, 512):
                    w_ = min(512, ncols - c0)
                    nch = w_ // 128
                    b = nb()
                    for cc in range(nch):
                        c = c0 // 128 + cc
                        if n == 128:
                            TR(ps[b][:, cc * 128:(cc + 1) * 128], srcf(c, t0, n), [skeyf(c)], [("ps", b)])
                        else:
                            CP("dve", tpad[:, 0:n], srcf(c, t0, n), [skeyf(c)], ["tpad"])
                            TR(ps[b][:, cc * 128:(cc + 1) * 128], tpad[:, :], ["tpad"], [("ps", b)])
                    s = nstg()
                    CP(evac_eng(), stg[s][0:n, 0:w_], ps[b][0:n, 0:w_], [("ps", b)], [("stg", s)])
                    DMA("act", dst2d[t0:t0 + n, c0:c0 + w_], stg[s][0:n, 0:w_], [("stg", s)], [], "stg%d" % s)

        mem_front_done = set()

        def mem_front(bl):
            if bl in mem_front_done or bl >= nseq:
                return
            mem_front_done.add(bl)
            memx = lambda c: xc[:, :, :].rearrange("p a (h t) -> p (a h) t", h=2)[:, c, :]
            memxk = lambda c: ("xc", c // 2)
            mn0 = lambda c: xcb[:, :, :].rearrange("p a (h t) -> p (a h) t", h=2)[:, c, :]
            mn0k = lambda c: ("xcb", c // 2)
            mn1 = lambda c: dbf[:, :, :].rearrange("p a (h t) -> p (a h) t", h=2)[:, c, :]
            mn1k = lambda c: ("dbf", c // 2)
            xv = xc[:, :, :].rearrange("p a (h t) -> p (a h) t", h=2)
            load_T(lambda c0, nch, t0, n: xv[:, c0:c0 + nch, t0:t0 + n], memxk, mem_prompt[bl], NMEM, D)
            rmsnorm(memx, memxk, mn0, mn0k, "norm_mem0", NMEM, second=(mn1, mn1k, "norm_mem1"))

        def mem_phase(bl):
            mem_front(bl)
            for l in range(2 if MEMMASK & 2 else 0):
                if l == 0:
                    mn = lambda c: xcb[:, :, :].rearrange("p a (h t) -> p (a h) t", h=2)[:, c, :]
                    mnk = lambda c: ("xcb", c // 2)
                else:
                    mn = lambda c: dbf[:, :, :].rearrange("p a (h t) -> p (a h) t", h=2)[:, c, :]
                    mnk = lambda c: ("dbf", c // 2)
                if not (MEMMASK & 4):
                    continue
                for j in range(2):
                    wt, wk = wnext("mk%d" % l, j)
                    wv = wt[:, :].rearrange("p (k n) -> p k n", k=8)
                    for oc in range(4):
                        b = nb()
                        for k in range(8):
                            MM(ps[b][:, 0:NMEM], wv[:, k, oc * 128:(oc + 1) * 128], mn(k), k == 0, k == 7,
                               wk + [mnk(k)], [("ps", b)])
                        CP(evac_eng(), memk[l][:, j * 4 + oc, :], ps[b][:, 0:NMEM], [("ps", b)], [("memk", l)])
                    for tb in range(2 if MEMMASK & 8 else 0):
                        b = nb()
                        for k in range(8):
                            MM(ps[b][:, :], mn(k)[:, tb * 128:(tb + 1) * 128], wv[:, k, :], k == 0, k == 7,
                               wk + [mnk(k)], [("ps", b)])
                        s = nstg()
                        CP(evac_eng(), stg[s][:, :], ps[b][:, :], [("ps", b)], [("stg", s)])
                        DMA("act", mem_k_p[l, bl, tb * 128:(tb + 1) * 128, j * 512:(j + 1) * 512], stg[s][:, :],
                            [("stg", s)], [], "stg%d" % s)
                for j in range(2 if MEMMASK & 16 else 0):
                    wt, wk = wnext("mv%d" % l, j)
                    wv = wt[:, :].rearrange("p (k n) -> p k n", k=8)
                    for tb in range(2):
                        b = nb()
                        for k in range(8):
                            MM(ps[b][:, :], mn(k)[:, tb * 128:(tb + 1) * 128], wv[:, k, :], k == 0, k == 7,
                               wk + [mnk(k)], [("ps", b)])
                        s = nstg()
                        CP("act", stg[s][:, :], ps[b][:, :], [("ps", b)], [("stg", s)])
                        CP("dve", memv[l][:, tb, j * 512:(j + 1) * 512], ps[b][:, :], [("ps", b)], [("memv", l)])
                        DMA("act", mem_v_p[l, bl, tb * 128:(tb + 1) * 128, j * 512:(j + 1) * 512], stg[s][:, :],
                            [("stg", s)], [], "stg%d" % s)

        def mem_load_sample(l, bl, slot):
            for tb in range(2):
                for c0 in range(0, D, 512):
                    s = nstg()
                    DMA("sp", stg[s][:, :], cmem_k[l, bl, tb * 128:(tb + 1) * 128, c0:c0 + 512], (), [("stg", s)], "stg%d" % s)
                    b = nb()
                    for cc in range(4):
                        TR(ps[b][:, cc * 128:(cc + 1) * 128], stg[s][:, cc * 128:(cc + 1) * 128], [("stg", s)], [("ps", b)])
                    CP(evac_eng(), memk[slot][:, c0 // 128:c0 // 128 + 4, tb * 128:(tb + 1) * 128],
                       ps[b][:, :].rearrange("p (c t) -> p c t", c=4), [("ps", b)], [("memk", slot)])
                    s = nstg()
                    DMA("sp", stg[s][:, :], cmem_v[l, bl, tb * 128:(tb + 1) * 128, c0:c0 + 512], (), [("stg", s)], "stg%d" % s)
                    CP(evac_eng(), memv[slot][:, tb, c0:c0 + 512], stg[s][:, :], [("stg", s)], [("memv", slot)])

        def cross_attn(l, segs, T, kvslot_of):
            rmsnorm(lambda c: x[:, c, 0:T], lambda c: ("x", c), lambda c: h[:, c, 0:T], lambda c: ("h", c),
                    "norm_cross%d" % l, T)

            def qev(oc, b):
                CP(evac_eng(), q[:, oc, 0:T], ps[b][:, 0:T], [("ps", b)], [("q", oc)])
            proj_fm("mq%d" % l, 2, lambda k: h[:, k, 0:T], lambda k: ("h", k), T, qev, kouter=True)
            pi = 0
            for sg in segs:
                c0, n = sg["c0"], sg["n"]
                slot = kvslot_of(sg)
                for hd in range(4):
                    pt = pT2[pi % 2]
                    ptk = ("pT2", pi % 2)
                    rd = rden[pi % 2]
                    rdk = ("rden", pi % 2)
                    pi += 1
                    for kb in range(2):
                        b = nb()
                        for dc in range(2):
                            MM(ps[b][:, 0:n], memk[slot][:, 2 * hd + dc, kb * 128:(kb + 1) * 128],
                               q[:, 2 * hd + dc, c0:c0 + n], dc == 0, dc == 1,
                               [("memk", slot), ("q", 2 * hd + dc)], [("ps", b)])
                        ACT(pt[:, kb, 0:n], ps[b][:, 0:n], AF.Exp, [("ps", b)], [ptk], scale=1.0 / 16.0)
                    b = nb()
                    for kb in range(2):
                        MM(ps[b][:, 0:n], ones_bf[:, :], pt[:, kb, 0:n], kb == 0, kb == 1, [ptk, "ones"], [("ps", b)])
                    ACT(rd[:, 0:n], ps[b][:, 0:n], AF.Ln, [("ps", b)], [rdk])
                    ACT(rd[:, 0:n], rd[:, 0:n], AF.Exp, [rdk], [rdk], scale=-1.0)
                    for dc in range(2):
                        b = nb()
                        for kb in range(2):
                            MM(ps[b][:, 0:n], memv[slot][:, kb, hd * 256 + dc * 128: hd * 256 + (dc + 1) * 128],
                               pt[:, kb, 0:n], kb == 0, kb == 1, [("memv", slot), ptk], [("ps", b)])
                        TT("dve", ycat[:, 2 * hd + dc, c0:c0 + n], ps[b][:, 0:n], rd[:, 0:n], ALU.mult,
                           [("ps", b), rdk], [("ycat", 2 * hd + dc)])
            proj_fm("mo%d" % l, 2, lambda k: ycat[:, k, 0:T], lambda k: ("ycat", k), T, resid_evac(T), kouter=True)

        def mlp(l, T):
            rmsnorm(lambda c: x[:, c, 0:T], lambda c: ("x", c), lambda c: h[:, c, 0:T], lambda c: ("h", c),
                    "norm_mlp%d" % l, T)
            rr = [0]
            for half in range(2):
                for j in range(4):
                    wt, wk = wnext("up%d" % l, half * 4 + j)
                    wv = wt[:, :].rearrange("p (k n) -> p k n", k=8)
                    kob = None
                    if half == 0 and j == 0:
                        kob = [nb() for _ in range(4)]
                        for k in range(8):
                            for oc in range(4):
                                MM(ps[kob[oc]][:, 0:T], wv[:, k, oc * 128:(oc + 1) * 128], h[:, k, 0:T], k == 0, k == 7,
                                   wk + [("h", k)], [("ps", kob[oc])])
                    for oc in range(4):
                        if kob is not None:
                            b = kob[oc]
                        else:
                            b = nb()
                            for k in range(8):
                                MM(ps[b][:, 0:T], wv[:, k, oc * 128:(oc + 1) * 128], h[:, k, 0:T], k == 0, k == 7,
                                   wk + [("h", k)], [("ps", b)])
                        ti = rr[0] % 4
                        rr[0] += 1
                        t = tmp[ti]
                        ACT(t[:, 0:T], ps[b][:, 0:T], AF.Relu, [("ps", b)], [("tmp", ti)])
                        TT("dve", hid[:, j * 4 + oc, 0:T], t[:, 0:T], t[:, 0:T], ALU.mult, [("tmp", ti)], [("hid", j * 4 + oc)])
                for cb in range(4):
                    wt, wk = wnext("down%d" % l, half * 4 + cb)
                    wv = wt[:, :].rearrange("p (k n) -> p k n", k=16)
                    kob = None
                    if cb == 0:
                        kob = [nb(), nb()]
                        for k in range(16):
                            for cl in range(2):
                                MM(ps[kob[cl]][:, 0:T], wv[:, k, cl * 128:(cl + 1) * 128], hid[:, k, 0:T], k == 0, k == 15,
                                   wk + [("hid", k)], [("ps", kob[cl])])
                    for cl in range(2):
                        oc = cb * 2 + cl
                        if kob is not None:
                            b = kob[cl]
                        else:
                            b = nb()
                            for k in range(16):
                                MM(ps[b][:, 0:T], wv[:, k, cl * 128:(cl + 1) * 128], hid[:, k, 0:T], k == 0, k == 15,
                                   wk + [("hid", k)], [("ps", b)])
                        TT("dve", x[:, oc, 0:T], ps[b][:, 0:T], x[:, oc, 0:T], ALU.add, [("ps", b), ("x", oc)], [("x", oc)])

        def even_layer(segs, T, sample):
            rmsnorm(lambda c: x[:, c, 0:T], lambda c: ("x", c), lambda c: h[:, c, 0:T], lambda c: ("h", c),
                    "norm_mix0", T)
            hs = hstate_s if sample else hstate
            hsk = "hstate_s" if sample else "hstate"
            def hidf(i):
                return hid[:, 2 * i:2 * i + 2, :].bitcast(F32).rearrange("p a b -> p (a b)"), [("hid", 2 * i), ("hid", 2 * i + 1)]
            def qf(i):
                return q[:, 2 * i:2 * i + 2, :].bitcast(F32).rearrange("p a b -> p (a b)"), [("q", 2 * i), ("q", 2 * i + 1)]
            TA = [hidf(c) for c in range(4)]
            TB = [hidf(4 + c) for c in range(4)]
            TC = [qf(c) for c in range(4)]
            TG = [(tmp[i][:, 0:TP], [("tmp", i)]) for i in (0, 1, 2, 6)]
            R4 = range(4)

            def in_block(j):
                wt, wk = wnext("in", j)
                wv = wt[:, :].rearrange("p (k n) -> p k n", k=8)
                kob = None
                if j == 1:
                    kob = [nb() for _ in range(4)]
                    for k in range(8):
                        for oc4 in range(4):
                            MM(ps[kob[oc4]][:, 0:T], wv[:, k, oc4 * 128:(oc4 + 1) * 128], h[:, k, 0:T], k == 0, k == 7,
                               wk + [("h", k)], [("ps", kob[oc4])])
                for oc4 in range(4):
                    if kob is not None:
                        b = kob[oc4]
                    else:
                        b = nb()
                        for k in range(8):
                            MM(ps[b][:, 0:T], wv[:, k, oc4 * 128:(oc4 + 1) * 128], h[:, k, 0:T], k == 0, k == 7,
                               wk + [("h", k)], [("ps", b)])
                    if j == 0:
                        for sg in segs:
                            CP("act", up[:, oc4, sg["ucol"]:sg["ucol"] + sg["n"]], ps[b][:, sg["c0"]:sg["c0"] + sg["n"]],
                               [("ps", b)], [("up", oc4)])
                    elif j == 1:
                        for sg in segs:
                            CP("act", ux[:, oc4, sg["xcol"]:sg["xcol"] + sg["n"]], ps[b][:, sg["c0"]:sg["c0"] + sg["n"]],
                               [("ps", b)], [("ux", oc4)])
                    else:
                        CP("act", ug[:, oc4, 0:T], ps[b][:, 0:T], [("ps", b)], [("ug", oc4)])

            in_block(1)
            for c in R4:
                cw = [COL["conv_w%d" % k] + c for k in range(4)]
                cbc = COL["conv_b"] + c
                for sg in segs:
                    xo, n, c0 = sg["xcol"], sg["n"], sg["c0"]
                    TS("dve", xc[:, c, c0:c0 + n], ux[:, c, xo:xo + n], cvec[:, cw[3]:cw[3] + 1], cvec[:, cbc:cbc + 1],
                       ALU.mult, ALU.add, [("ux", c), "cvec"], [("xc", c)])
                    for k in (2, 1, 0):
                        sh = 3 - k
                        STT(xc[:, c, c0:c0 + n], ux[:, c, xo - sh:xo - sh + n], cvec[:, cw[k]:cw[k] + 1], xc[:, c, c0:c0 + n],
                            ALU.mult, ALU.add, [("ux", c), ("uxh", c), "cvec", ("xc", c)], [("xc", c)])
                CP("act", xcb[:, c, 0:T], xc[:, c, 0:T], [("xc", c)], [("xcb", c)])
            in_block(0)
            flush_pending()
            for g in range(4):
                Wn = 2 << g
                for sg in segs:
                    uc, n, c0 = sg["ucol"], sg["n"], sg["c0"]
                    cur = lambda lo, hi, g=g, uc=uc: up[:, g, uc + lo:uc + hi]
                    curk = [("up", g), ("uph", g)]
                    for lev in range(1, g + 2):
                        sh = 1 << (lev - 1)
                        lo = -(Wn - (1 << lev))
                        ti = 4 + (lev % 2)
                        o = tmp[ti]
                        TT("dve", o[:, 16 + lo:16 + n], cur(lo, n), cur(lo - sh, n - sh), ALU.add, curk, [("tmp", ti)])
                        cur = lambda lo_, hi_, o=o: o[:, 16 + lo_:16 + hi_]
                        curk = [("tmp", ti)]
                    STT(dbf[:, g, c0:c0 + n], cur(0, n), 1.0 / Wn, up[:, g, uc:uc + n], ALU.mult, ALU.subtract,
                        curk + [("up", g)], [("dbf", g)])
                    if sg["first"] and not sample:
                        m = Wn - 1
                        t6 = tmp[3]
                        TT("dve", t6[:, 0:m], cur(0, m), invc[:, g, 0:m], ALU.mult, curk + ["invc"], [("tmp", 3)])
                        TT("dve", dbf[:, g, c0:c0 + m], t6[:, 0:m], up[:, g, uc:uc + m], ALU.subtract,
                           [("tmp", 3), ("up", g)], [("dbf", g)])
            in_block(2)
            for c in R4:
                G, gk = TG[c]
                u_ = ug[:, c, 0:T]
                ACT(G[:, 0:T], u_, AF.Square, [("ug", c)], gk)
                ACT(G[:, 0:T], G[:, 0:T], AF.Identity, gk + ["onec"], gk, bias=onec[:, 0:1], scale=0.044715)
                TT("dve", G[:, 0:T], G[:, 0:T], u_, ALU.mult, gk + [("ug", c)], gk)
            ba = COL["b_rg_a"]
            bx = COL["b_rg_x"]
            for c in R4:
                A, ak = TA[c]
                b_ = nb()
                MM(ps[b_][:, 0:T], wbd[:, c, :], xcb[:, c, 0:T], True, True, ["wbd", ("xcb", c)], [("ps", b_)])
                ACT(A[:, 0:T], ps[b_][:, 0:T], AF.Sigmoid, [("ps", b_), "cvec"], ak, bias=cvec[:, ba + c:ba + c + 1])
            for c in R4:
                Bt, bk = TB[c]
                b_ = nb()
                MM(ps[b_][:, 0:T], wbd[:, 4 + c, :], xcb[:, c, 0:T], True, True, ["wbd", ("xcb", c)], [("ps", b_)])
                ACT(Bt[:, 0:T], ps[b_][:, 0:T], AF.Sigmoid, [("ps", b_), "cvec"], bk, bias=cvec[:, bx + c:bx + c + 1])
            for c in R4:
                G, gk = TG[c]
                ACT(G[:, 0:T], G[:, 0:T], AF.Sigmoid, gk, gk, scale=2.0 * GELU_K)
            for g in range(4):
                b = nb()
                MM(ps[b][:, 0:T], poolw[:, g, :], dbf[:, g, 0:T], True, True, ["poolw", ("dbf", g)], [("ps", b)])
                pc = COL["pool_scale"] + g
                TS("dve", ycat[:, g, 0:T], ps[b][:, 0:T], cvec[:, pc:pc + 1], None, ALU.mult, None, [("ps", b), "cvec"], [("ycat", g)])
            for pair in ((0, 1), (2, 3)):
                for c in pair:
                    A, ak = TA[c]
                    C, ck = TC[c]
                    ACT(C[:, 0:T], A[:, 0:T], AF.Exp, ak + ["c8b"], ck, scale=c8[:, 4 + c:5 + c])
                for c in pair:
                    A, ak = TA[c]
                    ACT(A[:, 0:T], A[:, 0:T], AF.Exp, ak + ["c8"], ak, scale=c8[:, c:c + 1])
                for c in pair:
                    C, ck = TC[c]
                    ACT(C[:, 0:T], C[:, 0:T], AF.Ln, ck + ["onec"], ck, bias=onec[:, 0:1], scale=-1.0)
                for c in pair:
                    C, ck = TC[c]
                    Bt, bk = TB[c]
                    ACT(C[:, 0:T], C[:, 0:T], AF.Exp, ck, ck, scale=0.5)
                    TT("dve", Bt[:, 0:T], Bt[:, 0:T], C[:, 0:T], ALU.mult, bk + ck, bk)
                    TT("dve", Bt[:, 0:T], Bt[:, 0:T], xc[:, c, 0:T], ALU.mult, bk + [("xc", c)], bk)
                for c in pair:
                    A, ak = TA[c]
                    Bt, bk = TB[c]
                    C, ck = TC[c]
                    G, gk = TG[c]
                    for sg in segs:
                        n, c0, bl = sg["n"], sg["c0"], sg["bl"]
                        P.add("dve", (lambda e, o=C[:, c0:c0 + n], a_=A[:, c0:c0 + n], bb=Bt[:, c0:c0 + n],
                                      ini=hs[:, c, bl:bl + 1]:
                                      e.tensor_tensor_scan(out=o, data0=a_, data1=bb, initial=ini, op0=ALU.mult, op1=ALU.add)),
                              ak + bk + [hsk] + ck, ck)
                        CP("pool", hs[:, c, bl:bl + 1], C[:, c0 + n - 1:c0 + n], ck, [hsk])
                    TT("dve", G[:, 0:T], G[:, 0:T], ug[:, c, 0:T], ALU.mult, gk + [("ug", c)], gk)
                    TT("dve", ycat[:, 4 + c, 0:T], C[:, 0:T], G[:, 0:T], ALU.mult, ck + gk, [("ycat", 4 + c)])

            for sg in segs:
                uc, xo, n, bl = sg["ucol"], sg["xcol"], sg["n"], sg["bl"]
                if sg["last"]:
                    pd = pool_s if sample else pool_p
                    cd = conv_s if sample else conv_p
                    store_T(lambda c, t0, nn: up[:, c, uc + n - 15:uc + n], lambda c: ("up", c), pd[bl], 15, 512)
                    store_T(lambda c, t0, nn: ux[:, c, xo + n - 3:xo + n], lambda c: ("ux", c), cd[bl], 3, 512)
                else:
                    for g in range(4):
                        CP("pool", up[:, g, uc - 15:uc], up[:, g, uc + n - 15:uc + n], [("up", g)], [("uph", g)])
                        CP("pool", ux[:, g, xo - 3:xo], ux[:, g, xo + n - 3:xo + n], [("ux", g)], [("uxh", g)])
            proj_fm("out", 2, lambda k: ycat[:, k, 0:T], lambda k: ("ycat", k), T, resid_evac(T), kouter=True)

        def odd_layer(segs, T, sample):
            rmsnorm(lambda c: x[:, c, 0:T], lambda c: ("x", c), lambda c: h[:, c, 0:T], lambda c: ("h", c),
                    "norm_mix1", T)

            def qev(oc, b):
                CP(evac_eng(), q[:, oc, 0:T], ps[b][:, 0:T], [("ps", b)], [("q", oc)])
            proj_fm("qkv", 2, lambda k: h[:, k, 0:T], lambda k: ("h", k), T, qev, kouter=True)
            wt, wk = wnext("qkv", 2)
            wv = wt[:, :].rearrange("p (k n) -> p k n", k=8)
            for kv in range(4):
                b = nb()
                for k in range(8):
                    MM(ps[b][:, 0:T], wv[:, k, kv * 128:(kv + 1) * 128], h[:, k, 0:T], k == 0, k == 7,
                       wk + [("h", k)], [("ps", b)])
                for sg in segs:
                    kc0 = sg["kb0"] + 128
                    CP(evac_eng(), kbuf[:, kv, kc0:kc0 + sg["n"]], ps[b][:, sg["c0"]:sg["c0"] + sg["n"]],
                       [("ps", b)], [("kbuf", kv)])
            wv5 = wt[:, :].rearrange("p (k v r d) -> p k v r d", k=8, v=4, r=2)
            for sg in segs:
                if not sg["last"]:
                    continue
                bl, n, c0 = sg["bl"], sg["n"], sg["c0"]
                nrow = min(128, n)
                t0 = c0 + n - nrow
                b = nb()
                for k in range(8):
                    MM(ps[b][0:nrow, 0:256].rearrange("p (v d) -> p v d", v=4), h[:, k, t0:t0 + nrow], wv5[:, k, :, 0, :],
                       k == 0, k == 7, wk + [("h", k)], [("ps", b)])
                s = nstg()
                CP(evac_eng(), stg[s][0:nrow, 0:256], ps[b][0:nrow, 0:256], [("ps", b)], [("stg", s)])
                kd = swa_k_s if sample else swa_k_p
                DMA("act", kd[bl, 128 - nrow:128, :], stg[s][0:nrow, 0:256], [("stg", s)], [], "stg%d" % s)
                if sample:
                    DMA("sp", swa_k_s[bl, 0:64, :], cache_k[bl, 64:128, :], (), [], "d2d")
            wt, wk = wnext("qkv", 3)
            wv = wt[:, :].rearrange("p (k n) -> p k n", k=8)
            for sg in segs:
                n, c0, bl = sg["n"], sg["c0"], sg["bl"]
                for j in range(n // 64):
                    b = nb()
                    for k in range(8):
                        MM(ps[b][0:64, :], h[:, k, c0 + j * 64:c0 + (j + 1) * 64], wv[:, k, :], k == 0, k == 7,
                           wk + [("h", k)], [("ps", b)])
                    vs = sg["vb0"] + 2 + j
                    CP(evac_eng(), vtok[0:64, vs, :], ps[b][0:64, :], [("ps", b)], [("vtok", vs)])
                    if sg["last"] and j >= n // 64 - 2:
                        s = nstg()
                        CP(evac_eng(), stg[s][0:64, 0:256].rearrange("p (v d) -> p v d", v=4),
                           ps[b][0:64, :].rearrange("p (v r d) -> p v r d", v=4, r=2)[:, :, 0, :], [("ps", b)], [("stg", s)])
                        vd = swa_v_s if sample else swa_v_p
                        row0 = 128 - (n // 64 - j) * 64
                        DMA("act", vd[bl, row0:row0 + 64, :], stg[s][0:64, 0:256], [("stg", s)], [], "stg%d" % s)
                if sample:
                    DMA("sp", swa_v_s[bl, 0:64, :], cache_v[bl, 64:128, :], (), [], "d2d")
            units = []
            for sg in segs:
                for nq in range(sg["n"] // 64):
                    for kv in range(4):
                        units.append((sg, nq, kv))

            def unit_info(ui):
                sg, nq, kv = units[ui]
                ext = [e_ for e_ in (nq, nq + 1, nq + 2) if (e_ >= 2 or sg["hist_valid"])]
                return sg, nq, kv, ext, sg["c0"] + nq * 64, pTs[ui % 2], ("pTs", ui % 2), dns[ui % 2], ("dns", ui % 2)

            def swa_scores(ui):
                sg, nq, kv, ext, qc0, pt, ptk, dn, dnk = unit_info(ui)
                ne = len(ext)
                for par in range(2):
                    b = nb()
                    for ji, e_ in enumerate(ext):
                        kcol = sg["kb0"] + e_ * 64
                        MM(ps[b][0:64, ji * 128:(ji + 1) * 128].rearrange("p (a l) -> p a l", a=2),
                           kbuf[par * 64:(par + 1) * 64, kv, kcol:kcol + 64],
                           q[par * 64:(par + 1) * 64, 2 * kv:2 * kv + 2, qc0:qc0 + 64], True, True,
                           [("kbuf", kv), ("kbufh", kv), ("q", 2 * kv), ("q", 2 * kv + 1)], [("ps", b)])
                    ACT(pt[0:64, 0:ne, par * 128:(par + 1) * 128],
                        ps[b][0:64, 0:ne * 128].rearrange("p (j c) -> p j c", j=ne), AF.Exp, [("ps", b)], [ptk], scale=0.125)

            def swa_pv(ui):
                sg, nq, kv, ext, qc0, pt, ptk, dn, dnk = unit_info(ui)
                bd = nb()
                MM(ps[bd][:, 0:256], ones_bf[0:64, :], esx[0:64, kv, :], True, False, ["ones", "esx"], [("ps", bd)])
                for ji in range(len(ext)):
                    MM(ps[bd][:, 0:256], ones_bf[0:64, :], pt[0:64, ji, :], False, ji == len(ext) - 1,
                       ["ones", ptk], [("ps", bd)])
                ACT(dn[:, :], ps[bd][:, 0:256], AF.Ln, [("ps", bd)], [dnk])
                ACT(dn[:, :], dn[:, :], AF.Exp, [dnk], [dnk], scale=-1.0)
                bo = nb()
                for ji, e_ in enumerate(ext):
                    vs = sg["vb0"] + e_
                    MM(ps[bo][:, 0:256], vtok[0:64, vs, kv * 128:(kv + 1) * 128], pt[0:64, ji, :], ji == 0,
                       ji == len(ext) - 1, [("vtok", vs), ptk], [("ps", bo)])
                for par in range(2):
                    sl = slice(par * 64, (par + 1) * 64)
                    TT("dve", ycat[sl, 2 * kv:2 * kv + 2, qc0:qc0 + 64],
                       ps[bo][sl, par * 128:(par + 1) * 128].rearrange("p (a l) -> p a l", a=2),
                       dn[sl, par * 128:(par + 1) * 128].rearrange("p (a l) -> p a l", a=2), ALU.mult,
                       [("ps", bo), dnk], [("ycat", 2 * kv), ("ycat", 2 * kv + 1)])

            swa_scores(0)
            for ui in range(len(units)):
                if ui + 1 < len(units):
                    swa_scores(ui + 1)
                swa_pv(ui)
            for sg in segs:
                if sample or sg["last"]:
                    continue
                kb0, n, vb0 = sg["kb0"], sg["n"], sg["vb0"]
                for kv in range(4):
                    CP("pool", kbuf[:, kv, kb0:kb0 + 128], kbuf[:, kv, kb0 + n:kb0 + n + 128], [("kbuf", kv)], [("kbufh", kv)])
                for j in range(2):
                    CP("pool", vtok[0:64, vb0 + j, :], vtok[0:64, vb0 + n // 64 + j, :], [("vtok", vb0 + n // 64 + j)],
                       [("vtok", vb0 + j)])
            proj_fm("wo", 2, lambda k: ycat[:, k, 0:T], lambda k: ("ycat", k), T, resid_evac(T), kouter=True)

        pending = []

        def flush_pending():
            while pending:
                pending.pop(0)()

        def final_norm_store(T, dst2d):
            yb = lambda c: hid[:, 2 * c:2 * c + 2, :].bitcast(F32).rearrange("p a b -> p (a b)")
            ybk = lambda c: [("hid", 2 * c), ("hid", 2 * c + 1)]
            rmsnorm(lambda c: x[:, c, 0:T], lambda c: ("x", c), lambda c: h[:, c, 0:T], lambda c: ("h", c),
                    "norm_final", T, inplace_out=lambda c: yb(c)[:, 0:T], out_keys=ybk)

            def do_store():
                for t0 in range(0, T, 128):
                    for c0 in range(0, D, 512):
                        b = nb()
                        for cc in range(4):
                            c = c0 // 128 + cc
                            TR(ps[b][:, cc * 128:(cc + 1) * 128], yb(c)[:, t0:t0 + 128], ybk(c), [("ps", b)])
                        s_ = nstg()
                        CP(evac_eng(), stg[s_][:, :], ps[b][:, :], [("ps", b)], [("stg", s_)])
                        DMA("act", dst2d[t0:t0 + 128, c0:c0 + 512], stg[s_][:, :], [("stg", s_)], [], "stg%d" % s_)
            pending.append(do_store)

        def run_tile(segs, T, sample, src2d, dst2d, kvslot0, kvslot1, next_mem=None):
            load_T(lambda c0, nch, t0, n: x[:, c0:c0 + nch, t0:t0 + n], lambda c: ("x", c), src2d, T, D)
            even_layer(segs, T, sample)
            cross_attn(0, segs, T, kvslot0)
            mlp(0, T)
            odd_layer(segs, T, sample)
            cross_attn(1, segs, T, kvslot1)
            if next_mem is not None and STAGE >= 99:
                mem_front(next_mem)
            mlp(1, T)
            final_norm_store(T, dst2d)

        tb_ids = [bid[b] for b in tile_blocks]
        mb_ids = [bid[b] for b in mem_blocks]
        if do_sample and not DEBUG_ONDEMAND:
            wseq.extend(tb_ids)
        for s_ in range(nseq if not DEBUG_ONDEMAND else 0):
            wseq.extend(mb_ids)
            for _ in range(SEQ // TP):
                wseq.extend(tb_ids)

        STAGE = int(os.environ.get("KDEBUG_STAGE", "99"))
        NTI = int(os.environ.get("KDEBUG_NTILE", str(SEQ // TP)))
        if not DEBUG_ONDEMAND:
            w_issue_upto(NW)
        setup_consts()
        if DEBUG_ONDEMAND and STAGE >= 1:
            prepass()
        if not DEBUG_ONDEMAND:
            assert sorted(wseq[:CONV_N]) == list(range(CONV_N))
        if do_sample and STAGE >= 4:
            T = NB * DEC
            segs = []
            for bl in range(NB):
                segs.append(dict(bl=bl, c0=bl * DEC, n=DEC, ucol=bl * 80 + 16, xcol=bl * 68 + 4, kb0=bl * 192, vb0=bl * 3,
                                 first=True, last=True, hist_valid=True))
            for sg in segs:
                bl = sg["bl"]
                uc, xo = sg["ucol"], sg["xcol"]
                load_T(lambda c0, nch, t0, n, uc=uc: up[:, c0:c0 + nch, uc - 15:uc], lambda c: ("uph", c), state_pool[bl], 15, 512)
                load_T(lambda c0, nch, t0, n, xo=xo: ux[:, c0:c0 + nch, xo - 3:xo], lambda c: ("uxh", c), state_conv[bl], 3, 512)
                s = nstg()
                DMA("sp", stg[s][:, 0:256], cache_k[bl], (), [("stg", s)], "stg%d" % s)
                s2 = nstg()
                for r_ in range(2):
                    CP("pool", stg[s2][:, :].rearrange("p (v r d) -> p v r d", v=4, r=2)[:, :, r_, :],
                       stg[s][:, 0:256].rearrange("p (v d) -> p v d", v=4), [("stg", s)], [("stg", s2)])
                b = nb()
                for kv in range(4):
                    TR(ps[b][:, kv * 128:(kv + 1) * 128], stg[s2][:, kv * 128:(kv + 1) * 128], [("stg", s2)], [("ps", b)])
                CP(evac_eng(), kbuf[:, :, sg["kb0"]:sg["kb0"] + 128], ps[b][:, :].rearrange("p (v t) -> p v t", v=4),
                   [("ps", b)], [("kbufh", kv_) for kv_ in range(4)])
                for j in range(2):
                    s = nstg()
                    DMA("sp", stg[s][0:64, 0:256], cache_v[bl, j * 64:(j + 1) * 64, :], (), [("stg", s)], "stg%d" % s)
                    for r_ in range(2):
                        CP("pool", vtok[0:64, sg["vb0"] + j, :].rearrange("p (v r d) -> p v r d", v=4, r=2)[:, :, r_, :],
                           stg[s][0:64, 0:256].rearrange("p (v d) -> p v d", v=4), [("stg", s)], [("vtok", sg["vb0"] + j)])
            load_T(lambda c0, nch, t0, n: hstate_s[:, c0:c0 + nch, 0:4], lambda c: "hstate_s", state_lru, 4, 512)

            kvctr = [0]

            def mk_kvslot(l):
                def f(sg):
                    slot = kvctr[0] % 2
                    kvctr[0] += 1
                    mem_load_sample(l, sg["bl"], slot)
                    return slot
                return f
            run_tile(segs, T, True, x_sample, y_sample, mk_kvslot(0), mk_kvslot(1), next_mem=0)
            store_T(lambda c, t0, n: hstate_s[:, c, 0:4], lambda c: "hstate_s", lru_s, 4, 512)

        for bl in range(nseq if STAGE >= 2 else 0):
            mem_phase(bl)
            for ti in range(NTI if STAGE >= 3 else 0):
                seg = dict(bl=bl, c0=0, n=TP, ucol=16, xcol=4, kb0=0, vb0=0, first=(ti == 0), last=(ti == SEQ // TP - 1),
                           hist_valid=(ti != 0))
                if ti == 0:
                    for g in range(4):
                        MSET("pool", up[:, g, 0:16], 0.0, [("uph", g)])
                        MSET("pool", ux[:, g, 0:4], 0.0, [("uxh", g)])
                run_tile([seg], TP, False, x_prompt[bl, ti * TP:(ti + 1) * TP, :], y_prompt[bl, ti * TP:(ti + 1) * TP, :],
                         lambda sg: 0, lambda sg: 1, next_mem=(bl + 1 if ti == SEQ // TP - 1 else None))
            if bl == nseq - 1:
                store_T(lambda c, t0, n: hstate[:, c, 0:4], lambda c: "hstate", lru_p, 4, 512)
        flush_pending()
        assert STAGE < 99 or wpos[0] == len(wseq), (wpos, len(wseq))

        chans = sorted(P.chan_n.keys())
        sems = {e_: es.enter_context(nc.semaphore("s_" + e_)) for e_ in COMPUTE}
        chan_sems = {c_: es.enter_context(nc.semaphore("d_" + c_)) for c_ in chans}
        block = es.enter_context(nc.Block())
        P.emit(nc, block, sems, chan_sems)
    return nc, len(P.ops)


_CACHE = {}


def _stack_vecs(inp):
    rows = []
    for nm in ["norm_mix", "norm_cross", "norm_mem", "norm_mlp"]:
        for l in range(2):
            rows.append(np.asarray(inp[nm][l]).reshape(8, 128))
    rows.append(np.asarray(inp["norm_final"]).reshape(8, 128))
    cw = np.asarray(inp["conv_w"])[0]
    for k in range(4):
        rows.append(cw[k].reshape(4, 128))
    for nm in ["conv_b", "b_rg_a", "b_rg_x", "rg_lambda", "pool_scale"]:
        rows.append(np.asarray(inp[nm])[0].reshape(4, 128))
    v = np.concatenate(rows, axis=0).astype(np.float32)
    out = np.zeros((128, 128), np.float32)
    out[:v.shape[0]] = v
    return out


def kernel(**inp):
    nseq = int(os.environ.get("KDEBUG_NSEQ", NB))
    do_sample = os.environ.get("KDEBUG_NOSAMPLE", "0") != "1"
    key = (nseq, do_sample)
    if key not in _CACHE:
        _CACHE[key] = build_program(nseq, do_sample)[0]
    nc = _CACHE[key]
    f = lambda a: np.ascontiguousarray(np.asarray(a, dtype=np.float32))
    shared = dict(
        vecs=_stack_vecs(inp), attn_sinks=f(inp["attn_sinks"]).reshape(1, 16), ident=np.eye(128, dtype=np.float32),
        w_in=f(inp["w_in_even"][0]), w_out=f(inp["w_out_even"][0]), w_qkv=f(inp["w_qkv_odd"][0]), w_o=f(inp["w_o_odd"][0]),
        w_mq=f(inp["w_mq"]), w_mk=f(inp["w_mk"]), w_mv=f(inp["w_mv"]), w_mo=f(inp["w_mo"]), w_up=f(inp["w_up"]),
        w_down=f(inp["w_down"]), pool_w=f(inp["pool_w"][0]), w_rg_a=f(inp["w_rg_a"][0]), w_rg_x=f(inp["w_rg_x"][0]),
    )
    in_maps = []
    for i in range(NCORE):
        sl = slice(i * NB, (i + 1) * NB)
        m = dict(shared)
        m.update(
            x_prompt=f(inp["x_prompt"][sl]), x_sample=f(inp["x_sample"][sl]).reshape(NB * DEC, D),
            state_pool=f(inp["state_pool"][0, sl]), state_conv=f(inp["state_conv"][0, sl]), state_lru=f(inp["state_lru"][0, sl]),
            cache_swa_k=f(inp["cache_swa_k"][0, sl]).reshape(NB, 128, 256), cache_swa_v=f(inp["cache_swa_v"][0, sl]).reshape(NB, 128, 256),
            cache_mem_k=f(inp["cache_mem_k"][:, sl]).reshape(2, NB, NMEM, D), cache_mem_v=f(inp["cache_mem_v"][:, sl]).reshape(2, NB, NMEM, D),
            mem_prompt=f(inp["mem_prompt"][sl]),
        )
        in_maps.append(m)
    res = run_bass_kernel_spmd(nc, in_maps, core_ids=list(range(NCORE)))
    R = res.results
    cat = lambda k, ax=0: np.concatenate([np.asarray(r[k]) for r in R], axis=ax)
    B = NCORE * NB
    y_prompt = cat("y_prompt")
    y_sample = cat("y_sample").reshape(B, DEC, D)
    pool_p = cat("pool_p")[None]
    conv_p = cat("conv_p")[None]
    lru_p = cat("lru_p")[None]
    swa_k_p = cat("swa_k_p").reshape(1, B, 128, 4, 64)
    swa_v_p = cat("swa_v_p").reshape(1, B, 128, 4, 64)
    mem_k_p = cat("mem_k_p", 1).reshape(2, B, NMEM, 4, 256)
    mem_v_p = cat("mem_v_p", 1).reshape(2, B, NMEM, 4, 256)
    pool_s = cat("pool_s")[None]
    conv_s = cat("conv_s")[None]
    lru_s = cat("lru_s")[None]
    swa_k_s = cat("swa_k_s").reshape(1, B, 128, 4, 64)
    swa_v_s = cat("swa_v_s").reshape(1, B, 128, 4, 64)
    return (y_prompt, y_sample, pool_p, conv_p, lru_p, swa_k_p, swa_v_p, mem_k_p, mem_v_p,
            pool_s, conv_s, lru_s, swa_k_s, swa_v_s)
```

```python
import os
import numpy as np
import concourse.bass as bass
import concourse.mybir as mybir
from concourse.bass_utils import run_bass_kernel_spmd

F32 = mybir.dt.float32
BF16 = mybir.dt.bfloat16
AF = mybir.ActivationFunctionType
ALU = mybir.AluOpType

NCORE = 8
D = 1024
KC = 8
TP = 512
SEQ = 2048
NB = 4
DEC = 64
NMEM = 256
EPS = 1e-6
NW = 4
GELU_K = 0.7978845608028654

COMPUTE = ("pe", "act", "dve", "pool")


class Prog:
    def __init__(self):
        self.ops = []
        self.lastw = {}
        self.readers = {}
        self.chan_n = {}

    def add(self, eng, fn, r=(), w=(), chan=None):
        i = len(self.ops)
        psr = [k for k in r if isinstance(k, tuple) and k[0] == "ps"]
        if psr:
            r = [k for k in r if not (isinstance(k, tuple) and k[0] == "ps")]
            w = list(w) + psr
        deps = set()
        raw = set()
        for k in r:
            j = self.lastw.get(k)
            if j is not None:
                deps.add(j)
                raw.add(j)
        for k in w:
            j = self.lastw.get(k)
            if j is not None:
                deps.add(j)
            for j in self.readers.get(k, ()):
                deps.add(j)
        deps.discard(i)
        cnt = None
        if chan is not None:
            cnt = self.chan_n.get(chan, 0) + 1
            self.chan_n[chan] = cnt
        self.ops.append(dict(eng=eng, fn=fn, deps=deps, raw=raw, chan=chan, cnt=cnt))
        for k in w:
            self.lastw[k] = i
            self.readers[k] = []
        for k in r:
            lst = self.readers.setdefault(k, [])
            if chan is None:
                lst[:] = [j for j in lst if not (self.ops[j]["chan"] is None and self.ops[j]["eng"] == eng)]
            lst.append(i)
        return i

    def emit(self, nc, block, sems, chan_sems):
        ops = self.ops
        for i, op in enumerate(ops):
            best = {}
            dmas = []
            for j in op["deps"]:
                d = ops[j]
                if d["chan"] is not None:
                    dmas.append(j)
                    continue
                if d["eng"] == op["eng"] and op["chan"] is None and op["eng"] == "pe":
                    continue
                if d["eng"] == op["eng"] and op["chan"] is not None:
                    pass
                if j > best.get(d["eng"], -1):
                    best[d["eng"]] = j
            op["wait_c"] = best
            op["wait_d"] = dmas
        sig = [False] * len(ops)
        for op in ops:
            for j in op["wait_c"].values():
                sig[j] = True
        counts = {e: 0 for e in COMPUTE}
        for i, op in enumerate(ops):
            if op["chan"] is None and sig[i]:
                counts[op["eng"]] += 1
                op["sigval"] = counts[op["eng"]]
        streams = {}
        for i, op in enumerate(ops):
            streams.setdefault(op["eng"], []).append(i)

        def run_stream(name, e):
            waited = {}
            for i in streams.get(name, []):
                op = ops[i]
                for eng2, j in op["wait_c"].items():
                    v = ops[j]["sigval"]
                    key = ("c", eng2)
                    if waited.get(key, 0) < v:
                        e.wait_ge(sems[eng2], v)
                        waited[key] = v
                for j in op["wait_d"]:
                    d = ops[j]
                    v = 16 * d["cnt"]
                    key = ("d", d["chan"])
                    if waited.get(key, 0) < v:
                        e.wait_ge(chan_sems[d["chan"]], v)
                        waited[key] = v
                if op["chan"] is not None:
                    v = 16 * (op["cnt"] - 1)
                    key = ("d", op["chan"])
                    if v > 0 and waited.get(key, 0) < v:
                        e.wait_ge(chan_sems[op["chan"]], v)
                        waited[key] = v
                ins = op["fn"](e)
                if op["chan"] is not None:
                    ins.then_inc(chan_sems[op["chan"]], 16)
                elif sig[i]:
                    ins.then_inc(sems[op["eng"]], 1)
            if name in ("sp", "act", "pool"):
                final = {}
                for i in streams.get(name, []):
                    op = ops[i]
                    if op["chan"] is not None:
                        final[op["chan"]] = max(final.get(op["chan"], 0), 16 * op["cnt"])
                for ch, v in final.items():
                    if waited.get(("d", ch), 0) < v:
                        e.wait_ge(chan_sems[ch], v)

        @block.sync
        def _(e):
            run_stream("sp", e)

        @block.tensor
        def _(e):
            run_stream("pe", e)

        @block.scalar
        def _(e):
            run_stream("act", e)

        @block.vector
        def _(e):
            run_stream("dve", e)

        @block.gpsimd
        def _(e):
            run_stream("pool", e)


def block_catalogue():
    blocks = []
    for l in range(2):
        if l == 0:
            blocks += [("in", 1), ("in", 0), ("in", 2)] + [("out", j) for j in range(2)]
        else:
            blocks += [("qkv", j) for j in range(4)] + [("wo", j) for j in range(2)]
        blocks += [("mq%d" % l, j) for j in range(2)] + [("mo%d" % l, j) for j in range(2)]
        for half in range(2):
            blocks += [("up%d" % l, half * 4 + j) for j in range(4)]
            blocks += [("down%d" % l, half * 4 + j) for j in range(4)]
    memb = []
    for l in range(2):
        memb += [("mk%d" % l, j) for j in range(2)] + [("mv%d" % l, j) for j in range(2)]
    return blocks, memb


def build_program(nseq=NB, do_sample=True):
    nc = bass.Bass("TRN2", target_bir_lowering=False)
    P = Prog()

    def din(name, shape):
        return nc.dram_tensor(name, shape, F32, kind="ExternalInput").ap()

    def dout(name, shape):
        return nc.dram_tensor(name, shape, F32, kind="ExternalOutput").ap()

    x_prompt = din("x_prompt", [NB, SEQ, D])
    x_sample = din("x_sample", [NB * DEC, D])
    state_pool = din("state_pool", [NB, 15, 512])
    state_conv = din("state_conv", [NB, 3, 512])
    state_lru = din("state_lru", [NB, 512])
    cache_k = din("cache_swa_k", [NB, 128, 256])
    cache_v = din("cache_swa_v", [NB, 128, 256])
    cmem_k = din("cache_mem_k", [2, NB, NMEM, D])
    cmem_v = din("cache_mem_v", [2, NB, NMEM, D])
    mem_prompt = din("mem_prompt", [NB, NMEM, D])
    vecs = din("vecs", [128, 128])
    sinks = din("attn_sinks", [1, 16])
    ident_d = din("ident", [128, 128])
    W = dict(
        w_in=din("w_in", [D, 1536]), w_out=din("w_out", [D, D]), w_qkv=din("w_qkv", [D, 1536]),
        w_o=din("w_o", [D, D]), w_mq=din("w_mq", [2, D, D]), w_mk=din("w_mk", [2, D, D]),
        w_mv=din("w_mv", [2, D, D]), w_mo=din("w_mo", [2, D, D]), w_up=din("w_up", [2, D, 4 * D]),
        w_down=din("w_down", [2, 4 * D, D]),
    )
    pool_w_d = din("pool_w", [4, 128, 128])
    w_rg_a_d = din("w_rg_a", [8, 64, 64])
    w_rg_x_d = din("w_rg_x", [8, 64, 64])

    y_prompt = dout("y_prompt", [NB, SEQ, D])
    y_sample = dout("y_sample", [NB * DEC, D])
    pool_p = dout("pool_p", [NB, 15, 512])
    conv_p = dout("conv_p", [NB, 3, 512])
    lru_p = dout("lru_p", [NB, 512])
    swa_k_p = dout("swa_k_p", [NB, 128, 256])
    swa_v_p = dout("swa_v_p", [NB, 128, 256])
    mem_k_p = dout("mem_k_p", [2, NB, NMEM, D])
    mem_v_p = dout("mem_v_p", [2, NB, NMEM, D])
    pool_s = dout("pool_s", [NB, 15, 512])
    conv_s = dout("conv_s", [NB, 3, 512])
    lru_s = dout("lru_s", [NB, 512])
    swa_k_s = dout("swa_k_s", [NB, 128, 256])
    swa_v_s = dout("swa_v_s", [NB, 128, 256])

    tile_blocks, mem_blocks = block_catalogue()
    all_blocks = tile_blocks + mem_blocks
    bid = {b: i for i, b in enumerate(all_blocks)}
    wscr = nc.dram_tensor("wscr", [len(all_blocks), 128, 4096], BF16, kind="Internal").ap()

    import contextlib
    es = contextlib.ExitStack()
    with es:
        def sb(name, shape, dt=F32):
            return es.enter_context(nc.sbuf_tensor(name, shape, dt))

        x = sb("x", [128, 8, TP])
        h = sb("h", [128, 8, TP], BF16)
        up = sb("up", [128, 4, 16 + TP])
        ux = sb("ux", [128, 4, 4 + TP])
        ug = sb("ug", [128, 4, TP])
        ycat = sb("ycat", [128, 8, TP], BF16)
        NTMP = 7
        tmp = [sb("tmp%d" % i, [128, 16 + TP]) for i in range(NTMP)]
        xc = sb("xc", [128, 4, TP])
        xcb = sb("xcb", [128, 4, TP], BF16)
        dbf = sb("dbf", [128, 4, TP], BF16)
        q = sb("q", [128, 8, TP], BF16)
        kbuf = sb("kbuf", [128, 4, 768], BF16)
        vtok = sb("vtok", [64, 12, 512], BF16)
        hid = sb("hid", [128, 16, TP], BF16)
        wring = [sb("wr%d" % i, [128, 4096], BF16) for i in range(NW)]
        memk = [sb("memk%d" % i, [128, 8, NMEM], BF16) for i in range(2)]
        memv = [sb("memv%d" % i, [128, 2, D], BF16) for i in range(2)]
        NSTG = 4
        stg = [sb("stg%d" % i, [128, 512]) for i in range(NSTG)]
        pT2 = [sb("pT2_%d" % i, [128, 2, TP], BF16) for i in range(2)]
        rden = [sb("rden%d" % i, [128, TP]) for i in range(2)]
        pTs = [sb("pTs%d" % i, [64, 3, 256], BF16) for i in range(2)]
        dns = [sb("dns%d" % i, [128, 256]) for i in range(2)]
        ident = sb("ident_sb", [128, 128])
        ones_bf = sb("ones_bf", [128, 128], BF16)
        ones_f = sb("ones_f", [128, 64])
        cvec = sb("cvec", [128, 128])
        negb = sb("negb", [128, 8])
        c8 = sb("c8", [128, 8])
        epsc = sb("epsc", [128, 1])
        onec = sb("onec", [128, 1])
        esink = sb("esink", [128, 16])
        esx = sb("esx", [64, 4, 256], BF16)
        poolw = sb("poolw", [128, 4, 128], BF16)
        wbd = sb("wbd", [128, 8, 128], BF16)
        invc = sb("invc", [128, 4, 16])
        hstate = sb("hstate", [128, 4, 4])
        hstate_s = sb("hstate_s", [128, 4, 4])
        tpad = sb("tpad", [128, 128])
        ps = [es.enter_context(nc.psum_tensor("ps%d" % i, [128, 512], F32)) for i in range(8)]

        COL = {}
        col = 0
        for nm, n in [("norm_mix0", 8), ("norm_mix1", 8), ("norm_cross0", 8), ("norm_cross1", 8),
                      ("norm_mem0", 8), ("norm_mem1", 8), ("norm_mlp0", 8), ("norm_mlp1", 8),
                      ("norm_final", 8), ("conv_w0", 4), ("conv_w1", 4), ("conv_w2", 4), ("conv_w3", 4),
                      ("conv_b", 4), ("b_rg_a", 4), ("b_rg_x", 4), ("rg_lambda", 4), ("pool_scale", 4)]:
            COL[nm] = col
            col += n
        assert col <= 128

        bank_ctr = [0]

        def nb():
            b = bank_ctr[0] % 8
            bank_ctr[0] += 1
            return b

        stg_ctr = [0]

        def nstg():
            s = stg_ctr[0] % NSTG
            stg_ctr[0] += 1
            return s

        def MM(out, lhsT, rhs, start, stop, r, w):
            P.add("pe", lambda e: e.matmul(out, lhsT=lhsT, rhs=rhs, start=start, stop=stop), r, w)

        def TR(out, in_, r, w):
            P.add("pe", lambda e: e.transpose(out, in_, ident[:, :]), list(r) + ["ident"], w)

        def ACT(out, in_, func, r, w, bias=None, scale=1.0):
            if bias is None:
                P.add("act", lambda e: e.activation(out=out, in_=in_, func=func, scale=scale), r, w)
            else:
                P.add("act", lambda e: e.activation(out=out, in_=in_, func=func, bias=bias, scale=scale), r, w)

        def TT(eng, out, in0, in1, op, r, w):
            P.add(eng, lambda e: e.tensor_tensor(out=out, in0=in0, in1=in1, op=op), r, w)

        def TS(eng, out, in0, s1, s2, op0, op1, r, w):
            if s2 is None:
                P.add(eng, lambda e: e.tensor_scalar(out=out, in0=in0, scalar1=s1, scalar2=None, op0=op0), r, w)
            else:
                P.add(eng, lambda e: e.tensor_scalar(out=out, in0=in0, scalar1=s1, scalar2=s2, op0=op0, op1=op1), r, w)

        def STT(out, in0, scalar, in1, op0, op1, r, w):
            P.add("dve", lambda e: e.scalar_tensor_tensor(out=out, in0=in0, scalar=scalar, in1=in1, op0=op0, op1=op1), r, w)

        def CP(eng, out, in_, r, w):
            if eng == "act":
                P.add("act", lambda e: e.copy(out=out, in_=in_), r, w)
            else:
                P.add(eng, lambda e: e.tensor_copy(out=out, in_=in_), r, w)

        def MSET(eng, ap, val, w):
            P.add(eng, lambda e: e.memset(ap, val), (), w)

        def DMA(queue, out, in_, r, w, chan):
            P.add(queue, lambda e: e.dma_start(out=out, in_=in_), r, w, chan=chan)

        evac_ctr = [0]

        def evac_eng():
            evac_ctr[0] += 1
            return "act" if evac_ctr[0] % 2 else "dve"

        def setup_consts():
            DMA("sp", ident[:, :], ident_d, (), ["ident"], "c_ident")
            DMA("sp", stg[0][:, 0:128], vecs, (), [("stg", 0)], "c_vecs")
            for i in range(1, NSTG):
                MSET("pool", stg[i][:, :], 0.0, [("stg", i)])
            MSET("pool", tpad[:, :], 0.0, ["tpad"])
            MSET("dve", ones_bf[:, :], 1.0, ["ones"])
            MSET("dve", ones_f[:, :], 1.0, ["ones_f"])
            MSET("dve", epsc[:, :], EPS, ["epsc"])
            MSET("dve", onec[:, :], 1.0, ["onec"])
            MSET("dve", hstate[:, :, :], 0.0, ["hstate"])
            b = nb()
            TR(ps[b][:, 0:128], stg[0][:, 0:128], [("stg", 0)], [("ps", b)])
            CP("dve", cvec[:, :], ps[b][:, 0:128], [("ps", b)], ["cvec"])
            ca = COL["b_rg_a"]
            TS("dve", negb[:, :], cvec[:, ca:ca + 8], -1.0, None, ALU.mult, None, ["cvec"], ["negb"])
            cl = COL["rg_lambda"]
            ACT(c8[:, 0:4], cvec[:, cl:cl + 4], AF.Exp, ["cvec"], ["c8"], scale=-1.0)
            ACT(c8[:, 0:4], c8[:, 0:4], AF.Ln, ["c8", "onec"], ["c8"], bias=onec[:, 0:1])
            TS("dve", c8[:, 4:8], c8[:, 0:4], -16.0, None, ALU.mult, None, ["c8"], ["c8b"])
            TS("dve", c8[:, 0:4], c8[:, 0:4], -8.0, None, ALU.mult, None, ["c8", "c8b"], ["c8"])
            DMA("sp", esink[:, :], sinks.partition_broadcast(128), (), ["esink"], "c_sink")
            ACT(esink[:, :], esink[:, :], AF.Exp, ["esink"], ["esink"])
            MSET("pool", esx[:, :, :], 0.0, ["esx"])
            for p0 in (0, 32):
                pr = slice(p0, p0 + 1)
                for kv in range(4):
                    for par in range(2):
                        for j2 in range(2):
                            g = kv * 4 + 2 * j2 + par
                            o = (par * 2 + j2) * 64
                            TS("dve", ug[pr, kv, o:o + 64], ones_f[pr, :], esink[pr, g:g + 1], None, ALU.mult, None,
                               ["esink", "ones_f"], [("ug", kv)])
                ugk = [("ug", kv) for kv in range(4)]
                if p0 == 0:
                    CP("dve", esx[pr, :, :], ug[pr, :, 0:256], ugk, ["esx"])
                else:
                    CP("dve", dbf[pr, :, 0:256], ug[pr, :, 0:256], ugk, [("dbf", 0)])
                    CP("dve", ug[pr, :, 256:512], dbf[pr, :, 0:256], [("dbf", 0)], ugk)
                    TT("dve", ug[pr, :, 256:512], ug[pr, :, 0:256], ug[pr, :, 256:512], ALU.subtract, ugk, ugk)
                    CP("dve", esx[pr, :, :], ug[pr, :, 256:512], ugk, ["esx"])
            DMA("sp", tmp[0][:, 0:512].rearrange("p (g e) -> p g e", g=4), pool_w_d.rearrange("g c e -> c g e"),
                (), [("tmp", 0)], "c_pw")
            CP("dve", poolw[:, :, :], tmp[0][:, 0:512].rearrange("p (g e) -> p g e", g=4), [("tmp", 0)], ["poolw"])
            for wi, wd in enumerate((w_rg_a_d, w_rg_x_d)):
                t = tmp[1 + wi]
                MSET("pool", t[:, 0:512], 0.0, [("tmp", 1 + wi)])
                tv = t[:, 0:512].rearrange("p (c j) -> p c j", c=4)
                src = wd.rearrange("(c r) i j -> r i c j", r=2)
                for r_ in range(2):
                    DMA("sp", tv[r_ * 64:(r_ + 1) * 64, :, r_ * 64:(r_ + 1) * 64], src[r_], (), [("tmp", 1 + wi)],
                        "c_bd%d%d" % (wi, r_))
                CP("dve", wbd[:, wi * 4:(wi + 1) * 4, :], tv, [("tmp", 1 + wi)], ["wbd"])
            for g in range(4):
                win = 2 << g
                for t_ in range(15):
                    MSET("pool", invc[:, g, t_:t_ + 1], 1.0 / min(t_ + 1, win), ["invc"])

        def wsrc(name, j, half):
            def std(Wm, j):
                src = Wm.rearrange("(k p) n -> p k n", p=128)[:, half * 4:(half + 1) * 4, j * 512:(j + 1) * 512]
                return [(lambda s: s.rearrange("p (k n) -> p k n", k=4), src)]
            if name == "in":
                return std(W["w_in"], j)
            if name == "out":
                return std(W["w_out"], j)
            if name == "wo":
                return std(W["w_o"], j)
            if name[:2] in ("mq", "mo", "mk", "mv", "up"):
                l = int(name[-1])
                return std(W["w_" + name[:-1]][l], j)
            if name.startswith("down"):
                l = int(name[-1])
                hh, cb = j // 4, j % 4
                src = W["w_down"][l].rearrange("(k p) n -> p k n", p=128)[
                    :, hh * 16 + half * 8: hh * 16 + half * 8 + 8, cb * 256:(cb + 1) * 256]
                return [(lambda s: s.rearrange("p (k n) -> p k n", k=8), src)]
            if name == "qkv":
                Wr = W["w_qkv"].rearrange("(k p) n -> p k n", p=128)
                if j < 2:
                    return std(W["w_qkv"], j)
                c0 = 1024 if j == 2 else 1280
                res = []
                for kk in range(4):
                    src = Wr[:, half * 4 + kk, c0:c0 + 256].rearrange("p (v d) -> p v d", v=4)
                    for r_ in range(2):
                        res.append((lambda s, r_=r_, kk=kk: s.rearrange("p (k v r d) -> p k v r d", k=4, v=4, r=2)[:, kk, :, r_, :], src))
                return res
            raise KeyError(name)

        def prepass():
            stage = [(x[:, 0:4, :].rearrange("p a b -> p (a b)"), [("x", c) for c in range(4)]),
                     (x[:, 4:8, :].rearrange("p a b -> p (a b)"), [("x", c) for c in range(4, 8)]),
                     (xc[:, :, :].rearrange("p a b -> p (a b)"), [("xc", c) for c in range(4)])]
            n = 0
            for bi, (name, j) in enumerate(all_blocks):
                for half in range(2):
                    sap, skeys = stage[n % 3]
                    for k_, (vf, src) in enumerate(wsrc(name, j, half)):
                        P.add("sp", (lambda e, o=vf(sap), s=src: e.dma_start(out=o, in_=s)), (), skeys,
                              chan="pp_ld%d_%d" % (n % 3, k_))
                    slot = (n // 2) % NW
                    dstv = wring[slot][:, half * 2048:(half + 1) * 2048]
                    eng = "dve" if n % 2 == 0 else "pool"
                    CP(eng, dstv, sap, skeys, [("w", slot, half)])
                    if half == 1:
                        P.add("act", (lambda e, o=wscr[bi], s=wring[slot][:, :]: e.dma_start(out=o, in_=s)),
                              [("w", slot, 0), ("w", slot, 1)], [("wscr", bi)], chan="pp_st%d" % slot)
                    n += 1

        DEBUG_ONDEMAND = int(os.environ.get("KDEBUG_STAGE", "99")) < 99
        MEMMASK = int(os.environ.get("KDEBUG_MEM", "255"))
        wseq = []
        wpos = [0, 0]

        def w_issue_upto(k):
            while wpos[1] < min(k, len(wseq)):
                i = wpos[1]
                b = wseq[i]
                if b is None:
                    break
                slot = i % NW
                full = None
                if not DEBUG_ONDEMAND and i < CONV_N:
                    name_, j_ = all_blocks[b]
                    if name_.startswith("down"):
                        l_ = int(name_[-1])
                        hh_, cb_ = j_ // 4, j_ % 4
                        full = (W["w_down"][l_].rearrange("(k p) n -> p k n", p=128)[:, hh_ * 16:(hh_ + 1) * 16, cb_ * 256:(cb_ + 1) * 256], 16)
                    elif not (name_ == "qkv" and j_ >= 2):
                        if name_ in ("in", "out"):
                            Wm_ = W["w_" + name_]
                        elif name_ == "wo":
                            Wm_ = W["w_o"]
                        elif name_ == "qkv":
                            Wm_ = W["w_qkv"]
                        else:
                            Wm_ = W["w_" + name_[:-1]][int(name_[-1])]
                        full = (Wm_.rearrange("(k p) n -> p k n", p=128)[:, :, j_ * 512:(j_ + 1) * 512], 8)
                if full is not None:
                    src_, kk_ = full
                    P.add("pool", (lambda e, o=wring[slot][:, :].rearrange("p (k n) -> p k n", k=kk_), s_=src_:
                                   e.dma_start(out=o, in_=s_)), (), [("w", slot, 0), ("w", slot, 1)], chan="cwf%d" % slot)
                elif not DEBUG_ONDEMAND and i < CONV_N:
                    name_, j_ = all_blocks[b]
                    for half in range(2):
                        sap = wring[slot][:, half * 2048:(half + 1) * 2048]
                        for k_, (vf, src) in enumerate(wsrc(name_, j_, half)):
                            P.add("pool", (lambda e, o=vf(sap), s_=src: e.dma_start(out=o, in_=s_)), (),
                                  [("w", slot, half)], chan="cw%d_%d_%d" % (slot, half, k_))
                else:
                    DMA("sp", wring[slot][:, :], wscr[b], [("wscr", b)], [("w", slot, 0), ("w", slot, 1)], "w%d" % slot)
                wpos[1] += 1

        CONV_N = len(all_blocks)

        def wnext(name, j):
            i = wpos[0]
            if not DEBUG_ONDEMAND and 0 < i <= CONV_N:
                pslot = (i - 1) % NW
                pb = wseq[i - 1]
                P.add("sp", (lambda e, o=wscr[pb], s_=wring[pslot][:, :]: e.dma_start(out=o, in_=s_)),
                      [("w", pslot, 0), ("w", pslot, 1)], [("wscr", pb)], chan="cst%d" % pslot)
            if DEBUG_ONDEMAND:
                while len(wseq) <= i:
                    wseq.append(None)
                wseq[i] = bid[(name, j)]
                w_issue_upto(i + 1)
            assert wseq[i] == bid[(name, j)], (i, name, j, all_blocks[wseq[i]])
            w_issue_upto(i + NW)
            wpos[0] += 1
            slot = i % NW
            return wring[slot], [("w", slot, 0), ("w", slot, 1)]

        def rmsnorm(src, skey, dst, dkey, gname, T, inplace_out=None, out_keys=None, second=None):
            for c in range(8):
                if c % 2 == 0:
                    ACT(dst(c), src(c), AF.Square, [skey(c)], [dkey(c)])
                else:
                    TT("dve", dst(c), src(c), src(c), ALU.mult, [skey(c)], [dkey(c)])
            b = nb()
            for c in range(8):
                MM(ps[b][:, 0:T], ones_bf[:, :], dst(c), c == 0, c == 7, [dkey(c), "ones"], [("ps", b)])
            t = tmp[6]
            ACT(t[:, 0:T], ps[b][:, 0:T], AF.Ln, [("ps", b), "epsc"], [("tmp", 6)], bias=epsc[:, 0:1], scale=1.0 / D)
            ACT(t[:, 0:T], t[:, 0:T], AF.Exp, [("tmp", 6)], [("tmp", 6)], scale=-0.5)
            g0 = COL[gname]
            for c in range(8):
                if inplace_out is None:
                    STT(dst(c), src(c), cvec[:, g0 + c:g0 + c + 1], t[:, 0:T], ALU.mult, ALU.mult,
                        [skey(c), ("tmp", 6), "cvec"], [dkey(c)])
                else:
                    STT(inplace_out(c), src(c), cvec[:, g0 + c:g0 + c + 1], t[:, 0:T], ALU.mult, ALU.mult,
                        [skey(c), ("tmp", 6), "cvec", dkey(c)], out_keys(c))
            if second is not None:
                dst2, dkey2, gname2 = second
                g2 = COL[gname2]
                for c in range(8):
                    STT(dst2(c), src(c), cvec[:, g2 + c:g2 + c + 1], t[:, 0:T], ALU.mult, ALU.mult,
                        [skey(c), ("tmp", 6), "cvec"], [dkey2(c)])

        def proj_fm(name, nblk, src, skey, T, evac, kc=8, cols=512, kouter=False):
            ncol_chunks = cols // 128
            for j in range(nblk):
                wt, wk = wnext(name, j)
                wv = wt[:, :].rearrange("p (k n) -> p k n", k=kc)
                if j == 0 and kouter:
                    banks = [nb() for _ in range(ncol_chunks)]
                    for k in range(kc):
                        for oc in range(ncol_chunks):
                            MM(ps[banks[oc]][:, 0:T], wv[:, k, oc * 128:(oc + 1) * 128], src(k), k == 0, k == kc - 1,
                               wk + [skey(k)], [("ps", banks[oc])])
                    for oc in range(ncol_chunks):
                        evac(j * ncol_chunks + oc, banks[oc])
                    continue
                for oc in range(ncol_chunks):
                    b = nb()
                    for k in range(kc):
                        MM(ps[b][:, 0:T], wv[:, k, oc * 128:(oc + 1) * 128], src(k), k == 0, k == kc - 1,
                           wk + [skey(k)], [("ps", b)])
                    evac(j * ncol_chunks + oc, b)

        def resid_evac(T):
            def f(oc, b):
                TT("dve", x[:, oc, 0:T], ps[b][:, 0:T], x[:, oc, 0:T], ALU.add, [("ps", b), ("x", oc)], [("x", oc)])
            return f

        def load_T(dstf, dkeyf, src2d, ntok, ncols, queue="sp"):
            for t0 in range(0, ntok, 128):
                n = min(128, ntok - t0)
                for c0 in range(0, ncols, 512):
                    w_ = min(512, ncols - c0)
                    s = nstg()
                    DMA(queue, stg[s][0:n, 0:w_], src2d[t0:t0 + n, c0:c0 + w_], (), [("stg", s)], "stg%d" % s)
                    b = nb()
                    nch = w_ // 128
                    for cc in range(nch):
                        TR(ps[b][:, cc * 128:(cc + 1) * 128], stg[s][:, cc * 128:(cc + 1) * 128], [("stg", s)], [("ps", b)])
                    src = ps[b][:, 0:nch * 128].rearrange("p (c t) -> p c t", c=nch)[:, :, 0:n]
                    CP(evac_eng(), dstf(c0 // 128, nch, t0, n), src, [("ps", b)],
                       [dkeyf(c0 // 128 + cc) for cc in range(nch)])

        def store_T(srcf, skeyf, dst2d, ntok, ncols, pad=False):
            for t0 in range(0, ntok, 128):
                n = min(128, ntok - t0)
                for c0 in range(0, ncols, 512):
                    w_ = min(512, ncols - c0)
                    nch = w_ // 128
                    b = nb()
                    for cc in range(nch):
                        c = c0 // 128 + cc
                        if n == 128:
                            TR(ps[b][:, cc * 128:(cc + 1) * 128], srcf(c, t0, n), [skeyf(c)], [("ps", b)])
                        else:
                            CP("dve", tpad[:, 0:n], srcf(c, t0, n), [skeyf(c)], ["tpad"])
                            TR(ps[b][:, cc * 128:(cc + 1) * 128], tpad[:, :], ["tpad"], [("ps", b)])
                    s = nstg()
                    CP(evac_eng(), stg[s][0:n, 0:w_], ps[b][0:n, 0:w_], [("ps", b)], [("stg", s)])
                    DMA("act", dst2d[t0:t0 + n, c0:c0 + w_], stg[s][0:n, 0:w_], [("stg", s)], [], "stg%d" % s)

        mem_front_done = set()

        def mem_front(bl):
            if bl in mem_front_done or bl >= nseq:
                return
            mem_front_done.add(bl)
            memx = lambda c: xc[:, :, :].rearrange("p a (h t) -> p (a h) t", h=2)[:, c, :]
            memxk = lambda c: ("xc", c // 2)
            mn0 = lambda c: xcb[:, :, :].rearrange("p a (h t) -> p (a h) t", h=2)[:, c, :]
            mn0k = lambda c: ("xcb", c // 2)
            mn1 = lambda c: dbf[:, :, :].rearrange("p a (h t) -> p (a h) t", h=2)[:, c, :]
            mn1k = lambda c: ("dbf", c // 2)
            xv = xc[:, :, :].rearrange("p a (h t) -> p (a h) t", h=2)
            load_T(lambda c0, nch, t0, n: xv[:, c0:c0 + nch, t0:t0 + n], memxk, mem_prompt[bl], NMEM, D)
            rmsnorm(memx, memxk, mn0, mn0k, "norm_mem0", NMEM, second=(mn1, mn1k, "norm_mem1"))

        def mem_phase(bl):
            mem_front(bl)
            for l in range(2 if MEMMASK & 2 else 0):
                if l == 0:
                    mn = lambda c: xcb[:, :, :].rearrange("p a (h t) -> p (a h) t", h=2)[:, c, :]
                    mnk = lambda c: ("xcb", c // 2)
                else:
                    mn = lambda c: dbf[:, :, :].rearrange("p a (h t) -> p (a h) t", h=2)[:, c, :]
                    mnk = lambda c: ("dbf", c // 2)
                if not (MEMMASK & 4):
                    continue
                for j in range(2):
                    wt, wk = wnext("mk%d" % l, j)
                    wv = wt[:, :].rearrange("p (k n) -> p k n", k=8)
                    for oc in range(4):
                        b = nb()
                        for k in range(8):
                            MM(ps[b][:, 0:NMEM], wv[:, k, oc * 128:(oc + 1) * 128], mn(k), k == 0, k == 7,
                               wk + [mnk(k)], [("ps", b)])
                        CP(evac_eng(), memk[l][:, j * 4 + oc, :], ps[b][:, 0:NMEM], [("ps", b)], [("memk", l)])
                    for tb in range(2 if MEMMASK & 8 else 0):
                        b = nb()
                        for k in range(8):
                            MM(ps[b][:, :], mn(k)[:, tb * 128:(tb + 1) * 128], wv[:, k, :], k == 0, k == 7,
                               wk + [mnk(k)], [("ps", b)])
                        s = nstg()
                        CP(evac_eng(), stg[s][:, :], ps[b][:, :], [("ps", b)], [("stg", s)])
                        DMA("act", mem_k_p[l, bl, tb * 128:(tb + 1) * 128, j * 512:(j + 1) * 512], stg[s][:, :],
                            [("stg", s)], [], "stg%d" % s)
                for j in range(2 if MEMMASK & 16 else 0):
                    wt, wk = wnext("mv%d" % l, j)
                    wv = wt[:, :].rearrange("p (k n) -> p k n", k=8)
                    for tb in range(2):
                        b = nb()
                        for k in range(8):
                            MM(ps[b][:, :], mn(k)[:, tb * 128:(tb + 1) * 128], wv[:, k, :], k == 0, k == 7,
                               wk + [mnk(k)], [("ps", b)])
                        s = nstg()
                        CP("act", stg[s][:, :], ps[b][:, :], [("ps", b)], [("stg", s)])
                        CP("dve", memv[l][:, tb, j * 512:(j + 1) * 512], ps[b][:, :], [("ps", b)], [("memv", l)])
                        DMA("act", mem_v_p[l, bl, tb * 128:(tb + 1) * 128, j * 512:(j + 1) * 512], stg[s][:, :],
                            [("stg", s)], [], "stg%d" % s)

        def mem_load_sample(l, bl, slot):
            for tb in range(2):
                for c0 in range(0, D, 512):
                    s = nstg()
                    DMA("sp", stg[s][:, :], cmem_k[l, bl, tb * 128:(tb + 1) * 128, c0:c0 + 512], (), [("stg", s)], "stg%d" % s)
                    b = nb()
                    for cc in range(4):
                        TR(ps[b][:, cc * 128:(cc + 1) * 128], stg[s][:, cc * 128:(cc + 1) * 128], [("stg", s)], [("ps", b)])
                    CP(evac_eng(), memk[slot][:, c0 // 128:c0 // 128 + 4, tb * 128:(tb + 1) * 128],
                       ps[b][:, :].rearrange("p (c t) -> p c t", c=4), [("ps", b)], [("memk", slot)])
                    s = nstg()
                    DMA("sp", stg[s][:, :], cmem_v[l, bl, tb * 128:(tb + 1) * 128, c0:c0 + 512], (), [("stg", s)], "stg%d" % s)
                    CP(evac_eng(), memv[slot][:, tb, c0:c0 + 512], stg[s][:, :], [("stg", s)], [("memv", slot)])

        def cross_attn(l, segs, T, kvslot_of):
            rmsnorm(lambda c: x[:, c, 0:T], lambda c: ("x", c), lambda c: h[:, c, 0:T], lambda c: ("h", c),
                    "norm_cross%d" % l, T)

            def qev(oc, b):
                CP(evac_eng(), q[:, oc, 0:T], ps[b][:, 0:T], [("ps", b)], [("q", oc)])
            proj_fm("mq%d" % l, 2, lambda k: h[:, k, 0:T], lambda k: ("h", k), T, qev, kouter=True)
            pi = 0
            for sg in segs:
                c0, n = sg["c0"], sg["n"]
                slot = kvslot_of(sg)
                for hd in range(4):
                    pt = pT2[pi % 2]
                    ptk = ("pT2", pi % 2)
                    rd = rden[pi % 2]
                    rdk = ("rden", pi % 2)
                    pi += 1
                    for kb in range(2):
                        b = nb()
                        for dc in range(2):
                            MM(ps[b][:, 0:n], memk[slot][:, 2 * hd + dc, kb * 128:(kb + 1) * 128],
                               q[:, 2 * hd + dc, c0:c0 + n], dc == 0, dc == 1,
                               [("memk", slot), ("q", 2 * hd + dc)], [("ps", b)])
                        ACT(pt[:, kb, 0:n], ps[b][:, 0:n], AF.Exp, [("ps", b)], [ptk], scale=1.0 / 16.0)
                    b = nb()
                    for kb in range(2):
                        MM(ps[b][:, 0:n], ones_bf[:, :], pt[:, kb, 0:n], kb == 0, kb == 1, [ptk, "ones"], [("ps", b)])
                    ACT(rd[:, 0:n], ps[b][:, 0:n], AF.Ln, [("ps", b)], [rdk])
                    ACT(rd[:, 0:n], rd[:, 0:n], AF.Exp, [rdk], [rdk], scale=-1.0)
                    for dc in range(2):
                        b = nb()
                        for kb in range(2):
                            MM(ps[b][:, 0:n], memv[slot][:, kb, hd * 256 + dc * 128: hd * 256 + (dc + 1) * 128],
                               pt[:, kb, 0:n], kb == 0, kb == 1, [("memv", slot), ptk], [("ps", b)])
                        TT("dve", ycat[:, 2 * hd + dc, c0:c0 + n], ps[b][:, 0:n], rd[:, 0:n], ALU.mult,
                           [("ps", b), rdk], [("ycat", 2 * hd + dc)])
            proj_fm("mo%d" % l, 2, lambda k: ycat[:, k, 0:T], lambda k: ("ycat", k), T, resid_evac(T), kouter=True)

        def mlp(l, T):
            rmsnorm(lambda c: x[:, c, 0:T], lambda c: ("x", c), lambda c: h[:, c, 0:T], lambda c: ("h", c),
                    "norm_mlp%d" % l, T)
            rr = [0]
            for half in range(2):
                for j in range(4):
                    wt, wk = wnext("up%d" % l, half * 4 + j)
                    wv = wt[:, :].rearrange("p (k n) -> p k n", k=8)
                    kob = None
                    if half == 0 and j == 0:
                        kob = [nb() for _ in range(4)]
                        for k in range(8):
                            for oc in range(4):
                                MM(ps[kob[oc]][:, 0:T], wv[:, k, oc * 128:(oc + 1) * 128], h[:, k, 0:T], k == 0, k == 7,
                                   wk + [("h", k)], [("ps", kob[oc])])
                    for oc in range(4):
                        if kob is not None:
                            b = kob[oc]
                        else:
                            b = nb()
                            for k in range(8):
                                MM(ps[b][:, 0:T], wv[:, k, oc * 128:(oc + 1) * 128], h[:, k, 0:T], k == 0, k == 7,
                                   wk + [("h", k)], [("ps", b)])
                        ti = rr[0] % 4
                        rr[0] += 1
                        t = tmp[ti]
                        ACT(t[:, 0:T], ps[b][:, 0:T], AF.Relu, [("ps", b)], [("tmp", ti)])
                        TT("dve", hid[:, j * 4 + oc, 0:T], t[:, 0:T], t[:, 0:T], ALU.mult, [("tmp", ti)], [("hid", j * 4 + oc)])
                for cb in range(4):
                    wt, wk = wnext("down%d" % l, half * 4 + cb)
                    wv = wt[:, :].rearrange("p (k n) -> p k n", k=16)
                    kob = None
                    if cb == 0:
                        kob = [nb(), nb()]
                        for k in range(16):
                            for cl in range(2):
                                MM(ps[kob[cl]][:, 0:T], wv[:, k, cl * 128:(cl + 1) * 128], hid[:, k, 0:T], k == 0, k == 15,
                                   wk + [("hid", k)], [("ps", kob[cl])])
                    for cl in range(2):
                        oc = cb * 2 + cl
                        if kob is not None:
                            b = kob[cl]
                        else:
                            b = nb()
                            for k in range(16):
                                MM(ps[b][:, 0:T], wv[:, k, cl * 128:(cl + 1) * 128], hid[:, k, 0:T], k == 0, k == 15,
                                   wk + [("hid", k)], [("ps", b)])
                        TT("dve", x[:, oc, 0:T], ps[b][:, 0:T], x[:, oc, 0:T], ALU.add, [("ps", b), ("x", oc)], [("x", oc)])

        def even_layer(segs, T, sample):
            rmsnorm(lambda c: x[:, c, 0:T], lambda c: ("x", c), lambda c: h[:, c, 0:T], lambda c: ("h", c),
                    "norm_mix0", T)
            hs = hstate_s if sample else hstate
            hsk = "hstate_s" if sample else "hstate"
            def hidf(i):
                return hid[:, 2 * i:2 * i + 2, :].bitcast(F32).rearrange("p a b -> p (a b)"), [("hid", 2 * i), ("hid", 2 * i + 1)]
            def qf(i):
                return q[:, 2 * i:2 * i + 2, :].bitcast(F32).rearrange("p a b -> p (a b)"), [("q", 2 * i), ("q", 2 * i + 1)]
            TA = [hidf(c) for c in range(4)]
            TB = [hidf(4 + c) for c in range(4)]
            TC = [qf(c) for c in range(4)]
            TG = [(tmp[i][:, 0:TP], [("tmp", i)]) for i in (0, 1, 2, 6)]
            R4 = range(4)

            def in_block(j):
                wt, wk = wnext("in", j)
                wv = wt[:, :].rearrange("p (k n) -> p k n", k=8)
                kob = None
                if j == 1:
                    kob = [nb() for _ in range(4)]
                    for k in range(8):
                        for oc4 in range(4):
                            MM(ps[kob[oc4]][:, 0:T], wv[:, k, oc4 * 128:(oc4 + 1) * 128], h[:, k, 0:T], k == 0, k == 7,
                               wk + [("h", k)], [("ps", kob[oc4])])
                for oc4 in range(4):
                    if kob is not None:
                        b = kob[oc4]
                    else:
                        b = nb()
                        for k in range(8):
                            MM(ps[b][:, 0:T], wv[:, k, oc4 * 128:(oc4 + 1) * 128], h[:, k, 0:T], k == 0, k == 7,
                               wk + [("h", k)], [("ps", b)])
                    if j == 0:
                        for sg in segs:
                            CP("act", up[:, oc4, sg["ucol"]:sg["ucol"] + sg["n"]], ps[b][:, sg["c0"]:sg["c0"] + sg["n"]],
                               [("ps", b)], [("up", oc4)])
                    elif j == 1:
                        for sg in segs:
                            CP("act", ux[:, oc4, sg["xcol"]:sg["xcol"] + sg["n"]], ps[b][:, sg["c0"]:sg["c0"] + sg["n"]],
                               [("ps", b)], [("ux", oc4)])
                    else:
                        CP("act", ug[:, oc4, 0:T], ps[b][:, 0:T], [("ps", b)], [("ug", oc4)])

            in_block(1)
            for c in R4:
                cw = [COL["conv_w%d" % k] + c for k in range(4)]
                cbc = COL["conv_b"] + c
                for sg in segs:
                    xo, n, c0 = sg["xcol"], sg["n"], sg["c0"]
                    TS("dve", xc[:, c, c0:c0 + n], ux[:, c, xo:xo + n], cvec[:, cw[3]:cw[3] + 1], cvec[:, cbc:cbc + 1],
                       ALU.mult, ALU.add, [("ux", c), "cvec"], [("xc", c)])
                    for k in (2, 1, 0):
                        sh = 3 - k
                        STT(xc[:, c, c0:c0 + n], ux[:, c, xo - sh:xo - sh + n], cvec[:, cw[k]:cw[k] + 1], xc[:, c, c0:c0 + n],
                            ALU.mult, ALU.add, [("ux", c), ("uxh", c), "cvec", ("xc", c)], [("xc", c)])
                CP("act", xcb[:, c, 0:T], xc[:, c, 0:T], [("xc", c)], [("xcb", c)])
            in_block(0)
            flush_pending()
            for g in range(4):
                Wn = 2 << g
                for sg in segs:
                    uc, n, c0 = sg["ucol"], sg["n"], sg["c0"]
                    cur = lambda lo, hi, g=g, uc=uc: up[:, g, uc + lo:uc + hi]
                    curk = [("up", g), ("uph", g)]
                    for lev in range(1, g + 2):
                        sh = 1 << (lev - 1)
                        lo = -(Wn - (1 << lev))
                        ti = 4 + (lev % 2)
                        o = tmp[ti]
                        TT("dve", o[:, 16 + lo:16 + n], cur(lo, n), cur(lo - sh, n - sh), ALU.add, curk, [("tmp", ti)])
                        cur = lambda lo_, hi_, o=o: o[:, 16 + lo_:16 + hi_]
                        curk = [("tmp", ti)]
                    STT(dbf[:, g, c0:c0 + n], cur(0, n), 1.0 / Wn, up[:, g, uc:uc + n], ALU.mult, ALU.subtract,
                        curk + [("up", g)], [("dbf", g)])
                    if sg["first"] and not sample:
                        m = Wn - 1
                        t6 = tmp[3]
                        TT("dve", t6[:, 0:m], cur(0, m), invc[:, g, 0:m], ALU.mult, curk + ["invc"], [("tmp", 3)])
                        TT("dve", dbf[:, g, c0:c0 + m], t6[:, 0:m], up[:, g, uc:uc + m], ALU.subtract,
                           [("tmp", 3), ("up", g)], [("dbf", g)])
            in_block(2)
            for c in R4:
                G, gk = TG[c]
                u_ = ug[:, c, 0:T]
                ACT(G[:, 0:T], u_, AF.Square, [("ug", c)], gk)
                ACT(G[:, 0:T], G[:, 0:T], AF.Identity, gk + ["onec"], gk, bias=onec[:, 0:1], scale=0.044715)
                TT("dve", G[:, 0:T], G[:, 0:T], u_, ALU.mult, gk + [("ug", c)], gk)
            ba = COL["b_rg_a"]
            bx = COL["b_rg_x"]
            for c in R4:
                A, ak = TA[c]
                b_ = nb()
                MM(ps[b_][:, 0:T], wbd[:, c, :], xcb[:, c, 0:T], True, True, ["wbd", ("xcb", c)], [("ps", b_)])
                ACT(A[:, 0:T], ps[b_][:, 0:T], AF.Sigmoid, [("ps", b_), "cvec"], ak, bias=cvec[:, ba + c:ba + c + 1])
            for c in R4:
                Bt, bk = TB[c]
                b_ = nb()
                MM(ps[b_][:, 0:T], wbd[:, 4 + c, :], xcb[:, c, 0:T], True, True, ["wbd", ("xcb", c)], [("ps", b_)])
                ACT(Bt[:, 0:T], ps[b_][:, 0:T], AF.Sigmoid, [("ps", b_), "cvec"], bk, bias=cvec[:, bx + c:bx + c + 1])
            for c in R4:
                G, gk = TG[c]
                ACT(G[:, 0:T], G[:, 0:T], AF.Sigmoid, gk, gk, scale=2.0 * GELU_K)
            for g in range(4):
                b = nb()
                MM(ps[b][:, 0:T], poolw[:, g, :], dbf[:, g, 0:T], True, True, ["poolw", ("dbf", g)], [("ps", b)])
                pc = COL["pool_scale"] + g
                TS("dve", ycat[:, g, 0:T], ps[b][:, 0:T], cvec[:, pc:pc + 1], None, ALU.mult, None, [("ps", b), "cvec"], [("ycat", g)])
            for pair in ((0, 1), (2, 3)):
                for c in pair:
                    A, ak = TA[c]
                    C, ck = TC[c]
                    ACT(C[:, 0:T], A[:, 0:T], AF.Exp, ak + ["c8b"], ck, scale=c8[:, 4 + c:5 + c])
                for c in pair:
                    A, ak = TA[c]
                    ACT(A[:, 0:T], A[:, 0:T], AF.Exp, ak + ["c8"], ak, scale=c8[:, c:c + 1])
                for c in pair:
                    C, ck = TC[c]
                    ACT(C[:, 0:T], C[:, 0:T], AF.Ln, ck + ["onec"], ck, bias=onec[:, 0:1], scale=-1.0)
                for c in pair:
                    C, ck = TC[c]
                    Bt, bk = TB[c]
                    ACT(C[:, 0:T], C[:, 0:T], AF.Exp, ck, ck, scale=0.5)
                    TT("dve", Bt[:, 0:T], Bt[:, 0:T], C[:, 0:T], ALU.mult, bk + ck, bk)
                    TT("dve", Bt[:, 0:T], Bt[:, 0:T], xc[:, c, 0:T], ALU.mult, bk + [("xc", c)], bk)
                for c in pair:
                    A, ak = TA[c]
                    Bt, bk = TB[c]
                    C, ck = TC[c]
                    G, gk = TG[c]
                    for sg in segs:
                        n, c0, bl = sg["n"], sg["c0"], sg["bl"]
                        P.add("dve", (lambda e, o=C[:, c0:c0 + n], a_=A[:, c0:c0 + n], bb=Bt[:, c0:c0 + n],
                                      ini=hs[:, c, bl:bl + 1]:
                                      e.tensor_tensor_scan(out=o, data0=a_, data1=bb, initial=ini, op0=ALU.mult, op1=ALU.add)),
                              ak + bk + [hsk] + ck, ck)
                        CP("pool", hs[:, c, bl:bl + 1], C[:, c0 + n - 1:c0 + n], ck, [hsk])
                    TT("dve", G[:, 0:T], G[:, 0:T], ug[:, c, 0:T], ALU.mult, gk + [("ug", c)], gk)
                    TT("dve", ycat[:, 4 + c, 0:T], C[:, 0:T], G[:, 0:T], ALU.mult, ck + gk, [("ycat", 4 + c)])

            for sg in segs:
                uc, xo, n, bl = sg["ucol"], sg["xcol"], sg["n"], sg["bl"]
                if sg["last"]:
                    pd = pool_s if sample else pool_p
                    cd = conv_s if sample else conv_p
                    store_T(lambda c, t0, nn: up[:, c, uc + n - 15:uc + n], lambda c: ("up", c), pd[bl], 15, 512)
                    store_T(lambda c, t0, nn: ux[:, c, xo + n - 3:xo + n], lambda c: ("ux", c), cd[bl], 3, 512)
                else:
                    for g in range(4):
                        CP("pool", up[:, g, uc - 15:uc], up[:, g, uc + n - 15:uc + n], [("up", g)], [("uph", g)])
                        CP("pool", ux[:, g, xo - 3:xo], ux[:, g, xo + n - 3:xo + n], [("ux", g)], [("uxh", g)])
            proj_fm("out", 2, lambda k: ycat[:, k, 0:T], lambda k: ("ycat", k), T, resid_evac(T), kouter=True)

        def odd_layer(segs, T, sample):
            rmsnorm(lambda c: x[:, c, 0:T], lambda c: ("x", c), lambda c: h[:, c, 0:T], lambda c: ("h", c),
                    "norm_mix1", T)

            def qev(oc, b):
                CP(evac_eng(), q[:, oc, 0:T], ps[b][:, 0:T], [("ps", b)], [("q", oc)])
            proj_fm("qkv", 2, lambda k: h[:, k, 0:T], lambda k: ("h", k), T, qev, kouter=True)
            wt, wk = wnext("qkv", 2)
            wv = wt[:, :].rearrange("p (k n) -> p k n", k=8)
            for kv in range(4):
                b = nb()
                for k in range(8):
                    MM(ps[b][:, 0:T], wv[:, k, kv * 128:(kv + 1) * 128], h[:, k, 0:T], k == 0, k == 7,
                       wk + [("h", k)], [("ps", b)])
                for sg in segs:
                    kc0 = sg["kb0"] + 128
                    CP(evac_eng(), kbuf[:, kv, kc0:kc0 + sg["n"]], ps[b][:, sg["c0"]:sg["c0"] + sg["n"]],
                       [("ps", b)], [("kbuf", kv)])
            wv5 = wt[:, :].rearrange("p (k v r d) -> p k v r d", k=8, v=4, r=2)
            for sg in segs:
                if not sg["last"]:
                    continue
                bl, n, c0 = sg["bl"], sg["n"], sg["c0"]
                nrow = min(128, n)
                t0 = c0 + n - nrow
                b = nb()
                for k in range(8):
                    MM(ps[b][0:nrow, 0:256].rearrange("p (v d) -> p v d", v=4), h[:, k, t0:t0 + nrow], wv5[:, k, :, 0, :],
                       k == 0, k == 7, wk + [("h", k)], [("ps", b)])
                s = nstg()
                CP(evac_eng(), stg[s][0:nrow, 0:256], ps[b][0:nrow, 0:256], [("ps", b)], [("stg", s)])
                kd = swa_k_s if sample else swa_k_p
                DMA("act", kd[bl, 128 - nrow:128, :], stg[s][0:nrow, 0:256], [("stg", s)], [], "stg%d" % s)
                if sample:
                    DMA("sp", swa_k_s[bl, 0:64, :], cache_k[bl, 64:128, :], (), [], "d2d")
            wt, wk = wnext("qkv", 3)
            wv = wt[:, :].rearrange("p (k n) -> p k n", k=8)
            for sg in segs:
                n, c0, bl = sg["n"], sg["c0"], sg["bl"]
                for j in range(n // 64):
                    b = nb()
                    for k in range(8):
                        MM(ps[b][0:64, :], h[:, k, c0 + j * 64:c0 + (j + 1) * 64], wv[:, k, :], k == 0, k == 7,
                           wk + [("h", k)], [("ps", b)])
                    vs = sg["vb0"] + 2 + j
                    CP(evac_eng(), vtok[0:64, vs, :], ps[b][0:64, :], [("ps", b)], [("vtok", vs)])
                    if sg["last"] and j >= n // 64 - 2:
                        s = nstg()
                        CP(evac_eng(), stg[s][0:64, 0:256].rearrange("p (v d) -> p v d", v=4),
                           ps[b][0:64, :].rearrange("p (v r d) -> p v r d", v=4, r=2)[:, :, 0, :], [("ps", b)], [("stg", s)])
                        vd = swa_v_s if sample else swa_v_p
                        row0 = 128 - (n // 64 - j) * 64
                        DMA("act", vd[bl, row0:row0 + 64, :], stg[s][0:64, 0:256], [("stg", s)], [], "stg%d" % s)
                if sample:
                    DMA("sp", swa_v_s[bl, 0:64, :], cache_v[bl, 64:128, :], (), [], "d2d")
            units = []
            for sg in segs:
                for nq in range(sg["n"] // 64):
                    for kv in range(4):
                        units.append((sg, nq, kv))

            def unit_info(ui):
                sg, nq, kv = units[ui]
                ext = [e_ for e_ in (nq, nq + 1, nq + 2) if (e_ >= 2 or sg["hist_valid"])]
                return sg, nq, kv, ext, sg["c0"] + nq * 64, pTs[ui % 2], ("pTs", ui % 2), dns[ui % 2], ("dns", ui % 2)

            def swa_scores(ui):
                sg, nq, kv, ext, qc0, pt, ptk, dn, dnk = unit_info(ui)
                ne = len(ext)
                for par in range(2):
                    b = nb()
                    for ji, e_ in enumerate(ext):
                        kcol = sg["kb0"] + e_ * 64
                        MM(ps[b][0:64, ji * 128:(ji + 1) * 128].rearrange("p (a l) -> p a l", a=2),
                           kbuf[par * 64:(par + 1) * 64, kv, kcol:kcol + 64],
                           q[par * 64:(par + 1) * 64, 2 * kv:2 * kv + 2, qc0:qc0 + 64], True, True,
                           [("kbuf", kv), ("kbufh", kv), ("q", 2 * kv), ("q", 2 * kv + 1)], [("ps", b)])
                    ACT(pt[0:64, 0:ne, par * 128:(par + 1) * 128],
                        ps[b][0:64, 0:ne * 128].rearrange("p (j c) -> p j c", j=ne), AF.Exp, [("ps", b)], [ptk], scale=0.125)

            def swa_pv(ui):
                sg, nq, kv, ext, qc0, pt, ptk, dn, dnk = unit_info(ui)
                bd = nb()
                MM(ps[bd][:, 0:256], ones_bf[0:64, :], esx[0:64, kv, :], True, False, ["ones", "esx"], [("ps", bd)])
                for ji in range(len(ext)):
                    MM(ps[bd][:, 0:256], ones_bf[0:64, :], pt[0:64, ji, :], False, ji == len(ext) - 1,
                       ["ones", ptk], [("ps", bd)])
                ACT(dn[:, :], ps[bd][:, 0:256], AF.Ln, [("ps", bd)], [dnk])
                ACT(dn[:, :], dn[:, :], AF.Exp, [dnk], [dnk], scale=-1.0)
                bo = nb()
                for ji, e_ in enumerate(ext):
                    vs = sg["vb0"] + e_
                    MM(ps[bo][:, 0:256], vtok[0:64, vs, kv * 128:(kv + 1) * 128], pt[0:64, ji, :], ji == 0,
                       ji == len(ext) - 1, [("vtok", vs), ptk], [("ps", bo)])
                for par in range(2):
                    sl = slice(par * 64, (par + 1) * 64)
                    TT("dve", ycat[sl, 2 * kv:2 * kv + 2, qc0:qc0 + 64],
                       ps[bo][sl, par * 128:(par + 1) * 128].rearrange("p (a l) -> p a l", a=2),
                       dn[sl, par * 128:(par + 1) * 128].rearrange("p (a l) -> p a l", a=2), ALU.mult,
                       [("ps", bo), dnk], [("ycat", 2 * kv), ("ycat", 2 * kv + 1)])

            swa_scores(0)
            for ui in range(len(units)):
                if ui + 1 < len(units):
                    swa_scores(ui + 1)
                swa_pv(ui)
            for sg in segs:
                if sample or sg["last"]:
                    continue
                kb0, n, vb0 = sg["kb0"], sg["n"], sg["vb0"]
                for kv in range(4):
                    CP("pool", kbuf[:, kv, kb0:kb0 + 128], kbuf[:, kv, kb0 + n:kb0 + n + 128], [("kbuf", kv)], [("kbufh", kv)])
                for j in range(2):
                    CP("pool", vtok[0:64, vb0 + j, :], vtok[0:64, vb0 + n // 64 + j, :], [("vtok", vb0 + n // 64 + j)],
                       [("vtok", vb0 + j)])
            proj_fm("wo", 2, lambda k: ycat[:, k, 0:T], lambda k: ("ycat", k), T, resid_evac(T), kouter=True)

        pending = []

        def flush_pending():
            while pending:
                pending.pop(0)()

        def final_norm_store(T, dst2d):
            yb = lambda c: hid[:, 2 * c:2 * c + 2, :].bitcast(F32).rearrange("p a b -> p (a b)")
            ybk = lambda c: [("hid", 2 * c), ("hid", 2 * c + 1)]
            rmsnorm(lambda c: x[:, c, 0:T], lambda c: ("x", c), lambda c: h[:, c, 0:T], lambda c: ("h", c),
                    "norm_final", T, inplace_out=lambda c: yb(c)[:, 0:T], out_keys=ybk)

            def do_store():
                for t0 in range(0, T, 128):
                    for c0 in range(0, D, 512):
                        b = nb()
                        for cc in range(4):
                            c = c0 // 128 + cc
                            TR(ps[b][:, cc * 128:(cc + 1) * 128], yb(c)[:, t0:t0 + 128], ybk(c), [("ps", b)])
                        s_ = nstg()
                        CP(evac_eng(), stg[s_][:, :], ps[b][:, :], [("ps", b)], [("stg", s_)])
                        DMA("act", dst2d[t0:t0 + 128, c0:c0 + 512], stg[s_][:, :], [("stg", s_)], [], "stg%d" % s_)
            pending.append(do_store)

        def run_tile(segs, T, sample, src2d, dst2d, kvslot0, kvslot1, next_mem=None):
            load_T(lambda c0, nch, t0, n: x[:, c0:c0 + nch, t0:t0 + n], lambda c: ("x", c), src2d, T, D)
            even_layer(segs, T, sample)
            cross_attn(0, segs, T, kvslot0)
            mlp(0, T)
            odd_layer(segs, T, sample)
            cross_attn(1, segs, T, kvslot1)
            if next_mem is not None and STAGE >= 99:
                mem_front(next_mem)
            mlp(1, T)
            final_norm_store(T, dst2d)

        tb_ids = [bid[b] for b in tile_blocks]
        mb_ids = [bid[b] for b in mem_blocks]
        if do_sample and not DEBUG_ONDEMAND:
            wseq.extend(tb_ids)
        for s_ in range(nseq if not DEBUG_ONDEMAND else 0):
            wseq.extend(mb_ids)
            for _ in range(SEQ // TP):
                wseq.extend(tb_ids)

        STAGE = int(os.environ.get("KDEBUG_STAGE", "99"))
        NTI = int(os.environ.get("KDEBUG_NTILE", str(SEQ // TP)))
        if not DEBUG_ONDEMAND:
            w_issue_upto(NW)
        setup_consts()
        if DEBUG_ONDEMAND and STAGE >= 1:
            prepass()
        if not DEBUG_ONDEMAND:
            assert sorted(wseq[:CONV_N]) == list(range(CONV_N))
        if do_sample and STAGE >= 4:
            T = NB * DEC
            segs = []
            for bl in range(NB):
                segs.append(dict(bl=bl, c0=bl * DEC, n=DEC, ucol=bl * 80 + 16, xcol=bl * 68 + 4, kb0=bl * 192, vb0=bl * 3,
                                 first=True, last=True, hist_valid=True))
            for sg in segs:
                bl = sg["bl"]
                uc, xo = sg["ucol"], sg["xcol"]
                load_T(lambda c0, nch, t0, n, uc=uc: up[:, c0:c0 + nch, uc - 15:uc], lambda c: ("uph", c), state_pool[bl], 15, 512)
                load_T(lambda c0, nch, t0, n, xo=xo: ux[:, c0:c0 + nch, xo - 3:xo], lambda c: ("uxh", c), state_conv[bl], 3, 512)
                s = nstg()
                DMA("sp", stg[s][:, 0:256], cache_k[bl], (), [("stg", s)], "stg%d" % s)
                s2 = nstg()
                for r_ in range(2):
                    CP("pool", stg[s2][:, :].rearrange("p (v r d) -> p v r d", v=4, r=2)[:, :, r_, :],
                       stg[s][:, 0:256].rearrange("p (v d) -> p v d", v=4), [("stg", s)], [("stg", s2)])
                b = nb()
                for kv in range(4):
                    TR(ps[b][:, kv * 128:(kv + 1) * 128], stg[s2][:, kv * 128:(kv + 1) * 128], [("stg", s2)], [("ps", b)])
                CP(evac_eng(), kbuf[:, :, sg["kb0"]:sg["kb0"] + 128], ps[b][:, :].rearrange("p (v t) -> p v t", v=4),
                   [("ps", b)], [("kbufh", kv_) for kv_ in range(4)])
                for j in range(2):
                    s = nstg()
                    DMA("sp", stg[s][0:64, 0:256], cache_v[bl, j * 64:(j + 1) * 64, :], (), [("stg", s)], "stg%d" % s)
                    for r_ in range(2):
                        CP("pool", vtok[0:64, sg["vb0"] + j, :].rearrange("p (v r d) -> p v r d", v=4, r=2)[:, :, r_, :],
                           stg[s][0:64, 0:256].rearrange("p (v d) -> p v d", v=4), [("stg", s)], [("vtok", sg["vb0"] + j)])
            load_T(lambda c0, nch, t0, n: hstate_s[:, c0:c0 + nch, 0:4], lambda c: "hstate_s", state_lru, 4, 512)

            kvctr = [0]

            def mk_kvslot(l):
                def f(sg):
                    slot = kvctr[0] % 2
                    kvctr[0] += 1
                    mem_load_sample(l, sg["bl"], slot)
                    return slot
                return f
            run_tile(segs, T, True, x_sample, y_sample, mk_kvslot(0), mk_kvslot(1), next_mem=0)
            store_T(lambda c, t0, n: hstate_s[:, c, 0:4], lambda c: "hstate_s", lru_s, 4, 512)

        for bl in range(nseq if STAGE >= 2 else 0):
            mem_phase(bl)
            for ti in range(NTI if STAGE >= 3 else 0):
                seg = dict(bl=bl, c0=0, n=TP, ucol=16, xcol=4, kb0=0, vb0=0, first=(ti == 0), last=(ti == SEQ // TP - 1),
                           hist_valid=(ti != 0))
                if ti == 0:
                    for g in range(4):
                        MSET("pool", up[:, g, 0:16], 0.0, [("uph", g)])
                        MSET("pool", ux[:, g, 0:4], 0.0, [("uxh", g)])
                run_tile([seg], TP, False, x_prompt[bl, ti * TP:(ti + 1) * TP, :], y_prompt[bl, ti * TP:(ti + 1) * TP, :],
                         lambda sg: 0, lambda sg: 1, next_mem=(bl + 1 if ti == SEQ // TP - 1 else None))
            if bl == nseq - 1:
                store_T(lambda c, t0, n: hstate[:, c, 0:4], lambda c: "hstate", lru_p, 4, 512)
        flush_pending()
        assert STAGE < 99 or wpos[0] == len(wseq), (wpos, len(wseq))

        chans = sorted(P.chan_n.keys())
        sems = {e_: es.enter_context(nc.semaphore("s_" + e_)) for e_ in COMPUTE}
        chan_sems = {c_: es.enter_context(nc.semaphore("d_" + c_)) for c_ in chans}
        block = es.enter_context(nc.Block())
        P.emit(nc, block, sems, chan_sems)
    return nc, len(P.ops)


_CACHE = {}


def _stack_vecs(inp):
    rows = []
    for nm in ["norm_mix", "norm_cross", "norm_mem", "norm_mlp"]:
        for l in range(2):
            rows.append(np.asarray(inp[nm][l]).reshape(8, 128))
    rows.append(np.asarray(inp["norm_final"]).reshape(8, 128))
    cw = np.asarray(inp["conv_w"])[0]
    for k in range(4):
        rows.append(cw[k].reshape(4, 128))
    for nm in ["conv_b", "b_rg_a", "b_rg_x", "rg_lambda", "pool_scale"]:
        rows.append(np.asarray(inp[nm])[0].reshape(4, 128))
    v = np.concatenate(rows, axis=0).astype(np.float32)
    out = np.zeros((128, 128), np.float32)
    out[:v.shape[0]] = v
    return out


def kernel(**inp):
    nseq = int(os.environ.get("KDEBUG_NSEQ", NB))
    do_sample = os.environ.get("KDEBUG_NOSAMPLE", "0") != "1"
    key = (nseq, do_sample)
    if key not in _CACHE:
        _CACHE[key] = build_program(nseq, do_sample)[0]
    nc = _CACHE[key]
    f = lambda a: np.ascontiguousarray(np.asarray(a, dtype=np.float32))
    shared = dict(
        vecs=_stack_vecs(inp), attn_sinks=f(inp["attn_sinks"]).reshape(1, 16), ident=np.eye(128, dtype=np.float32),
        w_in=f(inp["w_in_even"][0]), w_out=f(inp["w_out_even"][0]), w_qkv=f(inp["w_qkv_odd"][0]), w_o=f(inp["w_o_odd"][0]),
        w_mq=f(inp["w_mq"]), w_mk=f(inp["w_mk"]), w_mv=f(inp["w_mv"]), w_mo=f(inp["w_mo"]), w_up=f(inp["w_up"]),
        w_down=f(inp["w_down"]), pool_w=f(inp["pool_w"][0]), w_rg_a=f(inp["w_rg_a"][0]), w_rg_x=f(inp["w_rg_x"][0]),
    )
    in_maps = []
    for i in range(NCORE):
        sl = slice(i * NB, (i + 1) * NB)
        m = dict(shared)
        m.update(
            x_prompt=f(inp["x_prompt"][sl]), x_sample=f(inp["x_sample"][sl]).reshape(NB * DEC, D),
            state_pool=f(inp["state_pool"][0, sl]), state_conv=f(inp["state_conv"][0, sl]), state_lru=f(inp["state_lru"][0, sl]),
            cache_swa_k=f(inp["cache_swa_k"][0, sl]).reshape(NB, 128, 256), cache_swa_v=f(inp["cache_swa_v"][0, sl]).reshape(NB, 128, 256),
            cache_mem_k=f(inp["cache_mem_k"][:, sl]).reshape(2, NB, NMEM, D), cache_mem_v=f(inp["cache_mem_v"][:, sl]).reshape(2, NB, NMEM, D),
            mem_prompt=f(inp["mem_prompt"][sl]),
        )
        in_maps.append(m)
    res = run_bass_kernel_spmd(nc, in_maps, core_ids=list(range(NCORE)))
    R = res.results
    cat = lambda k, ax=0: np.concatenate([np.asarray(r[k]) for r in R], axis=ax)
    B = NCORE * NB
    y_prompt = cat("y_prompt")
    y_sample = cat("y_sample").reshape(B, DEC, D)
    pool_p = cat("pool_p")[None]
    conv_p = cat("conv_p")[None]
    lru_p = cat("lru_p")[None]
    swa_k_p = cat("swa_k_p").reshape(1, B, 128, 4, 64)
    swa_v_p = cat("swa_v_p").reshape(1, B, 128, 4, 64)
    mem_k_p = cat("mem_k_p", 1).reshape(2, B, NMEM, 4, 256)
    mem_v_p = cat("mem_v_p", 1).reshape(2, B, NMEM, 4, 256)
    pool_s = cat("pool_s")[None]
    conv_s = cat("conv_s")[None]
    lru_s = cat("lru_s")[None]
    swa_k_s = cat("swa_k_s").reshape(1, B, 128, 4, 64)
    swa_v_s = cat("swa_v_s").reshape(1, B, 128, 4, 64)
    return (y_prompt, y_sample, pool_p, conv_p, lru_p, swa_k_p, swa_v_p, mem_k_p, mem_v_p,
            pool_s, conv_s, lru_s, swa_k_s, swa_v_s)
```

```python
import os
import numpy as np
import concourse.bass as bass
import concourse.mybir as mybir
from concourse.bass_utils import run_bass_kernel_spmd

F32 = mybir.dt.float32
BF16 = mybir.dt.bfloat16
AF = mybir.ActivationFunctionType
ALU = mybir.AluOpType

NCORE = 8
D = 1024
KC = 8
TP = 512
SEQ = 2048
NB = 4
DEC = 64
NMEM = 256
EPS = 1e-6
NW = 4
GELU_K = 0.7978845608028654

COMPUTE = ("pe", "act", "dve", "pool")


class Prog:
    def __init__(self):
        self.ops = []
        self.lastw = {}
        self.readers = {}
        self.chan_n = {}

    def add(self, eng, fn, r=(), w=(), chan=None):
        i = len(self.ops)
        psr = [k for k in r if isinstance(k, tuple) and k[0] == "ps"]
        if psr:
            r = [k for k in r if not (isinstance(k, tuple) and k[0] == "ps")]
            w = list(w) + psr
        deps = set()
        raw = set()
        for k in r:
            j = self.lastw.get(k)
            if j is not None:
                deps.add(j)
                raw.add(j)
        for k in w:
            j = self.lastw.get(k)
            if j is not None:
                deps.add(j)
            for j in self.readers.get(k, ()):
                deps.add(j)
        deps.discard(i)
        cnt = None
        if chan is not None:
            cnt = self.chan_n.get(chan, 0) + 1
            self.chan_n[chan] = cnt
        self.ops.append(dict(eng=eng, fn=fn, deps=deps, raw=raw, chan=chan, cnt=cnt))
        for k in w:
            self.lastw[k] = i
            self.readers[k] = []
        for k in r:
            lst = self.readers.setdefault(k, [])
            if chan is None:
                lst[:] = [j for j in lst if not (self.ops[j]["chan"] is None and self.ops[j]["eng"] == eng)]
            lst.append(i)
        return i

    def emit(self, nc, block, sems, chan_sems):
        ops = self.ops
        for i, op in enumerate(ops):
            best = {}
            dmas = []
            for j in op["deps"]:
                d = ops[j]
                if d["chan"] is not None:
                    dmas.append(j)
                    continue
                if d["eng"] == op["eng"] and op["chan"] is None and op["eng"] == "pe":
                    continue
                if d["eng"] == op["eng"] and op["chan"] is not None:
                    pass
                if j > best.get(d["eng"], -1):
                    best[d["eng"]] = j
            op["wait_c"] = best
            op["wait_d"] = dmas
        sig = [False] * len(ops)
        for op in ops:
            for j in op["wait_c"].values():
                sig[j] = True
        counts = {e: 0 for e in COMPUTE}
        for i, op in enumerate(ops):
            if op["chan"] is None and sig[i]:
                counts[op["eng"]] += 1
                op["sigval"] = counts[op["eng"]]
        streams = {}
        for i, op in enumerate(ops):
            streams.setdefault(op["eng"], []).append(i)

        def run_stream(name, e):
            waited = {}
            for i in streams.get(name, []):
                op = ops[i]
                for eng2, j in op["wait_c"].items():
                    v = ops[j]["sigval"]
                    key = ("c", eng2)
                    if waited.get(key, 0) < v:
                        e.wait_ge(sems[eng2], v)
                        waited[key] = v
                for j in op["wait_d"]:
                    d = ops[j]
                    v = 16 * d["cnt"]
                    key = ("d", d["chan"])
                    if waited.get(key, 0) < v:
                        e.wait_ge(chan_sems[d["chan"]], v)
                        waited[key] = v
                if op["chan"] is not None:
                    v = 16 * (op["cnt"] - 1)
                    key = ("d", op["chan"])
                    if v > 0 and waited.get(key, 0) < v:
                        e.wait_ge(chan_sems[op["chan"]], v)
                        waited[key] = v
                ins = op["fn"](e)
                if op["chan"] is not None:
                    ins.then_inc(chan_sems[op["chan"]], 16)
                elif sig[i]:
                    ins.then_inc(sems[op["eng"]], 1)
            if name in ("sp", "act", "pool"):
                final = {}
                for i in streams.get(name, []):
                    op = ops[i]
                    if op["chan"] is not None:
                        final[op["chan"]] = max(final.get(op["chan"], 0), 16 * op["cnt"])
                for ch, v in final.items():
                    if waited.get(("d", ch), 0) < v:
                        e.wait_ge(chan_sems[ch], v)

        @block.sync
        def _(e):
            run_stream("sp", e)

        @block.tensor
        def _(e):
            run_stream("pe", e)

        @block.scalar
        def _(e):
            run_stream("act", e)

        @block.vector
        def _(e):
            run_stream("dve", e)

        @block.gpsimd
        def _(e):
            run_stream("pool", e)


def block_catalogue():
    blocks = []
    for l in range(2):
        if l == 0:
            blocks += [("in", 1), ("in", 0), ("in", 2)] + [("out", j) for j in range(2)]
        else:
            blocks += [("qkv", j) for j in range(4)] + [("wo", j) for j in range(2)]
        blocks += [("mq%d" % l, j) for j in range(2)] + [("mo%d" % l, j) for j in range(2)]
        for half in range(2):
            blocks += [("up%d" % l, half * 4 + j) for j in range(4)]
            blocks += [("down%d" % l, half * 4 + j) for j in range(4)]
    memb = []
    for l in range(2):
        memb += [("mk%d" % l, j) for j in range(2)] + [("mv%d" % l, j) for j in range(2)]
    return blocks, memb


def build_program(nseq=NB, do_sample=True):
    nc = bass.Bass("TRN2", target_bir_lowering=False)
    P = Prog()

    def din(name, shape):
        return nc.dram_tensor(name, shape, F32, kind="ExternalInput").ap()

    def dout(name, shape):
        return nc.dram_tensor(name, shape, F32, kind="ExternalOutput").ap()

    x_prompt = din("x_prompt", [NB, SEQ, D])
    x_sample = din("x_sample", [NB * DEC, D])
    state_pool = din("state_pool", [NB, 15, 512])
    state_conv = din("state_conv", [NB, 3, 512])
    state_lru = din("state_lru", [NB, 512])
    cache_k = din("cache_swa_k", [NB, 128, 256])
    cache_v = din("cache_swa_v", [NB, 128, 256])
    cmem_k = din("cache_mem_k", [2, NB, NMEM, D])
    cmem_v = din("cache_mem_v", [2, NB, NMEM, D])
    mem_prompt = din("mem_prompt", [NB, NMEM, D])
    vecs = din("vecs", [128, 128])
    sinks = din("attn_sinks", [1, 16])
    ident_d = din("ident", [128, 128])
    W = dict(
        w_in=din("w_in", [D, 1536]), w_out=din("w_out", [D, D]), w_qkv=din("w_qkv", [D, 1536]),
        w_o=din("w_o", [D, D]), w_mq=din("w_mq", [2, D, D]), w_mk=din("w_mk", [2, D, D]),
        w_mv=din("w_mv", [2, D, D]), w_mo=din("w_mo", [2, D, D]), w_up=din("w_up", [2, D, 4 * D]),
        w_down=din("w_down", [2, 4 * D, D]),
    )
    pool_w_d = din("pool_w", [4, 128, 128])
    w_rg_a_d = din("w_rg_a", [8, 64, 64])
    w_rg_x_d = din("w_rg_x", [8, 64, 64])

    y_prompt = dout("y_prompt", [NB, SEQ, D])
    y_sample = dout("y_sample", [NB * DEC, D])
    pool_p = dout("pool_p", [NB, 15, 512])
    conv_p = dout("conv_p", [NB, 3, 512])
    lru_p = dout("lru_p", [NB, 512])
    swa_k_p = dout("swa_k_p", [NB, 128, 256])
    swa_v_p = dout("swa_v_p", [NB, 128, 256])
    mem_k_p = dout("mem_k_p", [2, NB, NMEM, D])
    mem_v_p = dout("mem_v_p", [2, NB, NMEM, D])
    pool_s = dout("pool_s", [NB, 15, 512])
    conv_s = dout("conv_s", [NB, 3, 512])
    lru_s = dout("lru_s", [NB, 512])
    swa_k_s = dout("swa_k_s", [NB, 128, 256])
    swa_v_s = dout("swa_v_s", [NB, 128, 256])

    tile_blocks, mem_blocks = block_catalogue()
    all_blocks = tile_blocks + mem_blocks
    bid = {b: i for i, b in enumerate(all_blocks)}
    wscr = nc.dram_tensor("wscr", [len(all_blocks), 128, 4096], BF16, kind="Internal").ap()

    import contextlib
    es = contextlib.ExitStack()
    with es:
        def sb(name, shape, dt=F32):
            return es.enter_context(nc.sbuf_tensor(name, shape, dt))

        x = sb("x", [128, 8, TP])
        h = sb("h", [128, 8, TP], BF16)
        up = sb("up", [128, 4, 16 + TP])
        ux = sb("ux", [128, 4, 4 + TP])
        ug = sb("ug", [128, 4, TP])
        ycat = sb("ycat", [128, 8, TP], BF16)
        NTMP = 7
        tmp = [sb("tmp%d" % i, [128, 16 + TP]) for i in range(NTMP)]
        xc = sb("xc", [128, 4, TP])
        xcb = sb("xcb", [128, 4, TP], BF16)
        dbf = sb("dbf", [128, 4, TP], BF16)
        q = sb("q", [128, 8, TP], BF16)
        kbuf = sb("kbuf", [128, 4, 768], BF16)
        vtok = sb("vtok", [64, 12, 512], BF16)
        hid = sb("hid", [128, 16, TP], BF16)
        wring = [sb("wr%d" % i, [128, 4096], BF16) for i in range(NW)]
        memk = [sb("memk%d" % i, [128, 8, NMEM], BF16) for i in range(2)]
        memv = [sb("memv%d" % i, [128, 2, D], BF16) for i in range(2)]
        NSTG = 4
        stg = [sb("stg%d" % i, [128, 512]) for i in range(NSTG)]
        pT2 = [sb("pT2_%d" % i, [128, 2, TP], BF16) for i in range(2)]
        rden = [sb("rden%d" % i, [128, TP]) for i in range(2)]
        pTs = [sb("pTs%d" % i, [64, 3, 256], BF16) for i in range(2)]
        dns = [sb("dns%d" % i, [128, 256]) for i in range(2)]
        ident = sb("ident_sb", [128, 128])
        ones_bf = sb("ones_bf", [128, 128], BF16)
        ones_f = sb("ones_f", [128, 64])
        cvec = sb("cvec", [128, 128])
        negb = sb("negb", [128, 8])
        c8 = sb("c8", [128, 8])
        epsc = sb("epsc", [128, 1])
        onec = sb("onec", [128, 1])
        esink = sb("esink", [128, 16])
        esx = sb("esx", [64, 4, 256], BF16)
        poolw = sb("poolw", [128, 4, 128], BF16)
        wbd = sb("wbd", [128, 8, 128], BF16)
        invc = sb("invc", [128, 4, 16])
        hstate = sb("hstate", [128, 4, 4])
        hstate_s = sb("hstate_s", [128, 4, 4])
        tpad = sb("tpad", [128, 128])
        ps = [es.enter_context(nc.psum_tensor("ps%d" % i, [128, 512], F32)) for i in range(8)]

        COL = {}
        col = 0
        for nm, n in [("norm_mix0", 8), ("norm_mix1", 8), ("norm_cross0", 8), ("norm_cross1", 8),
                      ("norm_mem0", 8), ("norm_mem1", 8), ("norm_mlp0", 8), ("norm_mlp1", 8),
                      ("norm_final", 8), ("conv_w0", 4), ("conv_w1", 4), ("conv_w2", 4), ("conv_w3", 4),
                      ("conv_b", 4), ("b_rg_a", 4), ("b_rg_x", 4), ("rg_lambda", 4), ("pool_scale", 4)]:
            COL[nm] = col
            col += n
        assert col <= 128

        bank_ctr = [0]

        def nb():
            b = bank_ctr[0] % 8
            bank_ctr[0] += 1
            return b

        stg_ctr = [0]

        def nstg():
            s = stg_ctr[0] % NSTG
            stg_ctr[0] += 1
            return s

        def MM(out, lhsT, rhs, start, stop, r, w):
            P.add("pe", lambda e: e.matmul(out, lhsT=lhsT, rhs=rhs, start=start, stop=stop), r, w)

        def TR(out, in_, r, w):
            P.add("pe", lambda e: e.transpose(out, in_, ident[:, :]), list(r) + ["ident"], w)

        def ACT(out, in_, func, r, w, bias=None, scale=1.0):
            if bias is None:
                P.add("act", lambda e: e.activation(out=out, in_=in_, func=func, scale=scale), r, w)
            else:
                P.add("act", lambda e: e.activation(out=out, in_=in_, func=func, bias=bias, scale=scale), r, w)

        def TT(eng, out, in0, in1, op, r, w):
            P.add(eng, lambda e: e.tensor_tensor(out=out, in0=in0, in1=in1, op=op), r, w)

        def TS(eng, out, in0, s1, s2, op0, op1, r, w):
            if s2 is None:
                P.add(eng, lambda e: e.tensor_scalar(out=out, in0=in0, scalar1=s1, scalar2=None, op0=op0), r, w)
            else:
                P.add(eng, lambda e: e.tensor_scalar(out=out, in0=in0, scalar1=s1, scalar2=s2, op0=op0, op1=op1), r, w)

        def STT(out, in0, scalar, in1, op0, op1, r, w):
            P.add("dve", lambda e: e.scalar_tensor_tensor(out=out, in0=in0, scalar=scalar, in1=in1, op0=op0, op1=op1), r, w)

        def CP(eng, out, in_, r, w):
            if eng == "act":
                P.add("act", lambda e: e.copy(out=out, in_=in_), r, w)
            else:
                P.add(eng, lambda e: e.tensor_copy(out=out, in_=in_), r, w)

        def MSET(eng, ap, val, w):
            P.add(eng, lambda e: e.memset(ap, val), (), w)

        def DMA(queue, out, in_, r, w, chan):
            P.add(queue, lambda e: e.dma_start(out=out, in_=in_), r, w, chan=chan)

        evac_ctr = [0]

        def evac_eng():
            evac_ctr[0] += 1
            return "act" if evac_ctr[0] % 2 else "dve"

        def setup_consts():
            DMA("sp", ident[:, :], ident_d, (), ["ident"], "c_ident")
            DMA("sp", stg[0][:, 0:128], vecs, (), [("stg", 0)], "c_vecs")
            for i in range(1, NSTG):
                MSET("pool", stg[i][:, :], 0.0, [("stg", i)])
            MSET("pool", tpad[:, :], 0.0, ["tpad"])
            MSET("dve", ones_bf[:, :], 1.0, ["ones"])
            MSET("dve", ones_f[:, :], 1.0, ["ones_f"])
            MSET("dve", epsc[:, :], EPS, ["epsc"])
            MSET("dve", onec[:, :], 1.0, ["onec"])
            MSET("dve", hstate[:, :, :], 0.0, ["hstate"])
            b = nb()
            TR(ps[b][:, 0:128], stg[0][:, 0:128], [("stg", 0)], [("ps", b)])
            CP("dve", cvec[:, :], ps[b][:, 0:128], [("ps", b)], ["cvec"])
            ca = COL["b_rg_a"]
            TS("dve", negb[:, :], cvec[:, ca:ca + 8], -1.0, None, ALU.mult, None, ["cvec"], ["negb"])
            cl = COL["rg_lambda"]
            ACT(c8[:, 0:4], cvec[:, cl:cl + 4], AF.Exp, ["cvec"], ["c8"], scale=-1.0)
            ACT(c8[:, 0:4], c8[:, 0:4], AF.Ln, ["c8", "onec"], ["c8"], bias=onec[:, 0:1])
            TS("dve", c8[:, 4:8], c8[:, 0:4], -16.0, None, ALU.mult, None, ["c8"], ["c8b"])
            TS("dve", c8[:, 0:4], c8[:, 0:4], -8.0, None, ALU.mult, None, ["c8", "c8b"], ["c8"])
            DMA("sp", esink[:, :], sinks.partition_broadcast(128), (), ["esink"], "c_sink")
            ACT(esink[:, :], esink[:, :], AF.Exp, ["esink"], ["esink"])
            MSET("pool", esx[:, :, :], 0.0, ["esx"])
            for p0 in (0, 32):
                pr = slice(p0, p0 + 1)
                for kv in range(4):
                    for par in range(2):
                        for j2 in range(2):
                            g = kv * 4 + 2 * j2 + par
                            o = (par * 2 + j2) * 64
                            TS("dve", ug[pr, kv, o:o + 64], ones_f[pr, :], esink[pr, g:g + 1], None, ALU.mult, None,
                               ["esink", "ones_f"], [("ug", kv)])
                ugk = [("ug", kv) for kv in range(4)]
                if p0 == 0:
                    CP("dve", esx[pr, :, :], ug[pr, :, 0:256], ugk, ["esx"])
                else:
                    CP("dve", dbf[pr, :, 0:256], ug[pr, :, 0:256], ugk, [("dbf", 0)])
                    CP("dve", ug[pr, :, 256:512], dbf[pr, :, 0:256], [("dbf", 0)], ugk)
                    TT("dve", ug[pr, :, 256:512], ug[pr, :, 0:256], ug[pr, :, 256:512], ALU.subtract, ugk, ugk)
                    CP("dve", esx[pr, :, :], ug[pr, :, 256:512], ugk, ["esx"])
            DMA("sp", tmp[0][:, 0:512].rearrange("p (g e) -> p g e", g=4), pool_w_d.rearrange("g c e -> c g e"),
                (), [("tmp", 0)], "c_pw")
            CP("dve", poolw[:, :, :], tmp[0][:, 0:512].rearrange("p (g e) -> p g e", g=4), [("tmp", 0)], ["poolw"])
            for wi, wd in enumerate((w_rg_a_d, w_rg_x_d)):
                t = tmp[1 + wi]
                MSET("pool", t[:, 0:512], 0.0, [("tmp", 1 + wi)])
                tv = t[:, 0:512].rearrange("p (c j) -> p c j", c=4)
                src = wd.rearrange("(c r) i j -> r i c j", r=2)
                for r_ in range(2):
                    DMA("sp", tv[r_ * 64:(r_ + 1) * 64, :, r_ * 64:(r_ + 1) * 64], src[r_], (), [("tmp", 1 + wi)],
                        "c_bd%d%d" % (wi, r_))
                CP("dve", wbd[:, wi * 4:(wi + 1) * 4, :], tv, [("tmp", 1 + wi)], ["wbd"])
            for g in range(4):
                win = 2 << g
                for t_ in range(15):
                    MSET("pool", invc[:, g, t_:t_ + 1], 1.0 / min(t_ + 1, win), ["invc"])

        def wsrc(name, j, half):
            def std(Wm, j):
                src = Wm.rearrange("(k p) n -> p k n", p=128)[:, half * 4:(half + 1) * 4, j * 512:(j + 1) * 512]
                return [(lambda s: s.rearrange("p (k n) -> p k n", k=4), src)]
            if name == "in":
                return std(W["w_in"], j)
            if name == "out":
                return std(W["w_out"], j)
            if name == "wo":
                return std(W["w_o"], j)
            if name[:2] in ("mq", "mo", "mk", "mv", "up"):
                l = int(name[-1])
                return std(W["w_" + name[:-1]][l], j)
            if name.startswith("down"):
                l = int(name[-1])
                hh, cb = j // 4, j % 4
                src = W["w_down"][l].rearrange("(k p) n -> p k n", p=128)[
                    :, hh * 16 + half * 8: hh * 16 + half * 8 + 8, cb * 256:(cb + 1) * 256]
                return [(lambda s: s.rearrange("p (k n) -> p k n", k=8), src)]
            if name == "qkv":
                Wr = W["w_qkv"].rearrange("(k p) n -> p k n", p=128)
                if j < 2:
                    return std(W["w_qkv"], j)
                c0 = 1024 if j == 2 else 1280
                res = []
                for kk in range(4):
                    src = Wr[:, half * 4 + kk, c0:c0 + 256].rearrange("p (v d) -> p v d", v=4)
                    for r_ in range(2):
                        res.append((lambda s, r_=r_, kk=kk: s.rearrange("p (k v r d) -> p k v r d", k=4, v=4, r=2)[:, kk, :, r_, :], src))
                return res
            raise KeyError(name)

        def prepass():
            stage = [(x[:, 0:4, :].rearrange("p a b -> p (a b)"), [("x", c) for c in range(4)]),
                     (x[:, 4:8, :].rearrange("p a b -> p (a b)"), [("x", c) for c in range(4, 8)]),
                     (xc[:, :, :].rearrange("p a b -> p (a b)"), [("xc", c) for c in range(4)])]
            n = 0
            for bi, (name, j) in enumerate(all_blocks):
                for half in range(2):
                    sap, skeys = stage[n % 3]
                    for k_, (vf, src) in enumerate(wsrc(name, j, half)):
                        P.add("sp", (lambda e, o=vf(sap), s=src: e.dma_start(out=o, in_=s)), (), skeys,
                              chan="pp_ld%d_%d" % (n % 3, k_))
                    slot = (n // 2) % NW
                    dstv = wring[slot][:, half * 2048:(half + 1) * 2048]
                    eng = "dve" if n % 2 == 0 else "pool"
                    CP(eng, dstv, sap, skeys, [("w", slot, half)])
                    if half == 1:
                        P.add("act", (lambda e, o=wscr[bi], s=wring[slot][:, :]: e.dma_start(out=o, in_=s)),
                              [("w", slot, 0), ("w", slot, 1)], [("wscr", bi)], chan="pp_st%d" % slot)
                    n += 1

        DEBUG_ONDEMAND = int(os.environ.get("KDEBUG_STAGE", "99")) < 99
        MEMMASK = int(os.environ.get("KDEBUG_MEM", "255"))
        wseq = []
        wpos = [0, 0]

        def w_issue_upto(k):
            while wpos[1] < min(k, len(wseq)):
                i = wpos[1]
                b = wseq[i]
                if b is None:
                    break
                slot = i % NW
                full = None
                if not DEBUG_ONDEMAND and i < CONV_N:
                    name_, j_ = all_blocks[b]
                    if name_.startswith("down"):
                        l_ = int(name_[-1])
                        hh_, cb_ = j_ // 4, j_ % 4
                        full = (W["w_down"][l_].rearrange("(k p) n -> p k n", p=128)[:, hh_ * 16:(hh_ + 1) * 16, cb_ * 256:(cb_ + 1) * 256], 16)
                    elif not (name_ == "qkv" and j_ >= 2):
                        if name_ in ("in", "out"):
                            Wm_ = W["w_" + name_]
                        elif name_ == "wo":
                            Wm_ = W["w_o"]
                        elif name_ == "qkv":
                            Wm_ = W["w_qkv"]
                        else:
                            Wm_ = W["w_" + name_[:-1]][int(name_[-1])]
                        full = (Wm_.rearrange("(k p) n -> p k n", p=128)[:, :, j_ * 512:(j_ + 1) * 512], 8)
                if full is not None:
                    src_, kk_ = full
                    P.add("pool", (lambda e, o=wring[slot][:, :].rearrange("p (k n) -> p k n", k=kk_), s_=src_:
                                   e.dma_start(out=o, in_=s_)), (), [("w", slot, 0), ("w", slot, 1)], chan="cwf%d" % slot)
                elif not DEBUG_ONDEMAND and i < CONV_N:
                    name_, j_ = all_blocks[b]
                    for half in range(2):
                        sap = wring[slot][:, half * 2048:(half + 1) * 2048]
                        for k_, (vf, src) in enumerate(wsrc(name_, j_, half)):
                            P.add("pool", (lambda e, o=vf(sap), s_=src: e.dma_start(out=o, in_=s_)), (),
                                  [("w", slot, half)], chan="cw%d_%d_%d" % (slot, half, k_))
                else:
                    DMA("sp", wring[slot][:, :], wscr[b], [("wscr", b)], [("w", slot, 0), ("w", slot, 1)], "w%d" % slot)
                wpos[1] += 1

        CONV_N = len(all_blocks)

        def wnext(name, j):
            i = wpos[0]
            if not DEBUG_ONDEMAND and 0 < i <= CONV_N:
                pslot = (i - 1) % NW
                pb = wseq[i - 1]
                P.add("sp", (lambda e, o=wscr[pb], s_=wring[pslot][:, :]: e.dma_start(out=o, in_=s_)),
                      [("w", pslot, 0), ("w", pslot, 1)], [("wscr", pb)], chan="cst%d" % pslot)
            if DEBUG_ONDEMAND:
                while len(wseq) <= i:
                    wseq.append(None)
                wseq[i] = bid[(name, j)]
                w_issue_upto(i + 1)
            assert wseq[i] == bid[(name, j)], (i, name, j, all_blocks[wseq[i]])
            w_issue_upto(i + NW)
            wpos[0] += 1
            slot = i % NW
            return wring[slot], [("w", slot, 0), ("w", slot, 1)]

        def rmsnorm(src, skey, dst, dkey, gname, T, inplace_out=None, out_keys=None, second=None):
            for c in range(8):
                if c % 2 == 0:
                    ACT(dst(c), src(c), AF.Square, [skey(c)], [dkey(c)])
                else:
                    TT("dve", dst(c), src(c), src(c), ALU.mult, [skey(c)], [dkey(c)])
            b = nb()
            for c in range(8):
                MM(ps[b][:, 0:T], ones_bf[:, :], dst(c), c == 0, c == 7, [dkey(c), "ones"], [("ps", b)])
            t = tmp[6]
            ACT(t[:, 0:T], ps[b][:, 0:T], AF.Ln, [("ps", b), "epsc"], [("tmp", 6)], bias=epsc[:, 0:1], scale=1.0 / D)
            ACT(t[:, 0:T], t[:, 0:T], AF.Exp, [("tmp", 6)], [("tmp", 6)], scale=-0.5)
            g0 = COL[gname]
            for c in range(8):
                if inplace_out is None:
                    STT(dst(c), src(c), cvec[:, g0 + c:g0 + c + 1], t[:, 0:T], ALU.mult, ALU.mult,
                        [skey(c), ("tmp", 6), "cvec"], [dkey(c)])
                else:
                    STT(inplace_out(c), src(c), cvec[:, g0 + c:g0 + c + 1], t[:, 0:T], ALU.mult, ALU.mult,
                        [skey(c), ("tmp", 6), "cvec", dkey(c)], out_keys(c))
            if second is not None:
                dst2, dkey2, gname2 = second
                g2 = COL[gname2]
                for c in range(8):
                    STT(dst2(c), src(c), cvec[:, g2 + c:g2 + c + 1], t[:, 0:T], ALU.mult, ALU.mult,
                        [skey(c), ("tmp", 6), "cvec"], [dkey2(c)])

        def proj_fm(name, nblk, src, skey, T, evac, kc=8, cols=512, kouter=False):
            ncol_chunks = cols // 128
            for j in range(nblk):
                wt, wk = wnext(name, j)
                wv = wt[:, :].rearrange("p (k n) -> p k n", k=kc)
                if j == 0 and kouter:
                    banks = [nb() for _ in range(ncol_chunks)]
                    for k in range(kc):
                        for oc in range(ncol_chunks):
                            MM(ps[banks[oc]][:, 0:T], wv[:, k, oc * 128:(oc + 1) * 128], src(k), k == 0, k == kc - 1,
                               wk + [skey(k)], [("ps", banks[oc])])
                    for oc in range(ncol_chunks):
                        evac(j * ncol_chunks + oc, banks[oc])
                    continue
                for oc in range(ncol_chunks):
                    b = nb()
                    for k in range(kc):
                        MM(ps[b][:, 0:T], wv[:, k, oc * 128:(oc + 1) * 128], src(k), k == 0, k == kc - 1,
                           wk + [skey(k)], [("ps", b)])
                    evac(j * ncol_chunks + oc, b)

        def resid_evac(T):
            def f(oc, b):
                TT("dve", x[:, oc, 0:T], ps[b][:, 0:T], x[:, oc, 0:T], ALU.add, [("ps", b), ("x", oc)], [("x", oc)])
            return f

        def load_T(dstf, dkeyf, src2d, ntok, ncols, queue="sp"):
            for t0 in range(0, ntok, 128):
                n = min(128, ntok - t0)
                for c0 in range(0, ncols, 512):
                    w_ = min(512, ncols - c0)
                    s = nstg()
                    DMA(queue, stg[s][0:n, 0:w_], src2d[t0:t0 + n, c0:c0 + w_], (), [("stg", s)], "stg%d" % s)
                    b = nb()
                    nch = w_ // 128
                    for cc in range(nch):
                        TR(ps[b][:, cc * 128:(cc + 1) * 128], stg[s][:, cc * 128:(cc + 1) * 128], [("stg", s)], [("ps", b)])
                    src = ps[b][:, 0:nch * 128].rearrange("p (c t) -> p c t", c=nch)[:, :, 0:n]
                    CP(evac_eng(), dstf(c0 // 128, nch, t0, n), src, [("ps", b)],
                       [dkeyf(c0 // 128 + cc) for cc in range(nch)])

        def store_T(srcf, skeyf, dst2d, ntok, ncols, pad=False):
            for t0 in range(0, ntok, 128):
                n = min(128, ntok - t0)
                for c0 in range(0, ncols, 512):
                    w_ = min(512, ncols - c0)
                    nch = w_ // 128
                    b = nb()
                    for cc in range(nch):
                        c = c0 // 128 + cc
                        if n == 128:
                            TR(ps[b][:, cc * 128:(cc + 1) * 128], srcf(c, t0, n), [skeyf(c)], [("ps", b)])
                        else:
                            CP("dve", tpad[:, 0:n], srcf(c, t0, n), [skeyf(c)], ["tpad"])
                            TR(ps[b][:, cc * 128:(cc + 1) * 128], tpad[:, :], ["tpad"], [("ps", b)])
                    s = nstg()
                    CP(evac_eng(), stg[s][0:n, 0:w_], ps[b][0:n, 0:w_], [("ps", b)], [("stg", s)])
                    DMA("act", dst2d[t0:t0 + n, c0:c0 + w_], stg[s][0:n, 0:w_], [("stg", s)], [], "stg%d" % s)

        mem_front_done = set()

        def mem_front(bl):
            if bl in mem_front_done or bl >= nseq:
                return
            mem_front_done.add(bl)
            memx = lambda c: xc[:, :, :].rearrange("p a (h t) -> p (a h) t", h=2)[:, c, :]
            memxk = lambda c: ("xc", c // 2)
            mn0 = lambda c: xcb[:, :, :].rearrange("p a (h t) -> p (a h) t", h=2)[:, c, :]
            mn0k = lambda c: ("xcb", c // 2)
            mn1 = lambda c: dbf[:, :, :].rearrange("p a (h t) -> p (a h) t", h=2)[:, c, :]
            mn1k = lambda c: ("dbf", c // 2)
            xv = xc[:, :, :].rearrange("p a (h t) -> p (a h) t", h=2)
            load_T(lambda c0, nch, t0, n: xv[:, c0:c0 + nch, t0:t0 + n], memxk, mem_prompt[bl], NMEM, D)
            rmsnorm(memx, memxk, mn0, mn0k, "norm_mem0", NMEM, second=(mn1, mn1k, "norm_mem1"))

        def mem_phase(bl):
            mem_front(bl)
            for l in range(2 if MEMMASK & 2 else 0):
                if l == 0:
                    mn = lambda c: xcb[:, :, :].rearrange("p a (h t) -> p (a h) t", h=2)[:, c, :]
                    mnk = lambda c: ("xcb", c // 2)
                else:
                    mn = lambda c: dbf[:, :, :].rearrange("p a (h t) -> p (a h) t", h=2)[:, c, :]
                    mnk = lambda c: ("dbf", c // 2)
                if not (MEMMASK & 4):
                    continue
                for j in range(2):
                    wt, wk = wnext("mk%d" % l, j)
                    wv = wt[:, :].rearrange("p (k n) -> p k n", k=8)
                    for oc in range(4):
                        b = nb()
                        for k in range(8):
                            MM(ps[b][:, 0:NMEM], wv[:, k, oc * 128:(oc + 1) * 128], mn(k), k == 0, k == 7,
                               wk + [mnk(k)], [("ps", b)])
                        CP(evac_eng(), memk[l][:, j * 4 + oc, :], ps[b][:, 0:NMEM], [("ps", b)], [("memk", l)])
                    for tb in range(2 if MEMMASK & 8 else 0):
                        b = nb()
                        for k in range(8):
                            MM(ps[b][:, :], mn(k)[:, tb * 128:(tb + 1) * 128], wv[:, k, :], k == 0, k == 7,
                               wk + [mnk(k)], [("ps", b)])
                        s = nstg()
                        CP(evac_eng(), stg[s][:, :], ps[b][:, :], [("ps", b)], [("stg", s)])
                        DMA("act", mem_k_p[l, bl, tb * 128:(tb + 1) * 128, j * 512:(j + 1) * 512], stg[s][:, :],
                            [("stg", s)], [], "stg%d" % s)
                for j in range(2 if MEMMASK & 16 else 0):
                    wt, wk = wnext("mv%d" % l, j)
                    wv = wt[:, :].rearrange("p (k n) -> p k n", k=8)
                    for tb in range(2):
                        b = nb()
                        for k in range(8):
                            MM(ps[b][:, :], mn(k)[:, tb * 128:(tb + 1) * 128], wv[:, k, :], k == 0, k == 7,
                               wk + [mnk(k)], [("ps", b)])
                        s = nstg()
                        CP("act", stg[s][:, :], ps[b][:, :], [("ps", b)], [("stg", s)])
                        CP("dve", memv[l][:, tb, j * 512:(j + 1) * 512], ps[b][:, :], [("ps", b)], [("memv", l)])
                        DMA("act", mem_v_p[l, bl, tb * 128:(tb + 1) * 128, j * 512:(j + 1) * 512], stg[s][:, :],
                            [("stg", s)], [], "stg%d" % s)

        def mem_load_sample(l, bl, slot):
            for tb in range(2):
                for c0 in range(0, D, 512):
                    s = nstg()
                    DMA("sp", stg[s][:, :], cmem_k[l, bl, tb * 128:(tb + 1) * 128, c0:c0 + 512], (), [("stg", s)], "stg%d" % s)
                    b = nb()
                    for cc in range(4):
                        TR(ps[b][:, cc * 128:(cc + 1) * 128], stg[s][:, cc * 128:(cc + 1) * 128], [("stg", s)], [("ps", b)])
                    CP(evac_eng(), memk[slot][:, c0 // 128:c0 // 128 + 4, tb * 128:(tb + 1) * 128],
                       ps[b][:, :].rearrange("p (c t) -> p c t", c=4), [("ps", b)], [("memk", slot)])
                    s = nstg()
                    DMA("sp", stg[s][:, :], cmem_v[l, bl, tb * 128:(tb + 1) * 128, c0:c0 + 512], (), [("stg", s)], "stg%d" % s)
                    CP(evac_eng(), memv[slot][:, tb, c0:c0 + 512], stg[s][:, :], [("stg", s)], [("memv", slot)])

        def cross_attn(l, segs, T, kvslot_of):
            rmsnorm(lambda c: x[:, c, 0:T], lambda c: ("x", c), lambda c: h[:, c, 0:T], lambda c: ("h", c),
                    "norm_cross%d" % l, T)

            def qev(oc, b):
                CP(evac_eng(), q[:, oc, 0:T], ps[b][:, 0:T], [("ps", b)], [("q", oc)])
            proj_fm("mq%d" % l, 2, lambda k: h[:, k, 0:T], lambda k: ("h", k), T, qev, kouter=True)
            units = [(si, sg, hd) for si, sg in enumerate(segs) for hd in range(4)]
            slots = {}

            def ca_scores(ui):
                si, sg, hd = units[ui]
                if si not in slots:
                    slots[si] = kvslot_of(sg)
                slot = slots[si]
                c0, n = sg["c0"], sg["n"]
                pt = pT2[ui % 2]
                ptk = ("pT2", ui % 2)
                for kb in range(2):
                    b = nb()
                    for dc in range(2):
                        MM(ps[b][:, 0:n], memk[slot][:, 2 * hd + dc, kb * 128:(kb + 1) * 128],
                           q[:, 2 * hd + dc, c0:c0 + n], dc == 0, dc == 1,
                           [("memk", slot), ("q", 2 * hd + dc)], [("ps", b)])
                    ACT(pt[:, kb, 0:n], ps[b][:, 0:n], AF.Exp, [("ps", b)], [ptk], scale=1.0 / 16.0)

            def ca_pv(ui):
                si, sg, hd = units[ui]
                slot = slots[si]
                c0, n = sg["c0"], sg["n"]
                pt = pT2[ui % 2]
                ptk = ("pT2", ui % 2)
                rd = rden[ui % 2]
                rdk = ("rden", ui % 2)
                b = nb()
                for kb in range(2):
                    MM(ps[b][:, 0:n], ones_bf[:, :], pt[:, kb, 0:n], kb == 0, kb == 1, [ptk, "ones"], [("ps", b)])
                ACT(rd[:, 0:n], ps[b][:, 0:n], AF.Ln, [("ps", b)], [rdk])
                ACT(rd[:, 0:n], rd[:, 0:n], AF.Exp, [rdk], [rdk], scale=-1.0)
                for dc in range(2):
                    b = nb()
                    for kb in range(2):
                        MM(ps[b][:, 0:n], memv[slot][:, kb, hd * 256 + dc * 128: hd * 256 + (dc + 1) * 128],
                           pt[:, kb, 0:n], kb == 0, kb == 1, [("memv", slot), ptk], [("ps", b)])
                    TT("dve", ycat[:, 2 * hd + dc, c0:c0 + n], ps[b][:, 0:n], rd[:, 0:n], ALU.mult,
                       [("ps", b), rdk], [("ycat", 2 * hd + dc)])

            ca_scores(0)
            for ui in range(len(units)):
                if ui + 1 < len(units):
                    ca_scores(ui + 1)
                ca_pv(ui)
            proj_fm("mo%d" % l, 2, lambda k: ycat[:, k, 0:T], lambda k: ("ycat", k), T, resid_evac(T), kouter=True)

        def mlp(l, T):
            rmsnorm(lambda c: x[:, c, 0:T], lambda c: ("x", c), lambda c: h[:, c, 0:T], lambda c: ("h", c),
                    "norm_mlp%d" % l, T)
            rr = [0]
            for half in range(2):
                for j in range(4):
                    wt, wk = wnext("up%d" % l, half * 4 + j)
                    wv = wt[:, :].rearrange("p (k n) -> p k n", k=8)
                    kob = None
                    if half == 0 and j == 0:
                        kob = [nb() for _ in range(4)]
                        for k in range(8):
                            for oc in range(4):
                                MM(ps[kob[oc]][:, 0:T], wv[:, k, oc * 128:(oc + 1) * 128], h[:, k, 0:T], k == 0, k == 7,
                                   wk + [("h", k)], [("ps", kob[oc])])
                    for oc in range(4):
                        if kob is not None:
                            b = kob[oc]
                        else:
                            b = nb()
                            for k in range(8):
                                MM(ps[b][:, 0:T], wv[:, k, oc * 128:(oc + 1) * 128], h[:, k, 0:T], k == 0, k == 7,
                                   wk + [("h", k)], [("ps", b)])
                        ti = rr[0] % 4
                        rr[0] += 1
                        t = tmp[ti]
                        ACT(t[:, 0:T], ps[b][:, 0:T], AF.Relu, [("ps", b)], [("tmp", ti)])
                        TT("dve", hid[:, j * 4 + oc, 0:T], t[:, 0:T], t[:, 0:T], ALU.mult, [("tmp", ti)], [("hid", j * 4 + oc)])
                for cb in range(4):
                    wt, wk = wnext("down%d" % l, half * 4 + cb)
                    wv = wt[:, :].rearrange("p (k n) -> p k n", k=16)
                    kob = None
                    if cb == 0:
                        kob = [nb(), nb()]
                        for k in range(16):
                            for cl in range(2):
                                MM(ps[kob[cl]][:, 0:T], wv[:, k, cl * 128:(cl + 1) * 128], hid[:, k, 0:T], k == 0, k == 15,
                                   wk + [("hid", k)], [("ps", kob[cl])])
                    for cl in range(2):
                        oc = cb * 2 + cl
                        if kob is not None:
                            b = kob[cl]
                        else:
                            b = nb()
                            for k in range(16):
                                MM(ps[b][:, 0:T], wv[:, k, cl * 128:(cl + 1) * 128], hid[:, k, 0:T], k == 0, k == 15,
                                   wk + [("hid", k)], [("ps", b)])
                        TT("dve", x[:, oc, 0:T], ps[b][:, 0:T], x[:, oc, 0:T], ALU.add, [("ps", b), ("x", oc)], [("x", oc)])

        def even_layer(segs, T, sample):
            rmsnorm(lambda c: x[:, c, 0:T], lambda c: ("x", c), lambda c: h[:, c, 0:T], lambda c: ("h", c),
                    "norm_mix0", T)
            hs = hstate_s if sample else hstate
            hsk = "hstate_s" if sample else "hstate"
            def hidf(i):
                return hid[:, 2 * i:2 * i + 2, :].bitcast(F32).rearrange("p a b -> p (a b)"), [("hid", 2 * i), ("hid", 2 * i + 1)]
            def qf(i):
                return q[:, 2 * i:2 * i + 2, :].bitcast(F32).rearrange("p a b -> p (a b)"), [("q", 2 * i), ("q", 2 * i + 1)]
            TA = [hidf(c) for c in range(4)]
            TB = [hidf(4 + c) for c in range(4)]
            TC = [qf(c) for c in range(4)]
            TG = [(tmp[i][:, 0:TP], [("tmp", i)]) for i in (0, 1, 2, 6)]
            R4 = range(4)

            def in_block(j):
                wt, wk = wnext("in", j)
                wv = wt[:, :].rearrange("p (k n) -> p k n", k=8)
                kob = None
                if j == 1:
                    kob = [nb() for _ in range(4)]
                    for k in range(8):
                        for oc4 in range(4):
                            MM(ps[kob[oc4]][:, 0:T], wv[:, k, oc4 * 128:(oc4 + 1) * 128], h[:, k, 0:T], k == 0, k == 7,
                               wk + [("h", k)], [("ps", kob[oc4])])
                for oc4 in range(4):
                    if kob is not None:
                        b = kob[oc4]
                    else:
                        b = nb()
                        for k in range(8):
                            MM(ps[b][:, 0:T], wv[:, k, oc4 * 128:(oc4 + 1) * 128], h[:, k, 0:T], k == 0, k == 7,
                               wk + [("h", k)], [("ps", b)])
                    if j == 0:
                        for sg in segs:
                            CP("act", up[:, oc4, sg["ucol"]:sg["ucol"] + sg["n"]], ps[b][:, sg["c0"]:sg["c0"] + sg["n"]],
                               [("ps", b)], [("up", oc4)])
                    elif j == 1:
                        for sg in segs:
                            CP("act", ux[:, oc4, sg["xcol"]:sg["xcol"] + sg["n"]], ps[b][:, sg["c0"]:sg["c0"] + sg["n"]],
                               [("ps", b)], [("ux", oc4)])
                    else:
                        CP("act", ug[:, oc4, 0:T], ps[b][:, 0:T], [("ps", b)], [("ug", oc4)])

            in_block(1)
            for c in R4:
                cw = [COL["conv_w%d" % k] + c for k in range(4)]
                cbc = COL["conv_b"] + c
                for sg in segs:
                    xo, n, c0 = sg["xcol"], sg["n"], sg["c0"]
                    TS("dve", xc[:, c, c0:c0 + n], ux[:, c, xo:xo + n], cvec[:, cw[3]:cw[3] + 1], cvec[:, cbc:cbc + 1],
                       ALU.mult, ALU.add, [("ux", c), "cvec"], [("xc", c)])
                    for k in (2, 1, 0):
                        sh = 3 - k
                        STT(xc[:, c, c0:c0 + n], ux[:, c, xo - sh:xo - sh + n], cvec[:, cw[k]:cw[k] + 1], xc[:, c, c0:c0 + n],
                            ALU.mult, ALU.add, [("ux", c), ("uxh", c), "cvec", ("xc", c)], [("xc", c)])
                CP("act", xcb[:, c, 0:T], xc[:, c, 0:T], [("xc", c)], [("xcb", c)])
            in_block(0)
            flush_pending()
            for g in range(4):
                Wn = 2 << g
                for sg in segs:
                    uc, n, c0 = sg["ucol"], sg["n"], sg["c0"]
                    cur = lambda lo, hi, g=g, uc=uc: up[:, g, uc + lo:uc + hi]
                    curk = [("up", g), ("uph", g)]
                    for lev in range(1, g + 2):
                        sh = 1 << (lev - 1)
                        lo = -(Wn - (1 << lev))
                        ti = 4 + (lev % 2)
                        o = tmp[ti]
                        TT("dve", o[:, 16 + lo:16 + n], cur(lo, n), cur(lo - sh, n - sh), ALU.add, curk, [("tmp", ti)])
                        cur = lambda lo_, hi_, o=o: o[:, 16 + lo_:16 + hi_]
                        curk = [("tmp", ti)]
                    STT(dbf[:, g, c0:c0 + n], cur(0, n), 1.0 / Wn, up[:, g, uc:uc + n], ALU.mult, ALU.subtract,
                        curk + [("up", g)], [("dbf", g)])
                    if sg["first"] and not sample:
                        m = Wn - 1
                        t6 = tmp[3]
                        TT("dve", t6[:, 0:m], cur(0, m), invc[:, g, 0:m], ALU.mult, curk + ["invc"], [("tmp", 3)])
                        TT("dve", dbf[:, g, c0:c0 + m], t6[:, 0:m], up[:, g, uc:uc + m], ALU.subtract,
                           [("tmp", 3), ("up", g)], [("dbf", g)])
            in_block(2)
            for c in R4:
                G, gk = TG[c]
                u_ = ug[:, c, 0:T]
                ACT(G[:, 0:T], u_, AF.Square, [("ug", c)], gk)
                ACT(G[:, 0:T], G[:, 0:T], AF.Identity, gk + ["onec"], gk, bias=onec[:, 0:1], scale=0.044715)
                TT("dve", G[:, 0:T], G[:, 0:T], u_, ALU.mult, gk + [("ug", c)], gk)
            ba = COL["b_rg_a"]
            bx = COL["b_rg_x"]
            for c in R4:
                A, ak = TA[c]
                b_ = nb()
                MM(ps[b_][:, 0:T], wbd[:, c, :], xcb[:, c, 0:T], True, True, ["wbd", ("xcb", c)], [("ps", b_)])
                ACT(A[:, 0:T], ps[b_][:, 0:T], AF.Sigmoid, [("ps", b_), "cvec"], ak, bias=cvec[:, ba + c:ba + c + 1])
            for c in R4:
                Bt, bk = TB[c]
                b_ = nb()
                MM(ps[b_][:, 0:T], wbd[:, 4 + c, :], xcb[:, c, 0:T], True, True, ["wbd", ("xcb", c)], [("ps", b_)])
                ACT(Bt[:, 0:T], ps[b_][:, 0:T], AF.Sigmoid, [("ps", b_), "cvec"], bk, bias=cvec[:, bx + c:bx + c + 1])
            for c in R4:
                G, gk = TG[c]
                ACT(G[:, 0:T], G[:, 0:T], AF.Sigmoid, gk, gk, scale=2.0 * GELU_K)
            for g in range(4):
                b = nb()
                MM(ps[b][:, 0:T], poolw[:, g, :], dbf[:, g, 0:T], True, True, ["poolw", ("dbf", g)], [("ps", b)])
                pc = COL["pool_scale"] + g
                TS("dve", ycat[:, g, 0:T], ps[b][:, 0:T], cvec[:, pc:pc + 1], None, ALU.mult, None, [("ps", b), "cvec"], [("ycat", g)])
            for pair in ((0, 1), (2, 3)):
                for c in pair:
                    A, ak = TA[c]
                    C, ck = TC[c]
                    ACT(C[:, 0:T], A[:, 0:T], AF.Exp, ak + ["c8b"], ck, scale=c8[:, 4 + c:5 + c])
                for c in pair:
                    A, ak = TA[c]
                    ACT(A[:, 0:T], A[:, 0:T], AF.Exp, ak + ["c8"], ak, scale=c8[:, c:c + 1])
                for c in pair:
                    C, ck = TC[c]
                    ACT(C[:, 0:T], C[:, 0:T], AF.Ln, ck + ["onec"], ck, bias=onec[:, 0:1], scale=-1.0)
                for c in pair:
                    C, ck = TC[c]
                    Bt, bk = TB[c]
                    ACT(C[:, 0:T], C[:, 0:T], AF.Exp, ck, ck, scale=0.5)
                    TT("dve", Bt[:, 0:T], Bt[:, 0:T], C[:, 0:T], ALU.mult, bk + ck, bk)
                    TT("dve", Bt[:, 0:T], Bt[:, 0:T], xc[:, c, 0:T], ALU.mult, bk + [("xc", c)], bk)
                for c in pair:
                    A, ak = TA[c]
                    Bt, bk = TB[c]
                    C, ck = TC[c]
                    G, gk = TG[c]
                    for sg in segs:
                        n, c0, bl = sg["n"], sg["c0"], sg["bl"]
                        P.add("dve", (lambda e, o=C[:, c0:c0 + n], a_=A[:, c0:c0 + n], bb=Bt[:, c0:c0 + n],
                                      ini=hs[:, c, bl:bl + 1]:
                                      e.tensor_tensor_scan(out=o, data0=a_, data1=bb, initial=ini, op0=ALU.mult, op1=ALU.add)),
                              ak + bk + [hsk] + ck, ck)
                        CP("pool", hs[:, c, bl:bl + 1], C[:, c0 + n - 1:c0 + n], ck, [hsk])
                    TT("dve", G[:, 0:T], G[:, 0:T], ug[:, c, 0:T], ALU.mult, gk + [("ug", c)], gk)
                    TT("dve", ycat[:, 4 + c, 0:T], C[:, 0:T], G[:, 0:T], ALU.mult, ck + gk, [("ycat", 4 + c)])

            for sg in segs:
                uc, xo, n, bl = sg["ucol"], sg["xcol"], sg["n"], sg["bl"]
                if sg["last"]:
                    pd = pool_s if sample else pool_p
                    cd = conv_s if sample else conv_p
                    store_T(lambda c, t0, nn: up[:, c, uc + n - 15:uc + n], lambda c: ("up", c), pd[bl], 15, 512)
                    store_T(lambda c, t0, nn: ux[:, c, xo + n - 3:xo + n], lambda c: ("ux", c), cd[bl], 3, 512)
                else:
                    for g in range(4):
                        CP("pool", up[:, g, uc - 15:uc], up[:, g, uc + n - 15:uc + n], [("up", g)], [("uph", g)])
                        CP("pool", ux[:, g, xo - 3:xo], ux[:, g, xo + n - 3:xo + n], [("ux", g)], [("uxh", g)])
            proj_fm("out", 2, lambda k: ycat[:, k, 0:T], lambda k: ("ycat", k), T, resid_evac(T), kouter=True)

        def odd_layer(segs, T, sample):
            rmsnorm(lambda c: x[:, c, 0:T], lambda c: ("x", c), lambda c: h[:, c, 0:T], lambda c: ("h", c),
                    "norm_mix1", T)

            def qev(oc, b):
                CP(evac_eng(), q[:, oc, 0:T], ps[b][:, 0:T], [("ps", b)], [("q", oc)])
            proj_fm("qkv", 2, lambda k: h[:, k, 0:T], lambda k: ("h", k), T, qev, kouter=True)
            wt, wk = wnext("qkv", 2)
            wv = wt[:, :].rearrange("p (k n) -> p k n", k=8)
            for kv in range(4):
                b = nb()
                for k in range(8):
                    MM(ps[b][:, 0:T], wv[:, k, kv * 128:(kv + 1) * 128], h[:, k, 0:T], k == 0, k == 7,
                       wk + [("h", k)], [("ps", b)])
                for sg in segs:
                    kc0 = sg["kb0"] + 128
                    CP(evac_eng(), kbuf[:, kv, kc0:kc0 + sg["n"]], ps[b][:, sg["c0"]:sg["c0"] + sg["n"]],
                       [("ps", b)], [("kbuf", kv)])
            wv5 = wt[:, :].rearrange("p (k v r d) -> p k v r d", k=8, v=4, r=2)
            for sg in segs:
                if not sg["last"]:
                    continue
                bl, n, c0 = sg["bl"], sg["n"], sg["c0"]
                nrow = min(128, n)
                t0 = c0 + n - nrow
                b = nb()
                for k in range(8):
                    MM(ps[b][0:nrow, 0:256].rearrange("p (v d) -> p v d", v=4), h[:, k, t0:t0 + nrow], wv5[:, k, :, 0, :],
                       k == 0, k == 7, wk + [("h", k)], [("ps", b)])
                s = nstg()
                CP(evac_eng(), stg[s][0:nrow, 0:256], ps[b][0:nrow, 0:256], [("ps", b)], [("stg", s)])
                kd = swa_k_s if sample else swa_k_p
                DMA("act", kd[bl, 128 - nrow:128, :], stg[s][0:nrow, 0:256], [("stg", s)], [], "stg%d" % s)
                if sample:
                    DMA("sp", swa_k_s[bl, 0:64, :], cache_k[bl, 64:128, :], (), [], "d2d")
            wt, wk = wnext("qkv", 3)
            wv = wt[:, :].rearrange("p (k n) -> p k n", k=8)
            for sg in segs:
                n, c0, bl = sg["n"], sg["c0"], sg["bl"]
                for j in range(n // 64):
                    b = nb()
                    for k in range(8):
                        MM(ps[b][0:64, :], h[:, k, c0 + j * 64:c0 + (j + 1) * 64], wv[:, k, :], k == 0, k == 7,
                           wk + [("h", k)], [("ps", b)])
                    vs = sg["vb0"] + 2 + j
                    CP(evac_eng(), vtok[0:64, vs, :], ps[b][0:64, :], [("ps", b)], [("vtok", vs)])
                    if sg["last"] and j >= n // 64 - 2:
                        s = nstg()
                        CP(evac_eng(), stg[s][0:64, 0:256].rearrange("p (v d) -> p v d", v=4),
                           ps[b][0:64, :].rearrange("p (v r d) -> p v r d", v=4, r=2)[:, :, 0, :], [("ps", b)], [("stg", s)])
                        vd = swa_v_s if sample else swa_v_p
                        row0 = 128 - (n // 64 - j) * 64
                        DMA("act", vd[bl, row0:row0 + 64, :], stg[s][0:64, 0:256], [("stg", s)], [], "stg%d" % s)
                if sample:
                    DMA("sp", swa_v_s[bl, 0:64, :], cache_v[bl, 64:128, :], (), [], "d2d")
            units = []
            for sg in segs:
                for nq in range(sg["n"] // 64):
                    for kv in range(4):
                        units.append((sg, nq, kv))

            def unit_info(ui):
                sg, nq, kv = units[ui]
                ext = [e_ for e_ in (nq, nq + 1, nq + 2) if (e_ >= 2 or sg["hist_valid"])]
                return sg, nq, kv, ext, sg["c0"] + nq * 64, pTs[ui % 2], ("pTs", ui % 2), dns[ui % 2], ("dns", ui % 2)

            def swa_scores(ui):
                sg, nq, kv, ext, qc0, pt, ptk, dn, dnk = unit_info(ui)
                ne = len(ext)
                for par in range(2):
                    b = nb()
                    for ji, e_ in enumerate(ext):
                        kcol = sg["kb0"] + e_ * 64
                        MM(ps[b][0:64, ji * 128:(ji + 1) * 128].rearrange("p (a l) -> p a l", a=2),
                           kbuf[par * 64:(par + 1) * 64, kv, kcol:kcol + 64],
                           q[par * 64:(par + 1) * 64, 2 * kv:2 * kv + 2, qc0:qc0 + 64], True, True,
                           [("kbuf", kv), ("kbufh", kv), ("q", 2 * kv), ("q", 2 * kv + 1)], [("ps", b)])
                    ACT(pt[0:64, 0:ne, par * 128:(par + 1) * 128],
                        ps[b][0:64, 0:ne * 128].rearrange("p (j c) -> p j c", j=ne), AF.Exp, [("ps", b)], [ptk], scale=0.125)

            def swa_pv(ui):
                sg, nq, kv, ext, qc0, pt, ptk, dn, dnk = unit_info(ui)
                bd = nb()
                MM(ps[bd][:, 0:256], ones_bf[0:64, :], esx[0:64, kv, :], True, False, ["ones", "esx"], [("ps", bd)])
                for ji in range(len(ext)):
                    MM(ps[bd][:, 0:256], ones_bf[0:64, :], pt[0:64, ji, :], False, ji == len(ext) - 1,
                       ["ones", ptk], [("ps", bd)])
                ACT(dn[:, :], ps[bd][:, 0:256], AF.Ln, [("ps", bd)], [dnk])
                ACT(dn[:, :], dn[:, :], AF.Exp, [dnk], [dnk], scale=-1.0)
                bo = nb()
                for ji, e_ in enumerate(ext):
                    vs = sg["vb0"] + e_
                    MM(ps[bo][:, 0:256], vtok[0:64, vs, kv * 128:(kv + 1) * 128], pt[0:64, ji, :], ji == 0,
                       ji == len(ext) - 1, [("vtok", vs), ptk], [("ps", bo)])
                for par in range(2):
                    sl = slice(par * 64, (par + 1) * 64)
                    TT("dve", ycat[sl, 2 * kv:2 * kv + 2, qc0:qc0 + 64],
                       ps[bo][sl, par * 128:(par + 1) * 128].rearrange("p (a l) -> p a l", a=2),
                       dn[sl, par * 128:(par + 1) * 128].rearrange("p (a l) -> p a l", a=2), ALU.mult,
                       [("ps", bo), dnk], [("ycat", 2 * kv), ("ycat", 2 * kv + 1)])

            swa_scores(0)
            for ui in range(len(units)):
                if ui + 1 < len(units):
                    swa_scores(ui + 1)
                swa_pv(ui)
            for sg in segs:
                if sample or sg["last"]:
                    continue
                kb0, n, vb0 = sg["kb0"], sg["n"], sg["vb0"]
                for kv in range(4):
                    CP("pool", kbuf[:, kv, kb0:kb0 + 128], kbuf[:, kv, kb0 + n:kb0 + n + 128], [("kbuf", kv)], [("kbufh", kv)])
                for j in range(2):
                    CP("pool", vtok[0:64, vb0 + j, :], vtok[0:64, vb0 + n // 64 + j, :], [("vtok", vb0 + n // 64 + j)],
                       [("vtok", vb0 + j)])
            proj_fm("wo", 2, lambda k: ycat[:, k, 0:T], lambda k: ("ycat", k), T, resid_evac(T), kouter=True)

        pending = []

        def flush_pending():
            while pending:
                pending.pop(0)()

        def final_norm_store(T, dst2d):
            yb = lambda c: hid[:, 2 * c:2 * c + 2, :].bitcast(F32).rearrange("p a b -> p (a b)")
            ybk = lambda c: [("hid", 2 * c), ("hid", 2 * c + 1)]
            rmsnorm(lambda c: x[:, c, 0:T], lambda c: ("x", c), lambda c: h[:, c, 0:T], lambda c: ("h", c),
                    "norm_final", T, inplace_out=lambda c: yb(c)[:, 0:T], out_keys=ybk)

            def do_store():
                for t0 in range(0, T, 128):
                    for c0 in range(0, D, 512):
                        b = nb()
                        for cc in range(4):
                            c = c0 // 128 + cc
                            TR(ps[b][:, cc * 128:(cc + 1) * 128], yb(c)[:, t0:t0 + 128], ybk(c), [("ps", b)])
                        s_ = nstg()
                        CP(evac_eng(), stg[s_][:, :], ps[b][:, :], [("ps", b)], [("stg", s_)])
                        DMA("act", dst2d[t0:t0 + 128, c0:c0 + 512], stg[s_][:, :], [("stg", s_)], [], "stg%d" % s_)
            pending.append(do_store)

        def run_tile(segs, T, sample, src2d, dst2d, kvslot0, kvslot1, next_mem=None):
            load_T(lambda c0, nch, t0, n: x[:, c0:c0 + nch, t0:t0 + n], lambda c: ("x", c), src2d, T, D)
            even_layer(segs, T, sample)
            cross_attn(0, segs, T, kvslot0)
            mlp(0, T)
            odd_layer(segs, T, sample)
            cross_attn(1, segs, T, kvslot1)
            if next_mem is not None and STAGE >= 99:
                mem_front(next_mem)
            mlp(1, T)
            final_norm_store(T, dst2d)

        tb_ids = [bid[b] for b in tile_blocks]
        mb_ids = [bid[b] for b in mem_blocks]
        if do_sample and not DEBUG_ONDEMAND:
            wseq.extend(tb_ids)
        for s_ in range(nseq if not DEBUG_ONDEMAND else 0):
            wseq.extend(mb_ids)
            for _ in range(SEQ // TP):
                wseq.extend(tb_ids)

        STAGE = int(os.environ.get("KDEBUG_STAGE", "99"))
        NTI = int(os.environ.get("KDEBUG_NTILE", str(SEQ // TP)))
        if not DEBUG_ONDEMAND:
            w_issue_upto(NW)
        setup_consts()
        if DEBUG_ONDEMAND and STAGE >= 1:
            prepass()
        if not DEBUG_ONDEMAND:
            assert sorted(wseq[:CONV_N]) == list(range(CONV_N))
        if do_sample and STAGE >= 4:
            T = NB * DEC
            segs = []
            for bl in range(NB):
                segs.append(dict(bl=bl, c0=bl * DEC, n=DEC, ucol=bl * 80 + 16, xcol=bl * 68 + 4, kb0=bl * 192, vb0=bl * 3,
                                 first=True, last=True, hist_valid=True))
            for sg in segs:
                bl = sg["bl"]
                uc, xo = sg["ucol"], sg["xcol"]
                load_T(lambda c0, nch, t0, n, uc=uc: up[:, c0:c0 + nch, uc - 15:uc], lambda c: ("uph", c), state_pool[bl], 15, 512)
                load_T(lambda c0, nch, t0, n, xo=xo: ux[:, c0:c0 + nch, xo - 3:xo], lambda c: ("uxh", c), state_conv[bl], 3, 512)
                s = nstg()
                DMA("sp", stg[s][:, 0:256], cache_k[bl], (), [("stg", s)], "stg%d" % s)
                s2 = nstg()
                for r_ in range(2):
                    CP("pool", stg[s2][:, :].rearrange("p (v r d) -> p v r d", v=4, r=2)[:, :, r_, :],
                       stg[s][:, 0:256].rearrange("p (v d) -> p v d", v=4), [("stg", s)], [("stg", s2)])
                b = nb()
                for kv in range(4):
                    TR(ps[b][:, kv * 128:(kv + 1) * 128], stg[s2][:, kv * 128:(kv + 1) * 128], [("stg", s2)], [("ps", b)])
                CP(evac_eng(), kbuf[:, :, sg["kb0"]:sg["kb0"] + 128], ps[b][:, :].rearrange("p (v t) -> p v t", v=4),
                   [("ps", b)], [("kbufh", kv_) for kv_ in range(4)])
                for j in range(2):
                    s = nstg()
                    DMA("sp", stg[s][0:64, 0:256], cache_v[bl, j * 64:(j + 1) * 64, :], (), [("stg", s)], "stg%d" % s)
                    for r_ in range(2):
                        CP("pool", vtok[0:64, sg["vb0"] + j, :].rearrange("p (v r d) -> p v r d", v=4, r=2)[:, :, r_, :],
                           stg[s][0:64, 0:256].rearrange("p (v d) -> p v d", v=4), [("stg", s)], [("vtok", sg["vb0"] + j)])
            load_T(lambda c0, nch, t0, n: hstate_s[:, c0:c0 + nch, 0:4], lambda c: "hstate_s", state_lru, 4, 512)

            kvctr = [0]

            def mk_kvslot(l):
                def f(sg):
                    slot = kvctr[0] % 2
                    kvctr[0] += 1
                    mem_load_sample(l, sg["bl"], slot)
                    return slot
                return f
            run_tile(segs, T, True, x_sample, y_sample, mk_kvslot(0), mk_kvslot(1), next_mem=0)
            store_T(lambda c, t0, n: hstate_s[:, c, 0:4], lambda c: "hstate_s", lru_s, 4, 512)

        for bl in range(nseq if STAGE >= 2 else 0):
            mem_phase(bl)
            for ti in range(NTI if STAGE >= 3 else 0):
                seg = dict(bl=bl, c0=0, n=TP, ucol=16, xcol=4, kb0=0, vb0=0, first=(ti == 0), last=(ti == SEQ // TP - 1),
                           hist_valid=(ti != 0))
                if ti == 0:
                    for g in range(4):
                        MSET("pool", up[:, g, 0:16], 0.0, [("uph", g)])
                        MSET("pool", ux[:, g, 0:4], 0.0, [("uxh", g)])
                run_tile([seg], TP, False, x_prompt[bl, ti * TP:(ti + 1) * TP, :], y_prompt[bl, ti * TP:(ti + 1) * TP, :],
                         lambda sg: 0, lambda sg: 1, next_mem=(bl + 1 if ti == SEQ // TP - 1 else None))
            if bl == nseq - 1:
                store_T(lambda c, t0, n: hstate[:, c, 0:4], lambda c: "hstate", lru_p, 4, 512)
        flush_pending()
        assert STAGE < 99 or wpos[0] == len(wseq), (wpos, len(wseq))

        chans = sorted(P.chan_n.keys())
        sems = {e_: es.enter_context(nc.semaphore("s_" + e_)) for e_ in COMPUTE}
        chan_sems = {c_: es.enter_context(nc.semaphore("d_" + c_)) for c_ in chans}
        block = es.enter_context(nc.Block())
        P.emit(nc, block, sems, chan_sems)
    return nc, len(P.ops)


_CACHE = {}


def _stack_vecs(inp):
    rows = []
    for nm in ["norm_mix", "norm_cross", "norm_mem", "norm_mlp"]:
        for l in range(2):
            rows.append(np.asarray(inp[nm][l]).reshape(8, 128))
    rows.append(np.asarray(inp["norm_final"]).reshape(8, 128))
    cw = np.asarray(inp["conv_w"])[0]
    for k in range(4):
        rows.append(cw[k].reshape(4, 128))
    for nm in ["conv_b", "b_rg_a", "b_rg_x", "rg_lambda", "pool_scale"]:
        rows.append(np.asarray(inp[nm])[0].reshape(4, 128))
    v = np.concatenate(rows, axis=0).astype(np.float32)
    out = np.zeros((128, 128), np.float32)
    out[:v.shape[0]] = v
    return out


def kernel(**inp):
    nseq = int(os.environ.get("KDEBUG_NSEQ", NB))
    do_sample = os.environ.get("KDEBUG_NOSAMPLE", "0") != "1"
    key = (nseq, do_sample)
    if key not in _CACHE:
        _CACHE[key] = build_program(nseq, do_sample)[0]
    nc = _CACHE[key]
    f = lambda a: np.ascontiguousarray(np.asarray(a, dtype=np.float32))
    shared = dict(
        vecs=_stack_vecs(inp), attn_sinks=f(inp["attn_sinks"]).reshape(1, 16), ident=np.eye(128, dtype=np.float32),
        w_in=f(inp["w_in_even"][0]), w_out=f(inp["w_out_even"][0]), w_qkv=f(inp["w_qkv_odd"][0]), w_o=f(inp["w_o_odd"][0]),
        w_mq=f(inp["w_mq"]), w_mk=f(inp["w_mk"]), w_mv=f(inp["w_mv"]), w_mo=f(inp["w_mo"]), w_up=f(inp["w_up"]),
        w_down=f(inp["w_down"]), pool_w=f(inp["pool_w"][0]), w_rg_a=f(inp["w_rg_a"][0]), w_rg_x=f(inp["w_rg_x"][0]),
    )
    in_maps = []
    for i in range(NCORE):
        sl = slice(i * NB, (i + 1) * NB)
        m = dict(shared)
        m.update(
            x_prompt=f(inp["x_prompt"][sl]), x_sample=f(inp["x_sample"][sl]).reshape(NB * DEC, D),
            state_pool=f(inp["state_pool"][0, sl]), state_conv=f(inp["state_conv"][0, sl]), state_lru=f(inp["state_lru"][0, sl]),
            cache_swa_k=f(inp["cache_swa_k"][0, sl]).reshape(NB, 128, 256), cache_swa_v=f(inp["cache_swa_v"][0, sl]).reshape(NB, 128, 256),
            cache_mem_k=f(inp["cache_mem_k"][:, sl]).reshape(2, NB, NMEM, D), cache_mem_v=f(inp["cache_mem_v"][:, sl]).reshape(2, NB, NMEM, D),
            mem_prompt=f(inp["mem_prompt"][sl]),
        )
        in_maps.append(m)
    res = run_bass_kernel_spmd(nc, in_maps, core_ids=list(range(NCORE)))
    R = res.results
    cat = lambda k, ax=0: np.concatenate([np.asarray(r[k]) for r in R], axis=ax)
    B = NCORE * NB
    y_prompt = cat("y_prompt")
    y_sample = cat("y_sample").reshape(B, DEC, D)
    pool_p = cat("pool_p")[None]
    conv_p = cat("conv_p")[None]
    lru_p = cat("lru_p")[None]
    swa_k_p = cat("swa_k_p").reshape(1, B, 128, 4, 64)
    swa_v_p = cat("swa_v_p").reshape(1, B, 128, 4, 64)
    mem_k_p = cat("mem_k_p", 1).reshape(2, B, NMEM, 4, 256)
    mem_v_p = cat("mem_v_p", 1).reshape(2, B, NMEM, 4, 256)
    pool_s = cat("pool_s")[None]
    conv_s = cat("conv_s")[None]
    lru_s = cat("lru_s")[None]
    swa_k_s = cat("swa_k_s").reshape(1, B, 128, 4, 64)
    swa_v_s = cat("swa_v_s").reshape(1, B, 128, 4, 64)
    return (y_prompt, y_sample, pool_p, conv_p, lru_p, swa_k_p, swa_v_p, mem_k_p, mem_v_p,
            pool_s, conv_s, lru_s, swa_k_s, swa_v_s)
```
